# Optimizing a Trainium2 kernel written in Bass

```python
import jax, jax.numpy as jnp
from jax import lax
import numpy as np

D_MODEL = 2048
BATCH = 4
SEQ = 2048
DEPTH = 4

GRID_W = 64
CTX_LEN = 256
D_MIX = D_MODEL
W_CONV = D_MIX // 2
W_MLSTM = D_MIX - W_CONV
CONV_K = 31
CONV_PAD = CONV_K // 2
H_ML = 8
DH_ML = W_MLSTM // H_ML
CHUNK = 64
N_GATES = 4 * H_ML
N_IN = 3 * W_CONV + 5 * W_MLSTM + N_GATES
EPS = 1e-6
NEG = -1e30

kernel_name = "hymba_conformer_mlstm_prefix_dit"


def rmsnorm(x, g):
    x32 = x.astype(jnp.float32)
    r = x32 * lax.rsqrt(jnp.mean(x32 * x32, axis=-1, keepdims=True) + EPS)
    return (r * g.astype(jnp.float32)).astype(x.dtype)


def layernorm(x, g, b):
    x32 = x.astype(jnp.float32)
    mu = jnp.mean(x32, axis=-1, keepdims=True)
    xc = x32 - mu
    r = xc * lax.rsqrt(jnp.mean(xc * xc, axis=-1, keepdims=True) + EPS)
    return (r * g.astype(jnp.float32) + b.astype(jnp.float32)).astype(x.dtype)


def conv_latent(u, w, b):
    bsz, n, ch = u.shape
    rows = n // GRID_W
    grid = u.reshape(bsz, rows, GRID_W, ch)
    half = ch // 2
    dn = ('NHWC', 'HWIO', 'NHWC')
    yh = lax.conv_general_dilated(grid[..., :half], w[None, :, None, :half], (1, 1),
                                  [(0, 0), (CONV_PAD, CONV_PAD)], dimension_numbers=dn,
                                  feature_group_count=half)
    yv = lax.conv_general_dilated(grid[..., half:], w[:, None, None, half:], (1, 1),
                                  [(CONV_PAD, CONV_PAD), (0, 0)], dimension_numbers=dn,
                                  feature_group_count=ch - half)
    return jnp.concatenate([yh, yv], axis=-1).reshape(bsz, n, ch) + b


def conv_context(u, w, b):
    ch = u.shape[-1]
    y = lax.conv_general_dilated(u, w[:, None, :], (1,), [(CONV_PAD, CONV_PAD)],
                                 dimension_numbers=('NWC', 'WIO', 'NWC'),
                                 feature_group_count=ch)
    return y + b


def conformer_branch(a, g, z, conv_fn, w_dw, b_dw, ln_g, ln_b, w_pw2):
    u = a * jax.nn.sigmoid(g)
    u = conv_fn(u, w_dw, b_dw)
    u = layernorm(u, ln_g, ln_b)
    u = jax.nn.silu(u) @ w_pw2
    return u * jax.nn.silu(z)


def to_heads(p):
    bsz, n, _ = p.shape
    return p.reshape(bsz, n, H_ML, DH_ML).transpose(0, 2, 1, 3).astype(jnp.float32)


def mlstm_scan(q, k, v, log_i, log_f, state):
    bsz, nh, n, dh = q.shape
    nc = n // CHUNK

    def chunks(t):
        return jnp.moveaxis(t.reshape(t.shape[:2] + (nc, CHUNK) + t.shape[3:]), 2, 0)

    tril = jnp.tril(jnp.ones((CHUNK, CHUNK), dtype=bool))

    def step(carry, xs):
        c_st, n_st, m_st = carry
        qc, kc, vc, li, lf = xs
        b = jnp.cumsum(lf, axis=-1)
        dmat = b[..., :, None] - b[..., None, :] + li[..., None, :]
        dmat = jnp.where(tril, dmat, NEG)
        inter = b + m_st[..., None]
        m_t = jnp.maximum(inter, jnp.max(dmat, axis=-1))
        w_inter = jnp.exp(inter - m_t)
        pmat = jnp.exp(dmat - m_t[..., None]) * jnp.einsum('bhtd,bhsd->bhts', qc, kc)
        num = (w_inter[..., None] * jnp.einsum('bhtd,bhde->bhte', qc, c_st)
               + jnp.einsum('bhts,bhse->bhte', pmat, vc))
        den = w_inter * jnp.einsum('bhtd,bhd->bht', qc, n_st) + jnp.sum(pmat, axis=-1)
        h = num / jnp.maximum(jnp.abs(den), jnp.exp(-m_t))[..., None]
        b_last = b[..., -1]
        gl = b_last[..., None] - b + li
        m_new = jnp.maximum(b_last + m_st, jnp.max(gl, axis=-1))
        a = jnp.exp(b_last + m_st - m_new)
        ws = jnp.exp(gl - m_new[..., None])
        c_new = a[..., None, None] * c_st + jnp.einsum('bhsd,bhse->bhde', kc * ws[..., None], vc)
        n_new = a[..., None] * n_st + jnp.einsum('bhs,bhsd->bhd', ws, kc)
        return (c_new, n_new, m_new), h

    state, hs = lax.scan(step, state, (chunks(q), chunks(k), chunks(v), chunks(log_i), chunks(log_f)))
    h = jnp.moveaxis(hs, 0, 2).reshape(bsz, nh, n, dh)
    return h, state


def split_gates(gp, b_gate):
    g = (gp.astype(jnp.float32) + b_gate.astype(jnp.float32)).transpose(0, 2, 1)
    i_f, f_f, i_b, f_b = jnp.split(g, 4, axis=1)
    return i_f, jax.nn.log_sigmoid(f_f), i_b, jax.nn.log_sigmoid(f_b)


def mlstm_bidir(qx, kx, vx, gx, qc, kc, vc, gc, b_gate):
    scale = DH_ML ** -0.5
    qx, kx, vx = to_heads(qx) * scale, to_heads(kx), to_heads(vx)
    qc, kc, vc = to_heads(qc) * scale, to_heads(kc), to_heads(vc)
    ixf, fxf, ixb, fxb = split_gates(gx, b_gate)
    icf, fcf, icb, fcb = split_gates(gc, b_gate)
    bsz = qx.shape[0]
    init = (jnp.zeros((bsz, H_ML, DH_ML, DH_ML), jnp.float32),
            jnp.zeros((bsz, H_ML, DH_ML), jnp.float32),
            jnp.full((bsz, H_ML), NEG, jnp.float32))
    flip = lambda t: jnp.flip(t, axis=2)
    hcf, st_f = mlstm_scan(qc, kc, vc, icf, fcf, init)
    hxf, _ = mlstm_scan(qx, kx, vx, ixf, fxf, st_f)
    hcb, st_b = mlstm_scan(flip(qc), flip(kc), flip(vc), flip(icb), flip(fcb), init)
    hxb, _ = mlstm_scan(flip(qx), flip(kx), flip(vx), flip(ixb), flip(fxb), st_b)
    return hxf + flip(hxb), hcf + flip(hcb)


def mlstm_out(h, o, z, g_head, dtype):
    mu = jnp.mean(h, axis=-1, keepdims=True)
    hc = h - mu
    hn = hc * lax.rsqrt(jnp.mean(hc * hc, axis=-1, keepdims=True) + EPS)
    bsz, _, n, _ = h.shape
    hn = hn.transpose(0, 2, 1, 3).reshape(bsz, n, W_MLSTM) * g_head.astype(jnp.float32)
    return (hn * jax.nn.sigmoid(o.astype(jnp.float32))).astype(dtype) * jax.nn.silu(z)


def split_proj(p):
    cuts = np.cumsum([W_CONV, W_CONV, W_CONV, W_MLSTM, W_MLSTM, W_MLSTM, W_MLSTM, W_MLSTM])
    return jnp.split(p, [int(t) for t in cuts], axis=-1)


def setup_inputs(seed: int = 0) -> dict:
    key = jax.random.key(seed)
    ks = jax.random.split(key, 17)
    nrm = jax.random.normal
    f32 = jnp.float32
    gate_base = jnp.concatenate([jnp.zeros((H_ML,), f32), jnp.full((H_ML,), 3.0, f32),
                                 jnp.zeros((H_ML,), f32), jnp.full((H_ML,), 3.0, f32)])
    return {
        "x": nrm(ks[0], (BATCH, SEQ, D_MODEL), f32),
        "c": nrm(ks[1], (BATCH, D_MODEL), f32),
        "ctx": nrm(ks[2], (BATCH, CTX_LEN, D_MODEL), f32),
        "c_ctx": nrm(ks[3], (D_MODEL,), f32),
        "w_ada": nrm(ks[4], (DEPTH, D_MODEL, 3 * D_MODEL), f32) * (0.5 * D_MODEL ** -0.5),
        "b_ada": nrm(ks[5], (DEPTH, 3 * D_MODEL), f32) * 0.01,
        "g_pre": 1.0 + 0.1 * nrm(ks[6], (DEPTH, D_MODEL), f32),
        "g_post": 1.0 + 0.1 * nrm(ks[7], (DEPTH, D_MODEL), f32),
        "w_in": nrm(ks[8], (DEPTH, D_MODEL, N_IN), f32) * D_MODEL ** -0.5,
        "b_gate": gate_base + 0.3 * nrm(ks[9], (DEPTH, N_GATES), f32),
        "w_dw": nrm(ks[10], (DEPTH, CONV_K, W_CONV), f32) * CONV_K ** -0.5,
        "b_dw": nrm(ks[11], (DEPTH, W_CONV), f32) * 0.01,
        "ln_g": 1.0 + 0.1 * nrm(ks[12], (DEPTH, W_CONV), f32),
        "ln_b": nrm(ks[13], (DEPTH, W_CONV), f32) * 0.01,
        "w_pw2": nrm(ks[14], (DEPTH, W_CONV, W_CONV), f32) * W_CONV ** -0.5,
        "g_head": 1.0 + 0.1 * nrm(ks[15], (DEPTH, W_MLSTM), f32),
        "w_out": nrm(ks[16], (DEPTH, D_MIX, D_MODEL), f32) * D_MIX ** -0.5,
    }


def reference(x, c, ctx, c_ctx, w_ada, b_ada, g_pre, g_post, w_in, b_gate, w_dw, b_dw,
              ln_g, ln_b, w_pw2, g_head, w_out):
    for l in range(DEPTH):
        last = l == DEPTH - 1
        ada_x = jax.nn.silu(c) @ w_ada[l] + b_ada[l]
        ada_c = jax.nn.silu(c_ctx) @ w_ada[l] + b_ada[l]
        sh_x, sc_x, gt_x = jnp.split(ada_x[:, None, :], 3, axis=-1)
        sh_c, sc_c, gt_c = jnp.split(ada_c, 3, axis=-1)
        hx = rmsnorm(x, g_pre[l]) * (1.0 + sc_x) + sh_x
        hc = rmsnorm(ctx, g_pre[l]) * (1.0 + sc_c) + sh_c
        ax, gx_, zx, qx, kx, vx, ox, zmx, gatex = split_proj(hx @ w_in[l])
        ac, gc_, zc, qc, kc, vc, oc, zmc, gatec = split_proj(hc @ w_in[l])
        h_x, h_c = mlstm_bidir(qx, kx, vx, gatex, qc, kc, vc, gatec, b_gate[l])
        ym_x = mlstm_out(h_x, ox, zmx, g_head[l], x.dtype)
        yc_x = conformer_branch(ax, gx_, zx, conv_latent, w_dw[l], b_dw[l], ln_g[l], ln_b[l], w_pw2[l])
        out_x = jnp.concatenate([yc_x, ym_x], axis=-1) @ w_out[l]
        x_new = x + gt_x * rmsnorm(out_x, g_post[l])
        if not last:
            ym_c = mlstm_out(h_c, oc, zmc, g_head[l], ctx.dtype)
            yc_c = conformer_branch(ac, gc_, zc, conv_context, w_dw[l], b_dw[l], ln_g[l], ln_b[l], w_pw2[l])
            out_c = jnp.concatenate([yc_c, ym_c], axis=-1) @ w_out[l]
            ctx = ctx + gt_c * rmsnorm(out_c, g_post[l])
        x = x_new
    return x
```

```python
import contextlib
import numpy as np
import concourse.bass as bass
import concourse.mybir as mybir
from concourse.bass_utils import run_bass_kernel_spmd

F32 = mybir.dt.float32
BF16 = mybir.dt.bfloat16
AF = mybir.ActivationFunctionType
ALU = mybir.AluOpType
AX = mybir.AxisListType

D = 2048
NCTX = 256
NLAT = 2048
T = NCTX + NLAT
NT = T // 128
DEPTH = 4
WC = 1024
H = 8
DH = 128
CH = 64
NCH = T // CH
NIN = 8224
OA, OG, OZ, OQ, OK_, OV, OO, OZM, OGT = 0, 1024, 2048, 3072, 4096, 5120, 6144, 7168, 8192
EPS = 1e-6
NEG = -1e30
TG = [(0, 256)] + [(256 + 512 * i, 512) for i in range(4)]


class Sched:
    EPOCH = 30000
    NDMA = 12

    def __init__(self, nc):
        self.nc = nc
        self.engs = {'pe': nc.tensor, 'act': nc.scalar, 'dve': nc.vector,
                     'pool': nc.gpsimd, 'sp': nc.sync}
        self.cnt = {e: 0 for e in self.engs}
        self.sems = {e: [] for e in self.engs}
        self.seen = {e: {} for e in self.engs}
        self.res = {}
        self.dq = {}
        self.nsem = 0
        self.nwait = 0
        self.nins = 0

    def _newsem(self, name):
        self.nsem += 1
        return self.nc.alloc_semaphore(name=name)

    def _esem(self, e, n):
        ep = (n - 1) // self.EPOCH
        while len(self.sems[e]) <= ep:
            self.sems[e].append(self._newsem(f"s_{e}_{len(self.sems[e])}"))
        return self.sems[e][ep], (n - 1) % self.EPOCH + 1

    def _wait(self, e, dep):
        if dep[0] == 'e':
            sem, val = self._esem(dep[1], dep[2])
        else:
            sem, val = dep[1], dep[2]
        k = id(sem)
        if self.seen[e].get(k, 0) >= val:
            return
        self.seen[e][k] = val
        self.engs[e].wait_ge(sem, val)
        self.nwait += 1

    def _deps(self, e, reads, writes, excl):
        deps = []
        for r in reads:
            st = self.res.get(r)
            if st and st['w'] is not None:
                deps.append(('raw', st['w']))
        for w in writes:
            st = self.res.get(w)
            if st:
                if st['w'] is not None:
                    deps.append(('waw', st['w']))
                for d in st['r'].values():
                    deps.append(('war', d))
        for w in excl:
            st = self.res.get(w)
            if st:
                if st['w'] is not None:
                    deps.append(('x', st['w']))
                for d in st['r'].values():
                    deps.append(('x', d))
        for kind, d in deps:
            if d[0] == 'e' and d[1] == e:
                if e == 'pe' or kind in ('war', 'x'):
                    continue
            self._wait(e, d)

    def _commit(self, me, reads, writes, excl, ekey):
        for w in writes:
            self.res[w] = {'w': me, 'r': {}}
        for r in reads:
            st = self.res.setdefault(r, {'w': None, 'r': {}})
            st['r'][ekey] = me
        for w in excl:
            self.res[w] = {'w': me, 'r': {}}

    def op(self, e, fn, reads=(), writes=(), excl=()):
        self._deps(e, reads, writes, excl)
        ins = fn(self.engs[e])
        self.cnt[e] += 1
        n = self.cnt[e]
        sem, val = self._esem(e, n)
        ins.then_inc(sem, 1)
        self.nins += 1
        me = ('e', e, n)
        self._commit(me, reads, writes, excl, e)
        return me

    def dma(self, q, out, in_, reads=(), writes=(), **kw):
        self._deps(q, reads, writes, ())
        st = self.dq.setdefault(q, {'sems': [], 'vals': [], 'i': 0})
        i = st['i'] % self.NDMA
        if len(st['sems']) <= i:
            st['sems'].append(self._newsem(f"d_{q}_{i}"))
            st['vals'].append(0)
        sem = st['sems'][i]
        if st['vals'][i] > 0:
            self._wait(q, ('d', sem, st['vals'][i]))
        st['vals'][i] += 16
        st['i'] += 1
        self.engs[q].dma_start(out=out, in_=in_, **kw).then_inc(sem, 16)
        self.nins += 1
        me = ('d', sem, st['vals'][i])
        self._commit(me, reads, writes, (), ('dma', q, i))
        return me

    def barrier(self):
        for e in self.engs:
            for f in self.engs:
                if f != e and self.cnt[f] > 0:
                    self._wait(e, ('e', f, self.cnt[f]))
            for q, st in self.dq.items():
                for sem, val in zip(st['sems'], st['vals']):
                    if val > 0:
                        self._wait(e, ('d', sem, val))
        self.res.clear()


def bc(ap, axis, n):
    a = ap.unsqueeze(axis)
    shp = list(a.shape)
    shp[axis] = n
    return a.to_broadcast(shp)


def build(nlayers=DEPTH, debug=False):
    nc = bass.Bass("TRN2", target_bir_lowering=False)
    S = Sched(nc)

    def din(name, shape):
        return nc.dram_tensor(name, shape, F32, kind="ExternalInput").ap()

    xin = din("xin", [T, D])
    cc = din("cc", [2, D])
    w_ada = din("w_ada", [DEPTH, D, 3 * D])
    b_ada = din("b_ada", [DEPTH, 3 * D])
    g_pre = din("g_pre", [DEPTH, D])
    g_post = din("g_post", [DEPTH, D])
    w_in = din("w_in", [DEPTH, D, NIN])
    b_gate = din("b_gate", [DEPTH, 32])
    w_dw = din("w_dw", [DEPTH, 31, WC])
    b_dw = din("b_dw", [DEPTH, WC])
    ln_g = din("ln_g", [DEPTH, WC])
    ln_b = din("ln_b", [DEPTH, WC])
    w_pw2 = din("w_pw2", [DEPTH, WC, WC])
    g_head = din("g_head", [DEPTH, WC])
    w_out = din("w_out", [DEPTH, D, D])
    xout = nc.dram_tensor("xout", [T, D], F32, kind="ExternalOutput").ap()

    skind = "ExternalOutput" if debug else "Internal"

    def dscr(name, shape, dt):
        return nc.dram_tensor(name, shape, dt, kind=skind).ap()

    ycT = dscr("ycT", [WC, T], BF16)
    qT_d = dscr("qT_d", [H, DH, T], BF16)
    kT_d = dscr("kT_d", [H, DH, T], BF16)
    ktm = dscr("ktm", [T, WC], BF16)
    vtm = dscr("vtm", [T, WC], BF16)
    otm = dscr("otm", [T, WC], BF16)
    zmtm = dscr("zmtm", [T, WC], BF16)
    hfb = [dscr("hf_d", [T, WC], F32), dscr("hb_d", [T, WC], F32)]
    adaD = dscr("adaD", [2, D], F32)
    dbg = {}
    if debug:
        dbg['hT'] = dscr("dbg_hT", [128, 16, T], BF16)
        dbg['convT'] = dscr("dbg_convT", [128, 8, T], BF16)
        dbg['gates'] = dscr("dbg_gates", [4, 40, T], F32)
        dbg['etm'] = dscr("dbg_etm", [2, 64, NCH * 16], F32)
        dbg['w0b'] = dscr("dbg_w0b", [128, NCH * 16], F32)

    gs = contextlib.ExitStack()
    with gs:
        def GT(name, shape, dt):
            return gs.enter_context(nc.sbuf_tensor(name, shape, dt))

        ident_f = GT("ident_f", [128, 128], F32)
        ident_b = GT("ident_b", [128, 128], BF16)
        ones_b = GT("ones_b", [128, 128], BF16)
        ones_f = GT("ones_f", [128, 128], F32)
        mask2 = GT("mask2", [64, 2, 64], F32)
        neghalf = GT("neghalf", [128, 512], F32)
        g_preT = GT("g_preT", [128, DEPTH * 16], F32)
        ln_gT = GT("ln_gT", [128, DEPTH * 8], F32)
        ln_bT = GT("ln_bT", [128, DEPTH * 8], F32)
        b_dwT = GT("b_dwT", [128, DEPTH * 8], F32)
        b_adaT = GT("b_adaT", [128, DEPTH * 48], F32)
        w_dwT = GT("w_dwT", [128, DEPTH * 8, 31], F32)
        bgI = GT("bgI", [40, DEPTH], F32)
        bgF = GT("bgF", [40, DEPTH], F32)
        nbgF = GT("nbgF", [40, DEPTH], F32)
        cT = GT("cT", [128, 16, 2], BF16)
        adaT = GT("adaT", [128, 48, 2], F32)
        s1T = GT("s1T", [128, 16, 2], F32)
        shT = GT("shT", [128, 16, 2], F32)
        ones_col = GT("ones_col", [64, 1], BF16)
        sel16 = GT("sel16", [40, 16], F32)

        ss = contextlib.ExitStack()
        with ss:
            stg = [ss.enter_context(nc.sbuf_tensor(f"stg{i}", [128, 128], F32)) for i in range(2)]
            wst = ss.enter_context(nc.sbuf_tensor("wst", [31, DEPTH, WC], F32))
            c32 = ss.enter_context(nc.sbuf_tensor("c32", [32, 128], F32))
            c32s = ss.enter_context(nc.sbuf_tensor("c32s", [32, 128], F32))
            pst = [ss.enter_context(nc.psum_tensor(f"pst{i}", [128, 512], F32)) for i in range(2)]

            S.op('pool', lambda e: e.memset(ident_f[:], 0.0), writes=['ident_f'])
            S.op('pool', lambda e: e.affine_select(out=ident_f[:], in_=ident_f[:], pattern=[[-1, 128]],
                                                   compare_op=ALU.not_equal, fill=1.0, base=0, channel_multiplier=1),
                 reads=['ident_f'], writes=['ident_f'])
            S.op('dve', lambda e: e.tensor_copy(out=ident_b[:], in_=ident_f[:]), reads=['ident_f'], writes=['ident_b'])
            S.op('dve', lambda e: e.tensor_copy(out=sel16[:, 0:8], in_=ident_f[0:40, 0:8]), reads=['ident_f'], writes=['sel16a'])
            S.op('dve', lambda e: e.tensor_copy(out=sel16[:, 8:16], in_=ident_f[0:40, 32:40]), reads=['ident_f'], writes=['sel16b'])
            S.op('pool', lambda e: e.memset(ones_b[:], 1.0), writes=['ones_b'])
            S.op('pool', lambda e: e.memset(ones_f[:], 1.0), writes=['ones_f'])
            S.op('pool', lambda e: e.memset(ones_col[:], 1.0), writes=['ones_col'])
            S.op('pool', lambda e: e.memset(neghalf[:], -0.5), writes=['neghalf'])
            S.op('pool', lambda e: e.memset(mask2[:], 1.0), writes=['mask2'])
            S.op('pool', lambda e: e.affine_select(out=mask2[:, 0, :], in_=mask2[:, 0, :], pattern=[[1, 64]],
                                                   compare_op=ALU.is_ge, fill=0.0, base=0, channel_multiplier=-1),
                 reads=['mask2'], writes=['mask2'])
            S.op('pool', lambda e: e.affine_select(out=mask2[:, 1, :], in_=mask2[:, 1, :], pattern=[[-1, 64]],
                                                   compare_op=ALU.is_ge, fill=0.0, base=0, channel_multiplier=1),
                 reads=['mask2'], writes=['mask2'])
            for t_ in (bgI, bgF):
                S.op('pool', lambda e: e.memset(t_[:], 0.0), writes=[t_.name if hasattr(t_, 'name') else id(t_)])
            S.barrier()
            for (dst, col0, r0) in ((bgI, 0, 0), (bgF, 8, 0), (bgI, 16, 32), (bgF, 24, 32)):
                S.dma('sp', dst[r0:r0 + 8, :], b_gate[:, col0:col0 + 8].rearrange("l h -> h l"),
                      writes=[('bg', col0)], allow_slow_non_contiguous=True)
            S.barrier()
            S.op('dve', lambda e: e.tensor_scalar(out=nbgF[:], in0=bgF[:], scalar1=-1.0, scalar2=None, op0=ALU.mult),
                 writes=['nbgF'])

            tcount = [0]

            def load_T(dst, src_rows, R):
                i = tcount[0] % 2
                tcount[0] += 1
                S.dma('sp', stg[i][0:R, :], src_rows, writes=[('stg', i)])
                S.op('pe', lambda e: e.transpose(out=pst[i][:, 0:R], in_=stg[i][0:R, :], identity=ident_f[0:R, 0:R]),
                     reads=[('stg', i)], excl=[('pst', i)])
                S.op('dve', lambda e: e.tensor_copy(out=dst, in_=pst[i][:, 0:R]), excl=[('pst', i)], writes=[('ld', tcount[0])])

            load_T(g_preT[:, :], g_pre.rearrange("l (k p) -> (l k) p", p=128), 64)
            load_T(ln_gT[:, :], ln_g.rearrange("l (k p) -> (l k) p", p=128), 32)
            load_T(ln_bT[:, :], ln_b.rearrange("l (k p) -> (l k) p", p=128), 32)
            load_T(b_dwT[:, :], b_dw.rearrange("l (k p) -> (l k) p", p=128), 32)
            bav = b_ada.rearrange("l (k p) -> (l k) p", p=128)
            load_T(b_adaT[:, 0:128], bav[0:128, :], 128)
            load_T(b_adaT[:, 128:192], bav[128:192, :], 64)
            S.dma('sp', wst[:], w_dw.rearrange("l k c -> k l c"), writes=['wst'])
            for l in range(DEPTH):
                for j in range(8):
                    i = tcount[0] % 2
                    tcount[0] += 1
                    S.op('pe', lambda e: e.transpose(out=pst[i][:, 0:31], in_=wst[0:31, l, j * 128:(j + 1) * 128],
                                                     identity=ident_f[0:31, 0:31]),
                         reads=['wst'], excl=[('pst', i)])
                    S.op('dve', lambda e: e.tensor_copy(out=w_dwT[:, l * 8 + j, :], in_=pst[i][:, 0:31]),
                         excl=[('pst', i)], writes=[('wdw', l, j)])
            S.dma('sp', c32[:], cc.rearrange("j (k p) -> (j k) p", p=128), writes=['c32'])
            S.op('act', lambda e: e.activation(out=c32s[:], in_=c32[:], func=AF.Silu), reads=['c32'], writes=['c32s'])
            S.op('pe', lambda e: e.transpose(out=pst[0][:, 0:32], in_=c32s[:], identity=ident_f[0:32, 0:32]),
                 reads=['c32s'], excl=[('pst', 0)])
            S.op('dve', lambda e: e.tensor_copy(out=cT[:].rearrange("p k j -> p j k"),
                                                in_=pst[0][:, 0:32].rearrange("p (j k) -> p j k", j=2)),
                 excl=[('pst', 0)], writes=['cT'])
            S.barrier()

        for l in range(nlayers):
            src = xin if l == 0 else xout
            last = (l == DEPTH - 1)
            Xs = contextlib.ExitStack()
            etm = [Xs.enter_context(nc.sbuf_tensor(f"etm{q}_{l}", [64, NCH, 16], F32)) for q in range(2)]
            w0b = Xs.enter_context(nc.sbuf_tensor(f"w0b{l}", [128, NCH, 16], F32))
            Gs = contextlib.ExitStack()
            GI = Gs.enter_context(nc.sbuf_tensor(f"GI{l}", [40, T], F32))
            GF = Gs.enter_context(nc.sbuf_tensor(f"GF{l}", [40, T], F32))
            Ls = contextlib.ExitStack()
            with Ls:
                hT = Ls.enter_context(nc.sbuf_tensor(f"hT{l}", [128, 16, T], BF16))

                As = contextlib.ExitStack()
                with As:
                    wb = [As.enter_context(nc.sbuf_tensor(f"wbA{l}_{i}", [128, 16, 512], BF16)) for i in range(2)]
                    psA = As.enter_context(nc.psum_tensor(f"psA{l}", [128, 48, 2], F32))
                    for n in range(12):
                        s = n % 2
                        S.dma('pool', wb[s][:], w_ada[l, :, n * 512:(n + 1) * 512].rearrange("(k p) n -> p k n", p=128),
                              writes=[('wb', s)])
                        for m in range(4):
                            blk = n * 4 + m
                            for k in range(16):
                                S.op('pe', lambda e: e.matmul(psA[:, blk, :], lhsT=wb[s][:, k, m * 128:(m + 1) * 128],
                                                              rhs=cT[:, k, :], start=(k == 0), stop=(k == 15)),
                                     reads=[('wb', s)], excl=['psA'])
                    S.op('dve', lambda e: e.tensor_tensor(out=adaT[:], in0=psA[:],
                                                          in1=bc(b_adaT[:, l * 48:(l + 1) * 48], 2, 2), op=ALU.add),
                         excl=['psA'], writes=['adaT'])
                    S.op('dve', lambda e: e.scalar_tensor_tensor(out=s1T[:], in0=adaT[:, 16:32, :], scalar=1.0,
                                                                 in1=bc(g_preT[:, l * 16:(l + 1) * 16], 2, 2),
                                                                 op0=ALU.add, op1=ALU.mult),
                         reads=['adaT'], writes=['s1T'])
                    S.op('dve', lambda e: e.tensor_copy(out=shT[:], in_=adaT[:, 0:16, :]), reads=['adaT'], writes=['shT'])
                    for jx in range(2):
                        S.dma('sp', adaD[jx, :].rearrange("(c p) -> p c", p=128), adaT[:, 32:48, jx], reads=['adaT'],
                              writes=[('adaD', jx)], allow_slow_non_contiguous=True)
                    S.barrier()

                Bs = contextlib.ExitStack()
                with Bs:
                    xt = [Bs.enter_context(nc.sbuf_tensor(f"xt{l}_{i}", [128, D], F32)) for i in range(2)]
                    xn = [Bs.enter_context(nc.sbuf_tensor(f"xn{l}_{i}", [128, D], BF16)) for i in range(2)]
                    tmpf = Bs.enter_context(nc.sbuf_tensor(f"tmpf{l}", [128, 8, 128], F32))
                    stt = Bs.enter_context(nc.sbuf_tensor(f"stt{l}", [128, 3 * NT], F32))
                    psT = [Bs.enter_context(nc.psum_tensor(f"psT{l}_{i}", [128, 8, 128], BF16)) for i in range(2)]
                    for i in range(NT):
                        s = i % 2
                        jx = 1 if i < 2 else 0
                        S.dma('sp', xt[s][:], src[i * 128:(i + 1) * 128, :], writes=[('xt', s)])
                        S.op('act', lambda e: e.activation(out=xn[s][:], in_=xt[s][:], func=AF.Square,
                                                           accum_out=stt[:, i:i + 1]),
                             reads=[('xt', s)], writes=[('xn', s), ('ss', i)])
                        S.op('dve', lambda e: e.tensor_scalar(out=stt[:, NT + i:NT + i + 1], in0=stt[:, i:i + 1],
                                                              scalar1=1.0 / D, scalar2=EPS, op0=ALU.mult, op1=ALU.add),
                             reads=[('ss', i)], writes=[('ms', i)])
                        S.op('pool', lambda e: e.tensor_tensor(out=stt[:, 2 * NT + i:2 * NT + i + 1],
                                                               in0=stt[:, NT + i:NT + i + 1], in1=neghalf[:, 0:1], op=ALU.pow),
                             reads=[('ms', i)], writes=[('rs', i)])
                        S.op('act', lambda e: e.activation(out=xn[s][:], in_=xt[s][:], func=AF.Copy,
                                                           scale=stt[:, 2 * NT + i:2 * NT + i + 1]),
                             reads=[('xt', s), ('rs', i)], writes=[('xn', s)])
                        for hh in range(2):
                            for k8 in range(8):
                                k = hh * 8 + k8
                                S.op('pe', lambda e: e.transpose(out=psT[hh][:, k8, :], in_=xn[s][:, k * 128:(k + 1) * 128],
                                                                 identity=ident_b[:]),
                                     reads=[('xn', s)], excl=[('psT', hh)])
                            S.op('dve', lambda e: e.tensor_tensor(out=tmpf[:], in0=psT[hh][:],
                                                                  in1=bc(s1T[:, hh * 8:(hh + 1) * 8, jx], 2, 128), op=ALU.mult),
                                 reads=['s1T'], excl=[('psT', hh)], writes=['tmpf'])
                            S.op('dve', lambda e: e.tensor_tensor(out=hT[:, hh * 8:(hh + 1) * 8, i * 128:(i + 1) * 128],
                                                                  in0=tmpf[:],
                                                                  in1=bc(shT[:, hh * 8:(hh + 1) * 8, jx], 2, 128), op=ALU.add),
                                 reads=['tmpf', 'shT'], writes=[('hT', i)])
                    if debug and l == 0:
                        S.dma('sp', dbg['hT'], hT[:], reads=[('hT', i) for i in range(NT)])
                    S.barrier()
                hT_all = [('hT', i) for i in range(NT)]

                Cs = contextlib.ExitStack()
                with Cs:
                    wb = [Cs.enter_context(nc.sbuf_tensor(f"wbC{l}_{i}", [128, 16, 512], BF16)) for i in range(2)]
                    convT = Cs.enter_context(nc.sbuf_tensor(f"convT{l}", [128, 8, T], BF16))
                    upl = Cs.enter_context(nc.sbuf_tensor(f"upl{l}", [128, 64 * 64], BF16))
                    upc = Cs.enter_context(nc.sbuf_tensor(f"upc{l}", [128, NCTX + 30], BF16))
                    dg = Cs.enter_context(nc.sbuf_tensor(f"dg{l}", [128, 31, 128], BF16))
                    sig = [Cs.enter_context(nc.sbuf_tensor(f"sig{l}_{i}", [128, 512], F32)) for i in range(2)]
                    sq = upl[:, :].rearrange("p (j t) -> p j t", t=512)
                    mean = Cs.enter_context(nc.sbuf_tensor(f"mean{l}", [128, 512], F32))
                    rstd = Cs.enter_context(nc.sbuf_tensor(f"rstd{l}", [128, 512], F32))
                    t1 = [Cs.enter_context(nc.sbuf_tensor(f"t1{l}_{i}", [128, 512], F32)) for i in range(2)]
                    yco = [Cs.enter_context(nc.sbuf_tensor(f"yco{l}_{i}", [128, 512], BF16)) for i in range(2)]
                    psa = [Cs.enter_context(nc.psum_tensor(f"psa{l}_{i}", [128, 512], F32)) for i in range(2)]
                    psg = [Cs.enter_context(nc.psum_tensor(f"psg{l}_{i}", [128, 512], F32)) for i in range(2)]
                    psc = [Cs.enter_context(nc.psum_tensor(f"psc{l}_{i}", [128, 512], F32)) for i in range(2)]
                    S.op('pool', lambda e: e.memset(upc[:], 0.0), writes=['upc'])
                    uplh = upl[:, 0:32 * 94].rearrange("p (r c) -> p r c", c=94)
                    uplv = upl[:, 0:62 * 64].rearrange("p (r c) -> p r c", c=64)
                    cnt = 0
                    for jp in range(4):
                        s = jp % 2
                        S.dma('pool', wb[s][:, :, 0:256],
                              w_in[l, :, OA + jp * 256:OA + (jp + 1) * 256].rearrange("(k p) n -> p k n", p=128),
                              writes=[('wb', s)])
                        S.dma('pool', wb[s][:, :, 256:512],
                              w_in[l, :, OG + jp * 256:OG + (jp + 1) * 256].rearrange("(k p) n -> p k n", p=128),
                              reads=[('wb', s)], writes=[('wb', s)])
                        for jj in range(2):
                            j = 2 * jp + jj
                            horiz = j < 4
                            if j == 0 or j == 4:
                                S.op('pool', lambda e: e.memset(upl[:], 0.0), writes=['upl'])
                            S.op('dve', lambda e: e.tensor_tensor(out=dg[:], in0=bc(ident_b[:], 1, 31),
                                                                  in1=bc(w_dwT[:, l * 8 + j, :], 2, 128), op=ALU.mult),
                                 writes=['dg'])
                            for n, (t0, tn) in enumerate(TG):
                                b = cnt % 2
                                cnt += 1
                                for k in range(16):
                                    S.op('pe', lambda e: e.matmul(psa[b][:, 0:tn], lhsT=wb[s][:, k, jj * 128:(jj + 1) * 128],
                                                                  rhs=hT[:, k, t0:t0 + tn], start=(k == 0), stop=(k == 15)),
                                         reads=[('wb', s)], excl=[('psa', b)])
                                for k in range(16):
                                    S.op('pe', lambda e: e.matmul(psg[b][:, 0:tn],
                                                                  lhsT=wb[s][:, k, 256 + jj * 128:256 + (jj + 1) * 128],
                                                                  rhs=hT[:, k, t0:t0 + tn], start=(k == 0), stop=(k == 15)),
                                         reads=[('wb', s)], excl=[('psg', b)])
                                S.op('act', lambda e: e.activation(out=sig[b][:, 0:tn], in_=psg[b][:, 0:tn], func=AF.Sigmoid),
                                     excl=[('psg', b)], writes=[('sig', b)])
                                if n == 0:
                                    uo = upc[:, 15:15 + NCTX]
                                    ui = psa[b][:, 0:tn]
                                    si = sig[b][:, 0:tn]
                                    ukey = 'upc'
                                elif horiz:
                                    r0 = 8 * (n - 1)
                                    uo = uplh[:, r0:r0 + 8, 15:79]
                                    ui = psa[b][:, :].rearrange("p (r c) -> p r c", c=64)
                                    si = sig[b][:, :].rearrange("p (r c) -> p r c", c=64)
                                    ukey = 'upl'
                                else:
                                    r0 = 15 + 8 * (n - 1)
                                    uo = uplv[:, r0:r0 + 8, :]
                                    ui = psa[b][:, :].rearrange("p (r c) -> p r c", c=64)
                                    si = sig[b][:, :].rearrange("p (r c) -> p r c", c=64)
                                    ukey = 'upl'
                                S.op('dve', lambda e: e.tensor_tensor(out=uo, in0=ui, in1=si, op=ALU.mult),
                                     reads=[('sig', b), ukey], excl=[('psa', b)], writes=[ukey])
                            for n, (t0, tn) in enumerate(TG):
                                b = cnt % 2
                                cnt += 1
                                for k in range(31):
                                    if n == 0:
                                        win = upc[:, k:k + NCTX]
                                        po = psc[b][:, 0:tn]
                                        ukey = 'upc'
                                    elif horiz:
                                        r0 = 8 * (n - 1)
                                        win = uplh[:, r0:r0 + 8, k:k + 64]
                                        po = psc[b][:, :].rearrange("p (r c) -> p r c", c=64)
                                        ukey = 'upl'
                                    else:
                                        r0 = 8 * (n - 1) + k
                                        win = uplv[:, r0:r0 + 8, :]
                                        po = psc[b][:, :].rearrange("p (r c) -> p r c", c=64)
                                        ukey = 'upl'
                                    S.op('pe', lambda e: e.matmul(po, lhsT=dg[:, k, :], rhs=win, start=(k == 0), stop=(k == 30)),
                                         reads=['dg', ukey], excl=[('psc', b)])
                                S.op('act', lambda e: e.activation(out=convT[:, j, t0:t0 + tn], in_=psc[b][:, 0:tn],
                                                                   func=AF.Identity, bias=b_dwT[:, l * 8 + j:l * 8 + j + 1]),
                                     excl=[('psc', b)], writes=[('cv', j, n)])
                    if debug and l == 0:
                        S.dma('sp', dbg['convT'], convT[:], reads=[('cv', j, n) for j in range(8) for n in range(5)])
                    for n, (t0, tn) in enumerate(TG):
                        S.op('act', lambda e: e.activation(out=sq[:, :, 0:tn], in_=convT[:, :, t0:t0 + tn], func=AF.Square),
                             reads=[('cv', j, n) for j in range(8)] + ['upl'], writes=['sq', 'upl'])
                        for j in range(8):
                            S.op('pe', lambda e: e.matmul(psa[0][:, 0:tn], lhsT=ones_b[:], rhs=convT[:, j, t0:t0 + tn],
                                                          start=(j == 0), stop=(j == 7)),
                                 reads=[('cv', j, n)], excl=[('psa', 0)])
                        for j in range(8):
                            S.op('pe', lambda e: e.matmul(psg[0][:, 0:tn], lhsT=ones_b[:], rhs=sq[:, j, 0:tn],
                                                          start=(j == 0), stop=(j == 7)),
                                 reads=['sq'], excl=[('psg', 0)])
                        S.op('act', lambda e: e.activation(out=mean[:, 0:tn], in_=psa[0][:, 0:tn], func=AF.Copy, scale=1.0 / WC),
                             excl=[('psa', 0)], writes=['mean'])
                        S.op('dve', lambda e: e.tensor_tensor(out=t1[0][:, 0:tn], in0=mean[:, 0:tn], in1=mean[:, 0:tn], op=ALU.mult),
                             reads=['mean'], writes=[('t1', 0)])
                        S.op('dve', lambda e: e.scalar_tensor_tensor(out=t1[1][:, 0:tn], in0=psg[0][:, 0:tn], scalar=1.0 / WC,
                                                                     in1=t1[0][:, 0:tn], op0=ALU.mult, op1=ALU.subtract),
                             reads=[('t1', 0)], excl=[('psg', 0)], writes=[('t1', 1)])
                        S.op('dve', lambda e: e.tensor_scalar(out=t1[1][:, 0:tn], in0=t1[1][:, 0:tn], scalar1=EPS, scalar2=None,
                                                              op0=ALU.add),
                             reads=[('t1', 1)], writes=[('t1', 1)])
                        S.op('act', lambda e: e.activation(out=t1[1][:, 0:tn], in_=t1[1][:, 0:tn], func=AF.Sqrt),
                             reads=[('t1', 1)], writes=[('t1', 1)])
                        S.op('dve', lambda e: e.reciprocal(out=rstd[:, 0:tn], in_=t1[1][:, 0:tn]),
                             reads=[('t1', 1)], writes=['rstd'])
                        for j in range(8):
                            b = j % 2
                            S.op('dve', lambda e: e.tensor_tensor(out=t1[b][:, 0:tn], in0=convT[:, j, t0:t0 + tn],
                                                                  in1=mean[:, 0:tn], op=ALU.subtract),
                                 reads=[('cv', j, n), 'mean'], writes=[('t1', b)])
                            S.op('dve', lambda e: e.tensor_tensor(out=t1[b][:, 0:tn], in0=t1[b][:, 0:tn], in1=rstd[:, 0:tn], op=ALU.mult),
                                 reads=[('t1', b), 'rstd'], writes=[('t1', b)])
                            S.op('act', lambda e: e.activation(out=convT[:, j, t0:t0 + tn], in_=t1[b][:, 0:tn], func=AF.Silu,
                                                               scale=ln_gT[:, l * 8 + j:l * 8 + j + 1],
                                                               bias=ln_bT[:, l * 8 + j:l * 8 + j + 1]),
                                 reads=[('t1', b)], writes=[('cv', j, n)])
                    wp = wb[0][:].rearrange("p k n -> p (k n)").rearrange("p (j n) -> p j n", n=WC)
                    S.dma('pool', wp, w_pw2[l].rearrange("(j p) n -> p j n", p=128),
                          reads=[('wb', 0)], writes=[('wb', 0)])
                    for zh in range(2):
                        S.dma('pool', wb[1][:], w_in[l, :, OZ + zh * 512:OZ + (zh + 1) * 512].rearrange("(k p) n -> p k n", p=128),
                              reads=[('wb', 1)], writes=[('wb', 1)])
                        for mm in range(4):
                            m = zh * 4 + mm
                            for n, (t0, tn) in enumerate(TG):
                                b = cnt % 2
                                cnt += 1
                                for j in range(8):
                                    S.op('pe', lambda e: e.matmul(psa[b][:, 0:tn], lhsT=wp[:, j, m * 128:(m + 1) * 128],
                                                                  rhs=convT[:, j, t0:t0 + tn], start=(j == 0), stop=(j == 7)),
                                         reads=[('wb', 0), ('cv', j, n)], excl=[('psa', b)])
                                for k in range(16):
                                    S.op('pe', lambda e: e.matmul(psg[b][:, 0:tn], lhsT=wb[1][:, k, mm * 128:(mm + 1) * 128],
                                                                  rhs=hT[:, k, t0:t0 + tn], start=(k == 0), stop=(k == 15)),
                                         reads=[('wb', 1)], excl=[('psg', b)])
                                S.op('act', lambda e: e.activation(out=sig[b][:, 0:tn], in_=psg[b][:, 0:tn], func=AF.Silu),
                                     excl=[('psg', b)], writes=[('sig', b)])
                                S.op('dve', lambda e: e.tensor_tensor(out=yco[b][:, 0:tn], in0=psa[b][:, 0:tn], in1=sig[b][:, 0:tn],
                                                                      op=ALU.mult),
                                     reads=[('sig', b)], excl=[('psa', b)], writes=[('yco', b)])
                                S.dma('sp', ycT[m * 128:(m + 1) * 128, t0:t0 + tn], yco[b][:, 0:tn], reads=[('yco', b)],
                                      writes=[('ycT', m, n)])
                    S.barrier()

                Ds = contextlib.ExitStack()
                with Ds:
                    wb = [Ds.enter_context(nc.sbuf_tensor(f"wbD{l}_{i}", [128, 16, 512], BF16)) for i in range(2)]
                    wg = Ds.enter_context(nc.sbuf_tensor(f"wg{l}", [128, 16, 2, 40], BF16))
                    ev = [Ds.enter_context(nc.sbuf_tensor(f"ev{l}_{i}", [128, 512], BF16)) for i in range(4)]
                    psd = [Ds.enter_context(nc.psum_tensor(f"psd{l}_{i}", [128, 512], F32)) for i in range(4)]
                    cnt = 0
                    gcnt = 0
                    for (off, dst, scl) in ((OQ, qT_d, DH ** -0.5), (OK_, kT_d, 1.0)):
                        for half in range(2):
                            s = gcnt % 2
                            gcnt += 1
                            S.dma('pool', wb[s][:], w_in[l, :, off + half * 512:off + (half + 1) * 512].rearrange("(k p) n -> p k n", p=128),
                                  writes=[('wb', s)])
                            for hb in range(4):
                                hd = half * 4 + hb
                                for n, (t0, tn) in enumerate(TG):
                                    b = cnt % 4
                                    cnt += 1
                                    for k in range(16):
                                        S.op('pe', lambda e: e.matmul(psd[b][:, 0:tn], lhsT=wb[s][:, k, hb * 128:(hb + 1) * 128],
                                                                      rhs=hT[:, k, t0:t0 + tn], start=(k == 0), stop=(k == 15)),
                                             reads=[('wb', s)], excl=[('psd', b)])
                                    if b % 2 == 0:
                                        S.op('act', lambda e: e.activation(out=ev[b][:, 0:tn], in_=psd[b][:, 0:tn], func=AF.Copy, scale=scl),
                                             excl=[('psd', b)], writes=[('ev', b)])
                                    else:
                                        S.op('dve', lambda e: e.tensor_scalar(out=ev[b][:, 0:tn], in0=psd[b][:, 0:tn], scalar1=scl,
                                                                              scalar2=None, op0=ALU.mult),
                                             excl=[('psd', b)], writes=[('ev', b)])
                                    S.dma('sp', dst[hd, :, t0:t0 + tn], ev[b][:, 0:tn], reads=[('ev', b)], writes=[('qk', off, hd, n)])
                    for (off, dst) in ((OK_, ktm), (OV, vtm), (OO, otm), (OZM, zmtm)):
                        for half in range(2):
                            s = gcnt % 2
                            gcnt += 1
                            S.dma('pool', wb[s][:], w_in[l, :, off + half * 512:off + (half + 1) * 512].rearrange("(k p) n -> p k n", p=128),
                                  writes=[('wb', s)])
                            for i in range(NT):
                                b = cnt % 4
                                cnt += 1
                                for k in range(16):
                                    S.op('pe', lambda e: e.matmul(psd[b][:, :], lhsT=hT[:, k, i * 128:(i + 1) * 128],
                                                                  rhs=wb[s][:, k, :], start=(k == 0), stop=(k == 15)),
                                         reads=[('wb', s)], excl=[('psd', b)])
                                if b % 2 == 0:
                                    S.op('act', lambda e: e.activation(out=ev[b][:], in_=psd[b][:], func=AF.Copy),
                                         excl=[('psd', b)], writes=[('ev', b)])
                                else:
                                    S.op('dve', lambda e: e.tensor_copy(out=ev[b][:], in_=psd[b][:]),
                                         excl=[('psd', b)], writes=[('ev', b)])
                                S.dma('sp', dst[i * 128:(i + 1) * 128, half * 512:(half + 1) * 512], ev[b][:], reads=[('ev', b)],
                                      writes=[('tm', off, i, half)])
                    S.op('pool', lambda e: e.memset(wg[:], 0.0), writes=['wg'])
                    for (gi, r0, c0) in ((0, 0, 0), (1, 0, 8), (0, 32, 16), (1, 32, 24)):
                        S.dma('pool', wg[:, :, gi, r0:r0 + 8],
                              w_in[l, :, OGT + c0:OGT + c0 + 8].rearrange("(k p) n -> p k n", p=128),
                              reads=['wg'], writes=['wg'], allow_slow_non_contiguous=True)
                    for n, (t0, tn) in enumerate(TG):
                        for gi in range(2):
                            b = cnt % 4
                            cnt += 1
                            for k in range(16):
                                S.op('pe', lambda e: e.matmul(psd[b][0:40, 0:tn], lhsT=wg[:, k, gi, :], rhs=hT[:, k, t0:t0 + tn],
                                                              start=(k == 0), stop=(k == 15)),
                                     reads=['wg'], excl=[('psd', b)])
                            if gi == 0:
                                S.op('act', lambda e: e.activation(out=GI[:, t0:t0 + tn], in_=psd[b][0:40, 0:tn], func=AF.Identity,
                                                                   bias=bgI[:, l:l + 1]),
                                     excl=[('psd', b)], writes=[('GI', n)])
                            else:
                                S.op('act', lambda e: e.activation(out=GF[:, t0:t0 + tn], in_=psd[b][0:40, 0:tn], func=AF.Exp,
                                                                   scale=-1.0, bias=nbgF[:, l:l + 1]),
                                     excl=[('psd', b)], writes=[('GF', n)])
                    S.barrier()
            Eps = contextlib.ExitStack()
            with Eps:
                def ET(name, shape, dt):
                    return Eps.enter_context(nc.sbuf_tensor(f"{name}{l}", shape, dt))
                PRE = ET("PRE", [40, T], F32)
                scanmask = ET("scanmask", [40, T], F32)
                S.op('pool', lambda e: e.memset(scanmask[:], 1.0), writes=['scanmask'])
                smv = scanmask[:].rearrange("p (c t) -> p c t", t=CH)
                S.op('pool', lambda e: e.memset(smv[:, :, 0:1], 0.0), reads=['scanmask'], writes=['scanmask'])
                cl = ET("cl", [40, 8, NCH], F32)
                w0x = ET("w0x", [40, NCH, 16], F32)
                psE = [Eps.enter_context(nc.psum_tensor(f"psE{l}_{i}", [128, 512], F32)) for i in range(3)]
                allG = [('GI', n) for n in range(5)] + [('GF', n) for n in range(5)]
                S.op('act', lambda e: e.activation(out=GF[:], in_=GF[:], func=AF.Ln, bias=1.0), writes=['GF'])
                S.op('dve', lambda e: e.tensor_scalar(out=GF[:], in0=GF[:], scalar1=-1.0, scalar2=None, op0=ALU.mult),
                     reads=['GF'], writes=['GF'])
                if debug and l == 0:
                    S.dma('sp', dbg['gates'][0], GI[:], reads=['GF'])
                    S.dma('sp', dbg['gates'][1], GF[:], reads=['GF'])
                S.op('dve', lambda e: e.tensor_tensor_scan(out=PRE[:], data0=scanmask[:], data1=GF[:], initial=0.0,
                                                           op0=ALU.mult, op1=ALU.add),
                     reads=['GF', 'scanmask'], writes=['PRE'])
                PREv = PRE[:].rearrange("p (c t) -> p c t", t=CH)
                GFv = GF[:].rearrange("p (c t) -> p c t", t=CH)
                GIv = GI[:].rearrange("p (c t) -> p c t", t=CH)
                S.op('dve', lambda e: e.tensor_copy(out=cl[:, 0, :], in_=PREv[:, :, CH - 1]), reads=['PRE'], writes=['cl0'])
                S.op('dve', lambda e: e.tensor_tensor(out=GF[32:40, :], in0=GF[32:40, :], in1=PRE[32:40, :], op=ALU.subtract),
                     reads=['GF', 'PRE'], writes=['GF'])
                S.op('dve', lambda e: e.tensor_tensor(out=GFv[32:40], in0=GFv[32:40], in1=bc(cl[32:40, 0, :], 2, CH), op=ALU.add),
                     reads=['GF', 'cl0'], writes=['GF'])
                S.op('dve', lambda e: e.tensor_copy(out=GF[0:32, :], in_=PRE[0:32, :]), reads=['GF', 'PRE'], writes=['GF'])
                S.op('dve', lambda e: e.tensor_tensor(out=GI[:], in0=GI[:], in1=GF[:], op=ALU.subtract), reads=['GF'], writes=['GI'])
                S.op('dve', lambda e: e.tensor_reduce(out=cl[:, 1, :], in_=GIv, axis=AX.X, op=ALU.max), reads=['GI'], writes=['cl1'])
                S.op('dve', lambda e: e.tensor_tensor(out=cl[:, 2, :], in0=cl[:, 0, :], in1=cl[:, 1, :], op=ALU.add),
                     reads=['cl0', 'cl1'], writes=['cl2'])

                def rev_ap(ap2, lo, n):
                    a = ap2[:, lo:lo + n]
                    return bass.AP(a.tensor, a.offset + (n - 1) * a.ap[1][0], [list(a.ap[0]), [-a.ap[1][0], n]])

                for (so, de) in ((0, 3), (2, 4)):
                    S.op('dve', lambda e: e.tensor_copy(out=cl[0:32, de, :], in_=cl[0:32, so, :]), reads=[f'cl{so}'], writes=[f'cl{de}'])
                    S.op('dve', lambda e: e.tensor_copy(out=cl[32:40, de, 0:4], in_=rev_ap(cl[32:40, so, :], 0, 4)),
                         reads=[f'cl{so}', f'cl{de}'], writes=[f'cl{de}'])
                    S.op('dve', lambda e: e.tensor_copy(out=cl[32:40, de, 4:NCH], in_=rev_ap(cl[32:40, so, :], 4, 32)),
                         reads=[f'cl{so}', f'cl{de}'], writes=[f'cl{de}'])
                S.op('dve', lambda e: e.tensor_tensor_scan(out=cl[:, 5, :], data0=cl[:, 3, :], data1=cl[:, 4, :], initial=NEG,
                                                           op0=ALU.add, op1=ALU.max),
                     reads=['cl3', 'cl4'], writes=['cl5'])
                S.op('dve', lambda e: e.tensor_tensor(out=cl[:, 6, :], in0=cl[:, 5, :], in1=cl[:, 3, :], op=ALU.subtract),
                     reads=['cl5', 'cl3'], writes=['cl6'])
                S.op('pool', lambda e: e.memset(cl[:, 7, 0:1], NEG), writes=['cl7'])
                S.op('dve', lambda e: e.tensor_copy(out=cl[:, 7, 1:NCH], in_=cl[:, 5, 0:NCH - 1]), reads=['cl5', 'cl7'], writes=['cl7'])
                S.op('dve', lambda e: e.tensor_tensor(out=cl[:, 7, :], in0=cl[:, 7, :], in1=cl[:, 6, :], op=ALU.subtract),
                     reads=['cl7', 'cl6'], writes=['cl7'])
                S.op('act', lambda e: e.activation(out=cl[:, 7, :], in_=cl[:, 7, :], func=AF.Exp), reads=['cl7'], writes=['cl7'])
                S.op('dve', lambda e: e.tensor_copy(out=cl[0:32, 4, :], in_=cl[0:32, 6, :]), reads=['cl6', 'cl4'], writes=['cl4'])
                S.op('dve', lambda e: e.tensor_copy(out=cl[32:40, 4, 0:4], in_=rev_ap(cl[32:40, 6, :], 0, 4)),
                     reads=['cl6', 'cl4'], writes=['cl4'])
                S.op('dve', lambda e: e.tensor_copy(out=cl[32:40, 4, 4:NCH], in_=rev_ap(cl[32:40, 6, :], 4, 32)),
                     reads=['cl6', 'cl4'], writes=['cl4'])
                S.op('dve', lambda e: e.tensor_tensor(out=GIv, in0=GIv, in1=bc(cl[:, 4, :], 2, CH), op=ALU.subtract),
                     reads=['GI', 'cl4'], writes=['GI'])
                S.op('act', lambda e: e.activation(out=GI[:], in_=GI[:], func=AF.Exp), reads=['GI'], writes=['GI'])
                S.op('dve', lambda e: e.tensor_tensor(out=GFv, in0=GFv, in1=bc(cl[:, 4, :], 2, CH), op=ALU.add),
                     reads=['GF', 'cl4'], writes=['GF'])
                S.op('act', lambda e: e.activation(out=GF[:], in_=GF[:], func=AF.Exp, scale=-1.0), reads=['GF'], writes=['GF'])
                if debug and l == 0:
                    S.dma('sp', dbg['gates'][2], GI[:], reads=['GI'])
                    S.dma('sp', dbg['gates'][3], GF[:], reads=['GF'])
                for q, (srcg, key) in enumerate(((GI, 'GI'), (GF, 'GF'))):
                    for bi, (c0, c1) in enumerate(((0, 32), (32, NCH))):
                        pb = psE[bi]
                        for c in range(c0, c1):
                            S.op('pe', lambda e: e.matmul(pb[0:64, (c - c0) * 16:(c - c0 + 1) * 16],
                                                          lhsT=srcg[0:40, c * CH:(c + 1) * CH], rhs=sel16[:, :], start=True, stop=True),
                                 reads=[key], excl=[('psE', bi)])
                        S.op('dve', lambda e: e.tensor_copy(out=etm[q][:, c0:c1, :].rearrange("p c h -> p (c h)"),
                                                            in_=pb[0:64, 0:(c1 - c0) * 16]),
                             excl=[('psE', bi)], writes=[('etm', q, bi)])
                S.op('dve', lambda e: e.tensor_tensor(out=w0x[:], in0=bc(cl[:, 7, :], 2, 16), in1=bc(sel16[:, :], 1, NCH),
                                                      op=ALU.mult),
                     reads=['cl7'], writes=['w0x'])
                w0xf = w0x[:].rearrange("p c h -> p (c h)")
                w0bf = w0b[:].rearrange("p c h -> p (c h)")
                for i in range(2):
                    S.op('pe', lambda e: e.matmul(psE[i][:, 0:288], lhsT=ones_f[0:40, :], rhs=w0xf[:, i * 288:(i + 1) * 288],
                                                  start=True, stop=True),
                         reads=['w0x'], excl=[('psE', i)])
                    S.op('dve', lambda e: e.tensor_copy(out=w0bf[:, i * 288:(i + 1) * 288], in_=psE[i][:, 0:288]),
                         excl=[('psE', i)], writes=[('w0b', i)])
                if debug and l == 0:
                    for q in range(2):
                        S.dma('sp', dbg['etm'][q], etm[q][:].rearrange("p c h -> p (c h)"), reads=[('etm', q, bi) for bi in range(2)])
                    S.dma('sp', dbg['w0b'], w0bf, reads=[('w0b', i) for i in range(2)])
                S.barrier()
            Gs.close()

            Ss = contextlib.ExitStack()
            with Ss:
                def ST(name, shape, dt):
                    return Ss.enter_context(nc.sbuf_tensor(f"{name}{l}", shape, dt))
                SC = 4
                NSC = NCH // SC
                HA = DH + 1
                qb = [[ST(f"qb{d}_{i}_", [128, H, SC * CH], BF16) for i in range(2)] for d in range(2)]
                kb = [[ST(f"kb{d}_{i}_", [128, H, SC * CH], BF16) for i in range(2)] for d in range(2)]
                ktb = [[ST(f"ktb{d}_{i}_", [64, SC, WC], BF16) for i in range(2)] for d in range(2)]
                vtb = [[ST(f"vtb{d}_{i}_", [64, SC, H, HA], BF16) for i in range(2)] for d in range(2)]
                Cst = [ST(f"Cst{d}_", [128, H, HA], F32) for d in range(2)]
                Cb = [ST(f"Cb{d}_", [128, H, HA], BF16) for d in range(2)]
                PT = [[ST(f"PT{d}_{i}_", [64, H, CH], BF16) for i in range(2)] for d in range(2)]
                EM = [[ST(f"EM{d}_{i}_", [64, H, CH], F32) for i in range(2)] for d in range(2)]
                kE = [[ST(f"kE{d}_{i}_", [64, H, DH], BF16) for i in range(2)] for d in range(2)]
                dpos = [ST(f"dpos{d}_", [64, H], F32) for d in range(2)]
                dden = [ST(f"dden{d}_", [64, H], F32) for d in range(2)]
                hout = [[ST(f"hout{d}_{i}_", [64, H, DH], F32) for i in range(2)] for d in range(2)]
                psS = [Ss.enter_context(nc.psum_tensor(f"psS{l}_{d}", [64, H, CH], F32)) for d in range(2)]
                psN = [Ss.enter_context(nc.psum_tensor(f"psN{l}_{i}", [64, 3, HA], F32)) for i in range(3)]
                psC = [Ss.enter_context(nc.psum_tensor(f"psC{l}_{i}", [128, 3, HA], F32)) for i in range(3)]
                GH = [(0, 3), (3, 3), (6, 2)]

                for d in range(2):
                    S.op('pool', lambda e: e.memset(Cst[d][:], 0.0), writes=[('Cst', d, h) for h in range(H)])
                    S.op('pool', lambda e: e.memset(Cb[d][:], 0.0), writes=[('Cb', d, 0), ('Cb', d, 1)])
                    for i in range(2):
                        S.op('pool', lambda e: e.memset(vtb[d][i][:, :, :, DH:HA], 1.0), writes=[('vtb1', d, i)])

                def nat_chunk(d, p):
                    if d == 0:
                        return p
                    return 3 - p if p < 4 else 39 - p

                sc_base = {}

                def load_sc(d, sp_):
                    p0 = sp_ * SC
                    cs = sorted(nat_chunk(d, p0 + i) for i in range(SC))
                    c0 = cs[0]
                    assert cs == list(range(c0, c0 + SC))
                    s = sp_ % 2
                    t0 = c0 * CH
                    q_ = 'sp' if d == 0 else 'pool'
                    S.dma(q_, qb[d][s][:], qT_d[:, :, t0:t0 + SC * CH].rearrange("h p t -> p h t"), writes=[('qb', d, s)])
                    S.dma(q_, kb[d][s][:], kT_d[:, :, t0:t0 + SC * CH].rearrange("h p t -> p h t"), writes=[('kb', d, s)])
                    S.dma(q_, ktb[d][s][:], ktm[t0:t0 + SC * CH, :].rearrange("(c s) e -> s c e", s=CH), writes=[('ktb', d, s)])
                    for ci in range(SC):
                        S.dma(q_, vtb[d][s][:, ci, :, 0:DH],
                              vtm[t0 + ci * CH:t0 + (ci + 1) * CH, :].rearrange("s (h e) -> s h e", e=DH),
                              reads=[('vtb1', d, s)], writes=[('vtb', d, s, ci)])
                    sc_base[(d, sp_)] = c0

                def stage1(p, d):
                    sp_ = p // SC
                    s = sp_ % 2
                    par = p % 2
                    c = nat_chunk(d, p)
                    ci = c - sc_base[(d, sp_)]
                    r0 = d * 8
                    tsl = slice(ci * CH, (ci + 1) * CH)
                    Ecol = etm[0][:, c, r0:r0 + 8]
                    S.op('pool', lambda e: e.tensor_tensor(out=EM[d][par][:], in0=bc(Ecol, 2, CH), in1=bc(mask2[:, d, :], 1, H), op=ALU.mult),
                         writes=[('EM', d, par)])
                    S.op('pool', lambda e: e.tensor_tensor(out=kE[d][par][:], in0=ktb[d][s][:, ci, :].rearrange("p (h e) -> p h e", e=DH),
                                                           in1=bc(Ecol, 2, DH), op=ALU.mult),
                         reads=[('ktb', d, s)], writes=[('kE', d, par)])
                    for h in range(H):
                        S.op('pe', lambda e: e.matmul(psS[d][:, h, :], lhsT=kb[d][s][:, h, tsl], rhs=qb[d][s][:, h, tsl],
                                                      start=True, stop=True),
                             reads=[('kb', d, s), ('qb', d, s)], excl=[('psS', d)])
                    S.op('dve', lambda e: e.tensor_tensor(out=PT[d][par][:], in0=psS[d][:], in1=EM[d][par][:], op=ALU.mult),
                         reads=[('EM', d, par)], excl=[('psS', d)], writes=[('PT', d, par)])

                def stage2(p, d):
                    sp_ = p // SC
                    s = sp_ % 2
                    par = p % 2
                    c = nat_chunk(d, p)
                    ci = c - sc_base[(d, sp_)]
                    r0 = d * 8
                    tsl = slice(ci * CH, (ci + 1) * CH)
                    Fcol = etm[1][:, c, r0:r0 + 8]
                    for h in range(H):
                        g, hh = h // 3, h % 3
                        vh = vtb[d][s][:, ci, h, :]
                        S.op('pe', lambda e: e.matmul(psN[g][:, hh, :], lhsT=qb[d][s][:, h, tsl], rhs=Cb[d][:, h, :],
                                                      start=True, stop=False),
                             reads=[('qb', d, s), ('Cb', d, h // 4)], excl=[('psN', g)])
                        S.op('pe', lambda e: e.matmul(psN[g][:, hh, :], lhsT=PT[d][par][:, h, :], rhs=vh, start=False, stop=True),
                             reads=[('PT', d, par), ('vtb', d, s, ci), ('vtb1', d, s)], excl=[('psN', g)])
                    last_step = (p + 1 >= NCH)
                    if not last_step:
                        for h in range(H):
                            g, hh = h // 3, h % 3
                            vh = vtb[d][s][:, ci, h, :]
                            S.op('pe', lambda e: e.matmul(psC[g][:, hh, :], lhsT=kE[d][par][:, h, :], rhs=vh, start=True, stop=True),
                                 reads=[('kE', d, par), ('vtb', d, s, ci), ('vtb1', d, s)], excl=[('psC', g)])
                    for g, (h0, nh) in enumerate(GH):
                        S.op('act', lambda e: e.activation(out=dpos[d][:, h0:h0 + nh], in_=psN[g][:, 0:nh, DH], func=AF.Copy),
                             excl=[('psN', g)], writes=[('dpos', d, g)])
                    S.op('dve', lambda e: e.tensor_scalar(out=dden[d][:], in0=dpos[d][:], scalar1=-1.0, scalar2=None, op0=ALU.mult),
                         reads=[('dpos', d, g) for g in range(3)], writes=[('dden', d)])
                    S.op('dve', lambda e: e.tensor_tensor(out=dden[d][:], in0=dden[d][:], in1=dpos[d][:], op=ALU.max),
                         reads=[('dden', d)] + [('dpos', d, g) for g in range(3)], writes=[('dden', d)])
                    S.op('dve', lambda e: e.tensor_tensor(out=dden[d][:], in0=dden[d][:], in1=Fcol, op=ALU.max),
                         reads=[('dden', d)], writes=[('dden', d)])
                    S.op('dve', lambda e: e.reciprocal(out=dden[d][:], in_=dden[d][:]), reads=[('dden', d)], writes=[('dden', d)])
                    for g in range(2):
                        S.op('dve', lambda e: e.tensor_tensor(out=hout[d][par][:, 3 * g:3 * g + 3, :], in0=psN[g][:, 0:3, 0:DH],
                                                              in1=bc(dden[d][:, 3 * g:3 * g + 3], 2, DH), op=ALU.mult),
                             reads=[('dden', d)], excl=[('psN', g)], writes=[('hout', d, par, h) for h in range(3 * g, 3 * g + 3)])
                    for h in range(6, H):
                        g, hh = h // 3, h % 3
                        S.op('act', lambda e: e.activation(out=hout[d][par][:, h, :], in_=psN[g][:, hh, 0:DH], func=AF.Copy,
                                                           scale=dden[d][:, h:h + 1]),
                             reads=[('dden', d)], excl=[('psN', g)], writes=[('hout', d, par, h)])
                    S.dma('sp', hfb[d][c * CH:(c + 1) * CH, :], hout[d][par][:].rearrange("p h e -> p (h e)"),
                          reads=[('hout', d, par, h) for h in range(H)], writes=[('h_d', d, c)])
                    if not last_step:
                        w0c = w0b[:, p, r0:r0 + 8]
                        w0n = w0b[:, p + 1, r0:r0 + 8]
                        for h in range(H):
                            g, hh = h // 3, h % 3
                            S.op('dve', lambda e: e.scalar_tensor_tensor(out=Cst[d][:, h, :], in0=Cst[d][:, h, :], scalar=w0c[:, h:h + 1],
                                                                         in1=psC[g][:, hh, :], op0=ALU.mult, op1=ALU.add),
                                 reads=[('Cst', d, h)], excl=[('psC', g)], writes=[('Cst', d, h)])
                        S.op('dve', lambda e: e.tensor_tensor(out=Cb[d][:, 0:4, :], in0=Cst[d][:, 0:4, :], in1=bc(w0n[:, 0:4], 2, HA), op=ALU.mult),
                             reads=[('Cst', d, h) for h in range(4)], writes=[('Cb', d, 0)])
                        for h in range(4, H):
                            S.op('act', lambda e: e.activation(out=Cb[d][:, h, :], in_=Cst[d][:, h, :], func=AF.Copy, scale=w0n[:, h:h + 1]),
                                 reads=[('Cst', d, h)], writes=[('Cb', d, 1)])

                for d in range(2):
                    load_sc(d, 0)
                for d in range(2):
                    stage1(0, d)
                for p in range(NCH):
                    if p % SC == 0 and p // SC + 1 < NSC:
                        for d in range(2):
                            load_sc(d, p // SC + 1)
                    if p + 1 < NCH:
                        for d in range(2):
                            stage1(p + 1, d)
                    for d in range(2):
                        stage2(p, d)
                S.barrier()
            Xs.close()
            Fs = contextlib.ExitStack()
            with Fs:
                def FT(name, shape, dt):
                    return Fs.enter_context(nc.sbuf_tensor(f"{name}{l}", shape, dt))
                wo = FT("wo", [128, 16, D], BF16)
                gtile = [FT("gtx", [128, D], F32), FT("gtc", [128, D], F32)]
                ghb = FT("ghb", [128, WC], F32)
                NB3 = 3
                hfs = [FT(f"hfs{i}_", [128, H, DH], F32) for i in range(NB3)]
                hbs = [FT(f"hbs{i}_", [128, H, DH], F32) for i in range(NB3)]
                ot = [FT(f"ot{i}_", [128, WC], BF16) for i in range(NB3)]
                zt = [FT(f"zt{i}_", [128, WC], BF16) for i in range(NB3)]
                sgo_ = [FT(f"sgo{i}_", [128, WC], F32) for i in range(2)]
                szm_ = [FT(f"szm{i}_", [128, WC], BF16) for i in range(2)]
                hsq = FT("hsq", [128, H, DH], BF16)
                st8_ = [FT(f"st8{i}_", [128, 7, H], F32) for i in range(2)]
                ym_ = [FT(f"ym{i}_", [128, WC], BF16) for i in range(2)]
                yT = [FT(f"yT{i}_", [128, 16, 128], BF16) for i in range(NB3)]
                xr = [FT(f"xr{i}_", [128, D], F32) for i in range(NB3)]
                xo = [FT(f"xo{i}_", [128, D], F32) for i in range(2)]
                st1 = FT("st1", [128, 8], F32)
                sqj = FT("sqj", [128, 512], BF16)
                pso = Fs.enter_context(nc.psum_tensor(f"pso{l}", [128, D], F32))
                psy = [Fs.enter_context(nc.psum_tensor(f"psy{l}_{i}", [128, 4, 128], BF16)) for i in range(2)]
                for half in range(2):
                    S.dma('pool', wo[:, :, half * 1024:(half + 1) * 1024],
                          w_out[l, :, half * 1024:(half + 1) * 1024].rearrange("(k p) n -> p k n", p=128), writes=[('wo', half)])
                gpb = xr[0]
                S.dma('sp', gpb[:], g_post[l:l + 1, :].to_broadcast([128, D]), writes=[('xr', 0)])
                S.dma('sp', ghb[:], g_head[l:l + 1, :].to_broadcast([128, WC]), writes=['ghb'])
                for jx in range(2):
                    S.dma('sp', gtile[jx][:], adaD[jx:jx + 1, :].to_broadcast([128, D]), writes=[('gt', jx)])
                    S.op('dve', lambda e: e.tensor_tensor(out=gtile[jx][:], in0=gtile[jx][:], in1=gpb[:], op=ALU.mult),
                         reads=[('xr', 0), ('gt', jx)], writes=[('gt', jx)])
                tiles = list(range(NT)) if not last else list(range(2, NT))

                def stageA(it):
                    i = tiles[it]
                    s = it % NB3
                    s2 = it % 2
                    sgo, szm, st8, ym = sgo_[s2], szm_[s2], st8_[s2], ym_[s2]
                    k2 = lambda n: (n, s2)
                    rs_ = slice(i * 128, (i + 1) * 128)
                    S.dma('sp', hfs[s][:].rearrange("p h e -> p (h e)"), hfb[0][rs_, :], writes=[('hfs', s)])
                    S.dma('sp', hbs[s][:].rearrange("p h e -> p (h e)"), hfb[1][rs_, :], writes=[('hbs', s)])
                    S.dma('sp', ot[s][:], otm[rs_, :], writes=[('ot', s)])
                    S.dma('sp', zt[s][:], zmtm[rs_, :], writes=[('zt', s)])
                    S.dma('sp', xr[s][:], src[rs_, :], writes=[('xr', s)])
                    S.dma('sp', yT[s][:, 0:8, :], ycT[:, rs_].rearrange("(j p) t -> p j t", p=128), writes=[('yTc', s)])
                    S.op('act', lambda e: e.activation(out=sgo[:], in_=ot[s][:], func=AF.Sigmoid), reads=[('ot', s)], writes=[k2('sgo')])
                    S.op('act', lambda e: e.activation(out=szm[:], in_=zt[s][:], func=AF.Silu), reads=[('zt', s)], writes=[k2('szm')])
                    S.op('pool', lambda e: e.tensor_tensor(out=sgo[:], in0=sgo[:], in1=ghb[:], op=ALU.mult),
                         reads=[k2('sgo'), 'ghb'], writes=[k2('sgo')])
                    S.op('pool', lambda e: e.tensor_tensor(out=sgo[:], in0=sgo[:], in1=szm[:], op=ALU.mult),
                         reads=[k2('sgo'), k2('szm')], writes=[k2('sgo')])
                    S.op('pool', lambda e: e.tensor_tensor(out=hfs[s][:], in0=hfs[s][:], in1=hbs[s][:], op=ALU.add),
                         reads=[('hfs', s), ('hbs', s)], writes=[('hfs', s)])
                    S.op('dve', lambda e: e.tensor_reduce(out=st8[:, 0, :], in_=hfs[s][:], axis=AX.X, op=ALU.add),
                         reads=[('hfs', s)], writes=[k2('st8_0')])
                    S.op('act', lambda e: e.activation(out=hsq[:], in_=hfs[s][:], func=AF.Square), reads=[('hfs', s)], writes=['hsq'])
                    S.op('dve', lambda e: e.tensor_reduce(out=st8[:, 1, :], in_=hsq[:], axis=AX.X, op=ALU.add),
                         reads=['hsq'], writes=[k2('st8_1')])
                    S.op('dve', lambda e: e.tensor_scalar(out=st8[:, 2, :], in0=st8[:, 0, :], scalar1=1.0 / DH, scalar2=None, op0=ALU.mult),
                         reads=[k2('st8_0')], writes=[k2('st8_2')])
                    S.op('dve', lambda e: e.tensor_tensor(out=st8[:, 3, :], in0=st8[:, 2, :], in1=st8[:, 2, :], op=ALU.mult),
                         reads=[k2('st8_2')], writes=[k2('st8_3')])
                    S.op('dve', lambda e: e.scalar_tensor_tensor(out=st8[:, 4, :], in0=st8[:, 1, :], scalar=1.0 / DH, in1=st8[:, 3, :],
                                                                 op0=ALU.mult, op1=ALU.subtract),
                         reads=[k2('st8_1'), k2('st8_3')], writes=[k2('st8_4')])
                    S.op('dve', lambda e: e.tensor_scalar(out=st8[:, 4, :], in0=st8[:, 4, :], scalar1=EPS, scalar2=None, op0=ALU.add),
                         reads=[k2('st8_4')], writes=[k2('st8_4')])
                    S.op('pool', lambda e: e.tensor_tensor(out=st8[:, 5, :], in0=st8[:, 4, :], in1=neghalf[:, 0:H], op=ALU.pow),
                         reads=[k2('st8_4')], writes=[k2('st8_5')])
                    S.op('dve', lambda e: e.scalar_tensor_tensor(out=st8[:, 6, :], in0=st8[:, 2, :], scalar=-1.0, in1=st8[:, 5, :],
                                                                 op0=ALU.mult, op1=ALU.mult),
                         reads=[k2('st8_2'), k2('st8_5')], writes=[k2('st8_6')])
                    for h in range(H):
                        S.op('act', lambda e: e.activation(out=hfs[s][:, h, :], in_=hfs[s][:, h, :], func=AF.Identity,
                                                           scale=st8[:, 5, h:h + 1], bias=st8[:, 6, h:h + 1]),
                             reads=[('hfs', s), k2('st8_5'), k2('st8_6')], writes=[('hfs', s)])
                    hflat = hfs[s][:].rearrange("p h e -> p (h e)")
                    S.op('dve', lambda e: e.tensor_tensor(out=ym[:], in0=hflat, in1=sgo[:], op=ALU.mult),
                         reads=[('hfs', s), k2('sgo')], writes=[k2('ym')])
                    for half in range(2):
                        for jj in range(4):
                            j = half * 4 + jj
                            S.op('pe', lambda e: e.transpose(out=psy[half][:, jj, :], in_=ym[:, j * 128:(j + 1) * 128], identity=ident_b[:]),
                                 reads=[k2('ym')], excl=[('psy', half)])
                        if half == 0:
                            S.op('act', lambda e: e.activation(out=yT[s][:, 8:12, :], in_=psy[0][:], func=AF.Copy),
                                 excl=[('psy', 0)], writes=[('yTm', s, 0)])
                        else:
                            S.op('dve', lambda e: e.tensor_copy(out=yT[s][:, 12:16, :], in_=psy[1][:]),
                                 excl=[('psy', 1)], writes=[('yTm', s, 1)])

                def stageB_pe(it):
                    s = it % NB3
                    for nn in range(4):
                        for k in range(16):
                            rk = [('yTc', s)] if k < 8 else [('yTm', s, (k - 8) // 4)]
                            S.op('pe', lambda e: e.matmul(pso[:, nn * 512:(nn + 1) * 512], lhsT=yT[s][:, k, :],
                                                          rhs=wo[:, k, nn * 512:(nn + 1) * 512], start=(k == 0), stop=(k == 15)),
                                 reads=rk + [('wo', nn // 2)], excl=[('pso', nn)])

                def stageB_ep(it):
                    i = tiles[it]
                    s = it % NB3
                    sx = it % 2
                    jx = 1 if i < 2 else 0
                    rs_ = slice(i * 128, (i + 1) * 128)
                    for nn in range(4):
                        cs_ = slice(nn * 512, (nn + 1) * 512)
                        S.op('act', lambda e: e.activation(out=sqj[:], in_=pso[:, cs_], func=AF.Square, accum_out=st1[:, nn:nn + 1]),
                             excl=[('pso', nn)], writes=['sqj', ('st1', nn)])
                        S.op('dve', lambda e: e.tensor_tensor(out=xo[sx][:, cs_], in0=pso[:, cs_], in1=gtile[jx][:, cs_], op=ALU.mult),
                             reads=[('gt', jx)], excl=[('pso', nn)], writes=[('xo', sx, nn)])
                    S.op('dve', lambda e: e.tensor_reduce(out=st1[:, 4:5], in_=st1[:, 0:4], axis=AX.X, op=ALU.add),
                         reads=[('st1', nn) for nn in range(4)], writes=['st1_s'])
                    S.op('dve', lambda e: e.tensor_scalar(out=st1[:, 5:6], in0=st1[:, 4:5], scalar1=1.0 / D, scalar2=EPS,
                                                          op0=ALU.mult, op1=ALU.add),
                         reads=['st1_s'], writes=['st1_1'])
                    S.op('pool', lambda e: e.tensor_tensor(out=st1[:, 6:7], in0=st1[:, 5:6], in1=neghalf[:, 0:1], op=ALU.pow),
                         reads=['st1_1'], writes=['st1_2'])
                    S.op('dve', lambda e: e.scalar_tensor_tensor(out=xo[sx][:], in0=xo[sx][:], scalar=st1[:, 6:7], in1=xr[s][:],
                                                                 op0=ALU.mult, op1=ALU.add),
                         reads=[('xo', sx, nn) for nn in range(4)] + ['st1_2', ('xr', s)], writes=[('xo', sx, nn) for nn in range(4)])
                    S.dma('sp', xout[rs_, :], xo[sx][:], reads=[('xo', sx, nn) for nn in range(4)], writes=[('xout', i)])

                stageA(0)
                if len(tiles) > 1:
                    stageA(1)
                for it in range(len(tiles)):
                    stageB_pe(it)
                    if it + 2 < len(tiles):
                        stageA(it + 2)
                    stageB_ep(it)
                S.barrier()
        S.barrier()
    return nc, S


_CACHE = {}


def kernel(x, c, ctx, c_ctx, w_ada, b_ada, g_pre, g_post, w_in, b_gate, w_dw, b_dw, ln_g, ln_b, w_pw2,
           g_head, w_out):
    if 'nc' not in _CACHE:
        _CACHE['nc'] = build()[0]
    nc = _CACHE['nc']
    f = lambda a: np.ascontiguousarray(np.asarray(a, dtype=np.float32))
    shared = {"w_ada": f(w_ada), "b_ada": f(b_ada), "g_pre": f(g_pre), "g_post": f(g_post), "w_in": f(w_in),
              "b_gate": f(b_gate), "w_dw": f(w_dw), "b_dw": f(b_dw), "ln_g": f(ln_g), "ln_b": f(ln_b),
              "w_pw2": f(w_pw2), "g_head": f(g_head), "w_out": f(w_out)}
    x = f(x); ctx = f(ctx); c = f(c); c_ctx = f(c_ctx)
    in_maps = []
    for core in range(8):
        b = core % 4
        m = dict(shared)
        m["xin"] = np.ascontiguousarray(np.concatenate([ctx[b], x[b]], axis=0))
        m["cc"] = np.ascontiguousarray(np.stack([c[b], c_ctx], axis=0))
        in_maps.append(m)
    res = run_bass_kernel_spmd(nc, in_maps, core_ids=list(range(8)))
    out = np.stack([np.asarray(res.results[b]["xout"])[NCTX:] for b in range(4)], axis=0)
    return out.astype(np.float32)
```

```python
import contextlib
import numpy as np
import concourse.bass as bass
import concourse.mybir as mybir
from concourse.bass_utils import run_bass_kernel_spmd

F32 = mybir.dt.float32
BF16 = mybir.dt.bfloat16
AF = mybir.ActivationFunctionType
ALU = mybir.AluOpType
AX = mybir.AxisListType

D = 2048
NCTX = 256
NLAT = 2048
T = NCTX + NLAT
NT = T // 128
DEPTH = 4
WC = 1024
H = 8
DH = 128
CH = 128
NCC = NCTX // CH
NCH = T // CH
NIN = 8224
OA, OG, OZ, OQ, OK_, OV, OO, OZM, OGT = 0, 1024, 2048, 3072, 4096, 5120, 6144, 7168, 8192
EPS = 1e-6
NEG = -1e30
TG = [(0, 256)] + [(256 + 512 * i, 512) for i in range(4)]


class Sched:
    EPOCH = 30000
    NDMA = 12

    def __init__(self, nc):
        self.nc = nc
        self.engs = {'pe': nc.tensor, 'act': nc.scalar, 'dve': nc.vector,
                     'pool': nc.gpsimd, 'sp': nc.sync}
        self.cnt = {e: 0 for e in self.engs}
        self.sems = {e: [] for e in self.engs}
        self.seen = {e: {} for e in self.engs}
        self.res = {}
        self.dq = {}
        self.nsem = 0
        self.nwait = 0
        self.nins = 0

    def _newsem(self, name):
        self.nsem += 1
        return self.nc.alloc_semaphore(name=name)

    def _esem(self, e, n):
        ep = (n - 1) // self.EPOCH
        while len(self.sems[e]) <= ep:
            self.sems[e].append(self._newsem(f"s_{e}_{len(self.sems[e])}"))
        return self.sems[e][ep], (n - 1) % self.EPOCH + 1

    def _wait(self, e, dep):
        if dep[0] == 'e':
            sem, val = self._esem(dep[1], dep[2])
        else:
            sem, val = dep[1], dep[2]
        k = id(sem)
        if self.seen[e].get(k, 0) >= val:
            return
        self.seen[e][k] = val
        self.engs[e].wait_ge(sem, val)
        self.nwait += 1

    def _deps(self, e, reads, writes, excl):
        deps = []
        for r in reads:
            st = self.res.get(r)
            if st and st['w'] is not None:
                deps.append(('raw', st['w']))
        for w in writes:
            st = self.res.get(w)
            if st:
                if st['w'] is not None:
                    deps.append(('waw', st['w']))
                for d in st['r'].values():
                    deps.append(('war', d))
        for w in excl:
            st = self.res.get(w)
            if st:
                if st['w'] is not None:
                    deps.append(('x', st['w']))
                for d in st['r'].values():
                    deps.append(('x', d))
        for kind, d in deps:
            if d[0] == 'e' and d[1] == e:
                if e == 'pe' or kind in ('war', 'x'):
                    continue
            self._wait(e, d)

    def _commit(self, me, reads, writes, excl, ekey):
        for w in writes:
            self.res[w] = {'w': me, 'r': {}}
        for r in reads:
            st = self.res.setdefault(r, {'w': None, 'r': {}})
            st['r'][ekey] = me
        for w in excl:
            self.res[w] = {'w': me, 'r': {}}

    def op(self, e, fn, reads=(), writes=(), excl=()):
        self._deps(e, reads, writes, excl)
        ins = fn(self.engs[e])
        self.cnt[e] += 1
        n = self.cnt[e]
        sem, val = self._esem(e, n)
        ins.then_inc(sem, 1)
        self.nins += 1
        me = ('e', e, n)
        self._commit(me, reads, writes, excl, e)
        return me

    def dma(self, q, out, in_, reads=(), writes=(), **kw):
        self._deps(q, reads, writes, ())
        st = self.dq.setdefault(q, {'sems': [], 'vals': [], 'i': 0})
        i = st['i'] % self.NDMA
        if len(st['sems']) <= i:
            st['sems'].append(self._newsem(f"d_{q}_{i}"))
            st['vals'].append(0)
        sem = st['sems'][i]
        if st['vals'][i] > 0:
            self._wait(q, ('d', sem, st['vals'][i]))
        st['vals'][i] += 16
        st['i'] += 1
        self.engs[q].dma_start(out=out, in_=in_, **kw).then_inc(sem, 16)
        self.nins += 1
        me = ('d', sem, st['vals'][i])
        self._commit(me, reads, writes, (), ('dma', q, i))
        return me

    def barrier(self):
        for e in self.engs:
            for f in self.engs:
                if f != e and self.cnt[f] > 0:
                    self._wait(e, ('e', f, self.cnt[f]))
            for q, st in self.dq.items():
                for sem, val in zip(st['sems'], st['vals']):
                    if val > 0:
                        self._wait(e, ('d', sem, val))
        self.res.clear()


def bc(ap, axis, n):
    a = ap.unsqueeze(axis)
    shp = list(a.shape)
    shp[axis] = n
    return a.to_broadcast(shp)


def build(nlayers=DEPTH, debug=False):
    nc = bass.Bass("TRN2", target_bir_lowering=False)
    S = Sched(nc)

    def din(name, shape):
        return nc.dram_tensor(name, shape, F32, kind="ExternalInput").ap()

    xin = din("xin", [T, D])
    cc = din("cc", [2, D])
    w_ada = din("w_ada", [DEPTH, D, 3 * D])
    b_ada = din("b_ada", [DEPTH, 3 * D])
    g_pre = din("g_pre", [DEPTH, D])
    g_post = din("g_post", [DEPTH, D])
    w_in = din("w_in", [DEPTH, D, NIN])
    b_gate = din("b_gate", [DEPTH, 32])
    w_dw = din("w_dw", [DEPTH, 31, WC])
    b_dw = din("b_dw", [DEPTH, WC])
    ln_g = din("ln_g", [DEPTH, WC])
    ln_b = din("ln_b", [DEPTH, WC])
    w_pw2 = din("w_pw2", [DEPTH, WC, WC])
    g_head = din("g_head", [DEPTH, WC])
    w_out = din("w_out", [DEPTH, D, D])
    xout = nc.dram_tensor("xout", [T, D], F32, kind="ExternalOutput").ap()

    skind = "ExternalOutput" if debug else "Internal"

    def dscr(name, shape, dt):
        return nc.dram_tensor(name, shape, dt, kind=skind).ap()

    ycT = dscr("ycT", [WC, T], BF16)
    qT_d = dscr("qT_d", [H, DH, T], BF16)
    kT_d = dscr("kT_d", [H, DH, T], BF16)
    ktm = dscr("ktm", [T, WC], BF16)
    vtm = dscr("vtm", [T, WC], BF16)
    otm = dscr("otm", [T, WC], BF16)
    zmtm = dscr("zmtm", [T, WC], BF16)
    hfb = [dscr("hf_d", [T, WC], F32), dscr("hb_d", [T, WC], F32)]
    adaD = dscr("adaD", [2, D], F32)
    dbg = {}
    if debug:
        dbg['hT'] = dscr("dbg_hT", [128, 16, T], BF16)
        dbg['convT'] = dscr("dbg_convT", [128, 8, T], BF16)
        dbg['gates'] = dscr("dbg_gates", [4, 40, T], F32)
        dbg['etm'] = dscr("dbg_etm", [2, CH, NCH * 16], F32)
        dbg['w0b'] = dscr("dbg_w0b", [128, NCH * 16], F32)

    gs = contextlib.ExitStack()
    with gs:
        def GT(name, shape, dt):
            return gs.enter_context(nc.sbuf_tensor(name, shape, dt))

        ident_f = GT("ident_f", [128, 128], F32)
        ident_b = GT("ident_b", [128, 128], BF16)
        ones_b = GT("ones_b", [128, 128], BF16)
        ones_f = GT("ones_f", [128, 128], F32)
        mask2 = GT("mask2", [CH, 2, CH], F32)
        neghalf = GT("neghalf", [128, 512], F32)
        g_preT = GT("g_preT", [128, DEPTH * 16], F32)
        ln_gT = GT("ln_gT", [128, DEPTH * 8], F32)
        ln_bT = GT("ln_bT", [128, DEPTH * 8], F32)
        b_dwT = GT("b_dwT", [128, DEPTH * 8], F32)
        b_adaT = GT("b_adaT", [128, DEPTH * 48], F32)
        w_dwT = GT("w_dwT", [128, DEPTH * 8, 31], F32)
        bgI = GT("bgI", [40, DEPTH], F32)
        bgF = GT("bgF", [40, DEPTH], F32)
        nbgF = GT("nbgF", [40, DEPTH], F32)
        cT = GT("cT", [128, 16, 2], BF16)
        adaT = GT("adaT", [128, 48, 2], F32)
        s1T = GT("s1T", [128, 16, 2], F32)
        shT = GT("shT", [128, 16, 2], F32)
        ones_col = GT("ones_col", [64, 1], BF16)
        sel16 = GT("sel16", [40, 16], F32)

        ss = contextlib.ExitStack()
        with ss:
            stg = [ss.enter_context(nc.sbuf_tensor(f"stg{i}", [128, 128], F32)) for i in range(2)]
            wst = ss.enter_context(nc.sbuf_tensor("wst", [31, DEPTH, WC], F32))
            c32 = ss.enter_context(nc.sbuf_tensor("c32", [32, 128], F32))
            c32s = ss.enter_context(nc.sbuf_tensor("c32s", [32, 128], F32))
            pst = [ss.enter_context(nc.psum_tensor(f"pst{i}", [128, 512], F32)) for i in range(2)]

            S.op('pool', lambda e: e.memset(ident_f[:], 0.0), writes=['ident_f'])
            S.op('pool', lambda e: e.affine_select(out=ident_f[:], in_=ident_f[:], pattern=[[-1, 128]],
                                                   compare_op=ALU.not_equal, fill=1.0, base=0, channel_multiplier=1),
                 reads=['ident_f'], writes=['ident_f'])
            S.op('dve', lambda e: e.tensor_copy(out=ident_b[:], in_=ident_f[:]), reads=['ident_f'], writes=['ident_b'])
            S.op('dve', lambda e: e.tensor_copy(out=sel16[:, 0:8], in_=ident_f[0:40, 0:8]), reads=['ident_f'], writes=['sel16a'])
            S.op('dve', lambda e: e.tensor_copy(out=sel16[:, 8:16], in_=ident_f[0:40, 32:40]), reads=['ident_f'], writes=['sel16b'])
            S.op('pool', lambda e: e.memset(ones_b[:], 1.0), writes=['ones_b'])
            S.op('pool', lambda e: e.memset(ones_f[:], 1.0), writes=['ones_f'])
            S.op('pool', lambda e: e.memset(ones_col[:], 1.0), writes=['ones_col'])
            S.op('pool', lambda e: e.memset(neghalf[:], -0.5), writes=['neghalf'])
            S.op('pool', lambda e: e.memset(mask2[:], 1.0), writes=['mask2'])
            S.op('pool', lambda e: e.affine_select(out=mask2[:, 0, :], in_=mask2[:, 0, :], pattern=[[1, CH]],
                                                   compare_op=ALU.is_ge, fill=0.0, base=0, channel_multiplier=-1),
                 reads=['mask2'], writes=['mask2'])
            S.op('pool', lambda e: e.affine_select(out=mask2[:, 1, :], in_=mask2[:, 1, :], pattern=[[-1, CH]],
                                                   compare_op=ALU.is_ge, fill=0.0, base=0, channel_multiplier=1),
                 reads=['mask2'], writes=['mask2'])
            for t_ in (bgI, bgF):
                S.op('pool', lambda e: e.memset(t_[:], 0.0), writes=[t_.name if hasattr(t_, 'name') else id(t_)])
            S.barrier()
            for (dst, col0, r0) in ((bgI, 0, 0), (bgF, 8, 0), (bgI, 16, 32), (bgF, 24, 32)):
                S.dma('sp', dst[r0:r0 + 8, :], b_gate[:, col0:col0 + 8].rearrange("l h -> h l"),
                      writes=[('bg', col0)], allow_slow_non_contiguous=True)
            S.barrier()
            S.op('dve', lambda e: e.tensor_scalar(out=nbgF[:], in0=bgF[:], scalar1=-1.0, scalar2=None, op0=ALU.mult),
                 writes=['nbgF'])

            tcount = [0]

            def load_T(dst, src_rows, R):
                i = tcount[0] % 2
                tcount[0] += 1
                S.dma('sp', stg[i][0:R, :], src_rows, writes=[('stg', i)])
                S.op('pe', lambda e: e.transpose(out=pst[i][:, 0:R], in_=stg[i][0:R, :], identity=ident_f[0:R, 0:R]),
                     reads=[('stg', i)], excl=[('pst', i)])
                S.op('dve', lambda e: e.tensor_copy(out=dst, in_=pst[i][:, 0:R]), excl=[('pst', i)], writes=[('ld', tcount[0])])

            load_T(g_preT[:, :], g_pre.rearrange("l (k p) -> (l k) p", p=128), 64)
            load_T(ln_gT[:, :], ln_g.rearrange("l (k p) -> (l k) p", p=128), 32)
            load_T(ln_bT[:, :], ln_b.rearrange("l (k p) -> (l k) p", p=128), 32)
            load_T(b_dwT[:, :], b_dw.rearrange("l (k p) -> (l k) p", p=128), 32)
            bav = b_ada.rearrange("l (k p) -> (l k) p", p=128)
            load_T(b_adaT[:, 0:128], bav[0:128, :], 128)
            load_T(b_adaT[:, 128:192], bav[128:192, :], 64)
            S.dma('sp', wst[:], w_dw.rearrange("l k c -> k l c"), writes=['wst'])
            for l in range(DEPTH):
                for j in range(8):
                    i = tcount[0] % 2
                    tcount[0] += 1
                    S.op('pe', lambda e: e.transpose(out=pst[i][:, 0:31], in_=wst[0:31, l, j * 128:(j + 1) * 128],
                                                     identity=ident_f[0:31, 0:31]),
                         reads=['wst'], excl=[('pst', i)])
                    S.op('dve', lambda e: e.tensor_copy(out=w_dwT[:, l * 8 + j, :], in_=pst[i][:, 0:31]),
                         excl=[('pst', i)], writes=[('wdw', l, j)])
            S.dma('sp', c32[:], cc.rearrange("j (k p) -> (j k) p", p=128), writes=['c32'])
            S.op('act', lambda e: e.activation(out=c32s[:], in_=c32[:], func=AF.Silu), reads=['c32'], writes=['c32s'])
            S.op('pe', lambda e: e.transpose(out=pst[0][:, 0:32], in_=c32s[:], identity=ident_f[0:32, 0:32]),
                 reads=['c32s'], excl=[('pst', 0)])
            S.op('dve', lambda e: e.tensor_copy(out=cT[:].rearrange("p k j -> p j k"),
                                                in_=pst[0][:, 0:32].rearrange("p (j k) -> p j k", j=2)),
                 excl=[('pst', 0)], writes=['cT'])
            S.barrier()

        for l in range(nlayers):
            src = xin if l == 0 else xout
            last = (l == DEPTH - 1)
            Xs = contextlib.ExitStack()
            etm = [Xs.enter_context(nc.sbuf_tensor(f"etm{q}_{l}", [CH, NCH, 16], F32)) for q in range(2)]
            w0b = Xs.enter_context(nc.sbuf_tensor(f"w0b{l}", [128, NCH, 16], F32))
            Gs = contextlib.ExitStack()
            GI = Gs.enter_context(nc.sbuf_tensor(f"GI{l}", [40, T], F32))
            GF = Gs.enter_context(nc.sbuf_tensor(f"GF{l}", [40, T], F32))
            Ls = contextlib.ExitStack()
            with Ls:
                hT = Ls.enter_context(nc.sbuf_tensor(f"hT{l}", [128, 16, T], BF16))

                As = contextlib.ExitStack()
                with As:
                    wb = [As.enter_context(nc.sbuf_tensor(f"wbA{l}_{i}", [128, 16, 512], BF16)) for i in range(2)]
                    psA = As.enter_context(nc.psum_tensor(f"psA{l}", [128, 48, 2], F32))
                    for n in range(12):
                        s = n % 2
                        S.dma('pool', wb[s][:], w_ada[l, :, n * 512:(n + 1) * 512].rearrange("(k p) n -> p k n", p=128),
                              writes=[('wb', s)])
                        for m in range(4):
                            blk = n * 4 + m
                            for k in range(16):
                                S.op('pe', lambda e: e.matmul(psA[:, blk, :], lhsT=wb[s][:, k, m * 128:(m + 1) * 128],
                                                              rhs=cT[:, k, :], start=(k == 0), stop=(k == 15)),
                                     reads=[('wb', s)], excl=['psA'])
                    S.op('dve', lambda e: e.tensor_tensor(out=adaT[:], in0=psA[:],
                                                          in1=bc(b_adaT[:, l * 48:(l + 1) * 48], 2, 2), op=ALU.add),
                         excl=['psA'], writes=['adaT'])
                    S.op('dve', lambda e: e.scalar_tensor_tensor(out=s1T[:], in0=adaT[:, 16:32, :], scalar=1.0,
                                                                 in1=bc(g_preT[:, l * 16:(l + 1) * 16], 2, 2),
                                                                 op0=ALU.add, op1=ALU.mult),
                         reads=['adaT'], writes=['s1T'])
                    S.op('dve', lambda e: e.tensor_copy(out=shT[:], in_=adaT[:, 0:16, :]), reads=['adaT'], writes=['shT'])
                    for jx in range(2):
                        S.dma('sp', adaD[jx, :].rearrange("(c p) -> p c", p=128), adaT[:, 32:48, jx], reads=['adaT'],
                              writes=[('adaD', jx)], allow_slow_non_contiguous=True)
                    S.barrier()

                Bs = contextlib.ExitStack()
                with Bs:
                    xt = [Bs.enter_context(nc.sbuf_tensor(f"xt{l}_{i}", [128, D], F32)) for i in range(2)]
                    xn = [Bs.enter_context(nc.sbuf_tensor(f"xn{l}_{i}", [128, D], BF16)) for i in range(2)]
                    tmpf = Bs.enter_context(nc.sbuf_tensor(f"tmpf{l}", [128, 8, 128], F32))
                    stt = Bs.enter_context(nc.sbuf_tensor(f"stt{l}", [128, 3 * NT], F32))
                    psT = [Bs.enter_context(nc.psum_tensor(f"psT{l}_{i}", [128, 8, 128], BF16)) for i in range(2)]
                    for i in range(NT):
                        s = i % 2
                        jx = 1 if i < 2 else 0
                        S.dma('sp', xt[s][:], src[i * 128:(i + 1) * 128, :], writes=[('xt', s)])
                        S.op('act', lambda e: e.activation(out=xn[s][:], in_=xt[s][:], func=AF.Square,
                                                           accum_out=stt[:, i:i + 1]),
                             reads=[('xt', s)], writes=[('xn', s), ('ss', i)])
                        S.op('dve', lambda e: e.tensor_scalar(out=stt[:, NT + i:NT + i + 1], in0=stt[:, i:i + 1],
                                                              scalar1=1.0 / D, scalar2=EPS, op0=ALU.mult, op1=ALU.add),
                             reads=[('ss', i)], writes=[('ms', i)])
                        S.op('pool', lambda e: e.tensor_tensor(out=stt[:, 2 * NT + i:2 * NT + i + 1],
                                                               in0=stt[:, NT + i:NT + i + 1], in1=neghalf[:, 0:1], op=ALU.pow),
                             reads=[('ms', i)], writes=[('rs', i)])
                        S.op('act', lambda e: e.activation(out=xn[s][:], in_=xt[s][:], func=AF.Copy,
                                                           scale=stt[:, 2 * NT + i:2 * NT + i + 1]),
                             reads=[('xt', s), ('rs', i)], writes=[('xn', s)])
                        for hh in range(2):
                            for k8 in range(8):
                                k = hh * 8 + k8
                                S.op('pe', lambda e: e.transpose(out=psT[hh][:, k8, :], in_=xn[s][:, k * 128:(k + 1) * 128],
                                                                 identity=ident_b[:]),
                                     reads=[('xn', s)], excl=[('psT', hh)])
                            S.op('dve', lambda e: e.tensor_tensor(out=tmpf[:], in0=psT[hh][:],
                                                                  in1=bc(s1T[:, hh * 8:(hh + 1) * 8, jx], 2, 128), op=ALU.mult),
                                 reads=['s1T'], excl=[('psT', hh)], writes=['tmpf'])
                            S.op('dve', lambda e: e.tensor_tensor(out=hT[:, hh * 8:(hh + 1) * 8, i * 128:(i + 1) * 128],
                                                                  in0=tmpf[:],
                                                                  in1=bc(shT[:, hh * 8:(hh + 1) * 8, jx], 2, 128), op=ALU.add),
                                 reads=['tmpf', 'shT'], writes=[('hT', i)])
                    if debug and l == 0:
                        S.dma('sp', dbg['hT'], hT[:], reads=[('hT', i) for i in range(NT)])
                    S.barrier()
                hT_all = [('hT', i) for i in range(NT)]

                Cs = contextlib.ExitStack()
                with Cs:
                    wb = [Cs.enter_context(nc.sbuf_tensor(f"wbC{l}_{i}", [128, 16, 512], BF16)) for i in range(2)]
                    convT = Cs.enter_context(nc.sbuf_tensor(f"convT{l}", [128, 8, T], BF16))
                    upl = Cs.enter_context(nc.sbuf_tensor(f"upl{l}", [128, 64 * 64], BF16))
                    upc = Cs.enter_context(nc.sbuf_tensor(f"upc{l}", [128, NCTX + 30], BF16))
                    dg = Cs.enter_context(nc.sbuf_tensor(f"dg{l}", [128, 31, 128], BF16))
                    sig = [Cs.enter_context(nc.sbuf_tensor(f"sig{l}_{i}", [128, 512], F32)) for i in range(2)]
                    sq = upl[:, :].rearrange("p (j t) -> p j t", t=512)
                    mean = Cs.enter_context(nc.sbuf_tensor(f"mean{l}", [128, 512], F32))
                    rstd = Cs.enter_context(nc.sbuf_tensor(f"rstd{l}", [128, 512], F32))
                    t1 = [Cs.enter_context(nc.sbuf_tensor(f"t1{l}_{i}", [128, 512], F32)) for i in range(2)]
                    yco = [Cs.enter_context(nc.sbuf_tensor(f"yco{l}_{i}", [128, 512], BF16)) for i in range(2)]
                    psa = [Cs.enter_context(nc.psum_tensor(f"psa{l}_{i}", [128, 512], F32)) for i in range(2)]
                    psg = [Cs.enter_context(nc.psum_tensor(f"psg{l}_{i}", [128, 512], F32)) for i in range(2)]
                    psc = [Cs.enter_context(nc.psum_tensor(f"psc{l}_{i}", [128, 512], F32)) for i in range(2)]
                    S.op('pool', lambda e: e.memset(upc[:], 0.0), writes=['upc'])
                    uplh = upl[:, 0:32 * 94].rearrange("p (r c) -> p r c", c=94)
                    uplv = upl[:, 0:62 * 64].rearrange("p (r c) -> p r c", c=64)
                    cnt = 0
                    for jp in range(4):
                        s = jp % 2
                        S.dma('pool', wb[s][:, :, 0:256],
                              w_in[l, :, OA + jp * 256:OA + (jp + 1) * 256].rearrange("(k p) n -> p k n", p=128),
                              writes=[('wb', s)])
                        S.dma('pool', wb[s][:, :, 256:512],
                              w_in[l, :, OG + jp * 256:OG + (jp + 1) * 256].rearrange("(k p) n -> p k n", p=128),
                              reads=[('wb', s)], writes=[('wb', s)])
                        for jj in range(2):
                            j = 2 * jp + jj
                            horiz = j < 4
                            if j == 0 or j == 4:
                                S.op('pool', lambda e: e.memset(upl[:], 0.0), writes=['upl'])
                            S.op('dve', lambda e: e.tensor_tensor(out=dg[:], in0=bc(ident_b[:], 1, 31),
                                                                  in1=bc(w_dwT[:, l * 8 + j, :], 2, 128), op=ALU.mult),
                                 writes=['dg'])
                            for n, (t0, tn) in enumerate(TG):
                                b = cnt % 2
                                cnt += 1
                                for k in range(16):
                                    S.op('pe', lambda e: e.matmul(psa[b][:, 0:tn], lhsT=wb[s][:, k, jj * 128:(jj + 1) * 128],
                                                                  rhs=hT[:, k, t0:t0 + tn], start=(k == 0), stop=(k == 15)),
                                         reads=[('wb', s)], excl=[('psa', b)])
                                for k in range(16):
                                    S.op('pe', lambda e: e.matmul(psg[b][:, 0:tn],
                                                                  lhsT=wb[s][:, k, 256 + jj * 128:256 + (jj + 1) * 128],
                                                                  rhs=hT[:, k, t0:t0 + tn], start=(k == 0), stop=(k == 15)),
                                         reads=[('wb', s)], excl=[('psg', b)])
                                S.op('act', lambda e: e.activation(out=sig[b][:, 0:tn], in_=psg[b][:, 0:tn], func=AF.Sigmoid),
                                     excl=[('psg', b)], writes=[('sig', b)])
                                if n == 0:
                                    uo = upc[:, 15:15 + NCTX]
                                    ui = psa[b][:, 0:tn]
                                    si = sig[b][:, 0:tn]
                                    ukey = 'upc'
                                elif horiz:
                                    r0 = 8 * (n - 1)
                                    uo = uplh[:, r0:r0 + 8, 15:79]
                                    ui = psa[b][:, :].rearrange("p (r c) -> p r c", c=64)
                                    si = sig[b][:, :].rearrange("p (r c) -> p r c", c=64)
                                    ukey = 'upl'
                                else:
                                    r0 = 15 + 8 * (n - 1)
                                    uo = uplv[:, r0:r0 + 8, :]
                                    ui = psa[b][:, :].rearrange("p (r c) -> p r c", c=64)
                                    si = sig[b][:, :].rearrange("p (r c) -> p r c", c=64)
                                    ukey = 'upl'
                                S.op('dve', lambda e: e.tensor_tensor(out=uo, in0=ui, in1=si, op=ALU.mult),
                                     reads=[('sig', b), ukey], excl=[('psa', b)], writes=[ukey])
                            for n, (t0, tn) in enumerate(TG):
                                b = cnt % 2
                                cnt += 1
                                for k in range(31):
                                    if n == 0:
                                        win = upc[:, k:k + NCTX]
                                        po = psc[b][:, 0:tn]
                                        ukey = 'upc'
                                    elif horiz:
                                        r0 = 8 * (n - 1)
                                        win = uplh[:, r0:r0 + 8, k:k + 64]
                                        po = psc[b][:, :].rearrange("p (r c) -> p r c", c=64)
                                        ukey = 'upl'
                                    else:
                                        r0 = 8 * (n - 1) + k
                                        win = uplv[:, r0:r0 + 8, :]
                                        po = psc[b][:, :].rearrange("p (r c) -> p r c", c=64)
                                        ukey = 'upl'
                                    S.op('pe', lambda e: e.matmul(po, lhsT=dg[:, k, :], rhs=win, start=(k == 0), stop=(k == 30)),
                                         reads=['dg', ukey], excl=[('psc', b)])
                                S.op('act', lambda e: e.activation(out=convT[:, j, t0:t0 + tn], in_=psc[b][:, 0:tn],
                                                                   func=AF.Identity, bias=b_dwT[:, l * 8 + j:l * 8 + j + 1]),
                                     excl=[('psc', b)], writes=[('cv', j, n)])
                    if debug and l == 0:
                        S.dma('sp', dbg['convT'], convT[:], reads=[('cv', j, n) for j in range(8) for n in range(5)])
                    for n, (t0, tn) in enumerate(TG):
                        S.op('act', lambda e: e.activation(out=sq[:, :, 0:tn], in_=convT[:, :, t0:t0 + tn], func=AF.Square),
                             reads=[('cv', j, n) for j in range(8)] + ['upl'], writes=['sq', 'upl'])
                        for j in range(8):
                            S.op('pe', lambda e: e.matmul(psa[0][:, 0:tn], lhsT=ones_b[:], rhs=convT[:, j, t0:t0 + tn],
                                                          start=(j == 0), stop=(j == 7)),
                                 reads=[('cv', j, n)], excl=[('psa', 0)])
                        for j in range(8):
                            S.op('pe', lambda e: e.matmul(psg[0][:, 0:tn], lhsT=ones_b[:], rhs=sq[:, j, 0:tn],
                                                          start=(j == 0), stop=(j == 7)),
                                 reads=['sq'], excl=[('psg', 0)])
                        S.op('act', lambda e: e.activation(out=mean[:, 0:tn], in_=psa[0][:, 0:tn], func=AF.Copy, scale=1.0 / WC),
                             excl=[('psa', 0)], writes=['mean'])
                        S.op('dve', lambda e: e.tensor_tensor(out=t1[0][:, 0:tn], in0=mean[:, 0:tn], in1=mean[:, 0:tn], op=ALU.mult),
                             reads=['mean'], writes=[('t1', 0)])
                        S.op('dve', lambda e: e.scalar_tensor_tensor(out=t1[1][:, 0:tn], in0=psg[0][:, 0:tn], scalar=1.0 / WC,
                                                                     in1=t1[0][:, 0:tn], op0=ALU.mult, op1=ALU.subtract),
                             reads=[('t1', 0)], excl=[('psg', 0)], writes=[('t1', 1)])
                        S.op('dve', lambda e: e.tensor_scalar(out=t1[1][:, 0:tn], in0=t1[1][:, 0:tn], scalar1=EPS, scalar2=None,
                                                              op0=ALU.add),
                             reads=[('t1', 1)], writes=[('t1', 1)])
                        S.op('act', lambda e: e.activation(out=t1[1][:, 0:tn], in_=t1[1][:, 0:tn], func=AF.Sqrt),
                             reads=[('t1', 1)], writes=[('t1', 1)])
                        S.op('dve', lambda e: e.reciprocal(out=rstd[:, 0:tn], in_=t1[1][:, 0:tn]),
                             reads=[('t1', 1)], writes=['rstd'])
                        for j in range(8):
                            b = j % 2
                            S.op('dve', lambda e: e.tensor_tensor(out=t1[b][:, 0:tn], in0=convT[:, j, t0:t0 + tn],
                                                                  in1=mean[:, 0:tn], op=ALU.subtract),
                                 reads=[('cv', j, n), 'mean'], writes=[('t1', b)])
                            S.op('dve', lambda e: e.tensor_tensor(out=t1[b][:, 0:tn], in0=t1[b][:, 0:tn], in1=rstd[:, 0:tn], op=ALU.mult),
                                 reads=[('t1', b), 'rstd'], writes=[('t1', b)])
                            S.op('act', lambda e: e.activation(out=convT[:, j, t0:t0 + tn], in_=t1[b][:, 0:tn], func=AF.Silu,
                                                               scale=ln_gT[:, l * 8 + j:l * 8 + j + 1],
                                                               bias=ln_bT[:, l * 8 + j:l * 8 + j + 1]),
                                 reads=[('t1', b)], writes=[('cv', j, n)])
                    wp = wb[0][:].rearrange("p k n -> p (k n)").rearrange("p (j n) -> p j n", n=WC)
                    S.dma('pool', wp, w_pw2[l].rearrange("(j p) n -> p j n", p=128),
                          reads=[('wb', 0)], writes=[('wb', 0)])
                    for zh in range(2):
                        S.dma('pool', wb[1][:], w_in[l, :, OZ + zh * 512:OZ + (zh + 1) * 512].rearrange("(k p) n -> p k n", p=128),
                              reads=[('wb', 1)], writes=[('wb', 1)])
                        for mm in range(4):
                            m = zh * 4 + mm
                            for n, (t0, tn) in enumerate(TG):
                                b = cnt % 2
                                cnt += 1
                                for j in range(8):
                                    S.op('pe', lambda e: e.matmul(psa[b][:, 0:tn], lhsT=wp[:, j, m * 128:(m + 1) * 128],
                                                                  rhs=convT[:, j, t0:t0 + tn], start=(j == 0), stop=(j == 7)),
                                         reads=[('wb', 0), ('cv', j, n)], excl=[('psa', b)])
                                for k in range(16):
                                    S.op('pe', lambda e: e.matmul(psg[b][:, 0:tn], lhsT=wb[1][:, k, mm * 128:(mm + 1) * 128],
                                                                  rhs=hT[:, k, t0:t0 + tn], start=(k == 0), stop=(k == 15)),
                                         reads=[('wb', 1)], excl=[('psg', b)])
                                S.op('act', lambda e: e.activation(out=sig[b][:, 0:tn], in_=psg[b][:, 0:tn], func=AF.Silu),
                                     excl=[('psg', b)], writes=[('sig', b)])
                                S.op('dve', lambda e: e.tensor_tensor(out=yco[b][:, 0:tn], in0=psa[b][:, 0:tn], in1=sig[b][:, 0:tn],
                                                                      op=ALU.mult),
                                     reads=[('sig', b)], excl=[('psa', b)], writes=[('yco', b)])
                                S.dma('sp', ycT[m * 128:(m + 1) * 128, t0:t0 + tn], yco[b][:, 0:tn], reads=[('yco', b)],
                                      writes=[('ycT', m, n)])
                    S.barrier()

                Ds = contextlib.ExitStack()
                with Ds:
                    wb = [Ds.enter_context(nc.sbuf_tensor(f"wbD{l}_{i}", [128, 16, 512], BF16)) for i in range(2)]
                    wg = Ds.enter_context(nc.sbuf_tensor(f"wg{l}", [128, 16, 2, 40], BF16))
                    ev = [Ds.enter_context(nc.sbuf_tensor(f"ev{l}_{i}", [128, 512], BF16)) for i in range(4)]
                    psd = [Ds.enter_context(nc.psum_tensor(f"psd{l}_{i}", [128, 512], F32)) for i in range(4)]
                    cnt = 0
                    gcnt = 0
                    for (off, dst, scl) in ((OQ, qT_d, DH ** -0.5), (OK_, kT_d, 1.0)):
                        for half in range(2):
                            s = gcnt % 2
                            gcnt += 1
                            S.dma('pool', wb[s][:], w_in[l, :, off + half * 512:off + (half + 1) * 512].rearrange("(k p) n -> p k n", p=128),
                                  writes=[('wb', s)])
                            for hb in range(4):
                                hd = half * 4 + hb
                                for n, (t0, tn) in enumerate(TG):
                                    b = cnt % 4
                                    cnt += 1
                                    for k in range(16):
                                        S.op('pe', lambda e: e.matmul(psd[b][:, 0:tn], lhsT=wb[s][:, k, hb * 128:(hb + 1) * 128],
                                                                      rhs=hT[:, k, t0:t0 + tn], start=(k == 0), stop=(k == 15)),
                                             reads=[('wb', s)], excl=[('psd', b)])
                                    if b % 2 == 0:
                                        S.op('act', lambda e: e.activation(out=ev[b][:, 0:tn], in_=psd[b][:, 0:tn], func=AF.Copy, scale=scl),
                                             excl=[('psd', b)], writes=[('ev', b)])
                                    else:
                                        S.op('dve', lambda e: e.tensor_scalar(out=ev[b][:, 0:tn], in0=psd[b][:, 0:tn], scalar1=scl,
                                                                              scalar2=None, op0=ALU.mult),
                                             excl=[('psd', b)], writes=[('ev', b)])
                                    S.dma('sp', dst[hd, :, t0:t0 + tn], ev[b][:, 0:tn], reads=[('ev', b)], writes=[('qk', off, hd, n)])
                    for (off, dst) in ((OK_, ktm), (OV, vtm), (OO, otm), (OZM, zmtm)):
                        for half in range(2):
                            s = gcnt % 2
                            gcnt += 1
                            S.dma('pool', wb[s][:], w_in[l, :, off + half * 512:off + (half + 1) * 512].rearrange("(k p) n -> p k n", p=128),
                                  writes=[('wb', s)])
                            for i in range(NT):
                                b = cnt % 4
                                cnt += 1
                                for k in range(16):
                                    S.op('pe', lambda e: e.matmul(psd[b][:, :], lhsT=hT[:, k, i * 128:(i + 1) * 128],
                                                                  rhs=wb[s][:, k, :], start=(k == 0), stop=(k == 15)),
                                         reads=[('wb', s)], excl=[('psd', b)])
                                if b % 2 == 0:
                                    S.op('act', lambda e: e.activation(out=ev[b][:], in_=psd[b][:], func=AF.Copy),
                                         excl=[('psd', b)], writes=[('ev', b)])
                                else:
                                    S.op('dve', lambda e: e.tensor_copy(out=ev[b][:], in_=psd[b][:]),
                                         excl=[('psd', b)], writes=[('ev', b)])
                                S.dma('sp', dst[i * 128:(i + 1) * 128, half * 512:(half + 1) * 512], ev[b][:], reads=[('ev', b)],
                                      writes=[('tm', off, i, half)])
                    S.op('pool', lambda e: e.memset(wg[:], 0.0), writes=['wg'])
                    for (gi, r0, c0) in ((0, 0, 0), (1, 0, 8), (0, 32, 16), (1, 32, 24)):
                        S.dma('pool', wg[:, :, gi, r0:r0 + 8],
                              w_in[l, :, OGT + c0:OGT + c0 + 8].rearrange("(k p) n -> p k n", p=128),
                              reads=['wg'], writes=['wg'], allow_slow_non_contiguous=True)
                    for n, (t0, tn) in enumerate(TG):
                        for gi in range(2):
                            b = cnt % 4
                            cnt += 1
                            for k in range(16):
                                S.op('pe', lambda e: e.matmul(psd[b][0:40, 0:tn], lhsT=wg[:, k, gi, :], rhs=hT[:, k, t0:t0 + tn],
                                                              start=(k == 0), stop=(k == 15)),
                                     reads=['wg'], excl=[('psd', b)])
                            if gi == 0:
                                S.op('act', lambda e: e.activation(out=GI[:, t0:t0 + tn], in_=psd[b][0:40, 0:tn], func=AF.Identity,
                                                                   bias=bgI[:, l:l + 1]),
                                     excl=[('psd', b)], writes=[('GI', n)])
                            else:
                                S.op('act', lambda e: e.activation(out=GF[:, t0:t0 + tn], in_=psd[b][0:40, 0:tn], func=AF.Exp,
                                                                   scale=-1.0, bias=nbgF[:, l:l + 1]),
                                     excl=[('psd', b)], writes=[('GF', n)])
                    S.barrier()
            Eps = contextlib.ExitStack()
            with Eps:
                def ET(name, shape, dt):
                    return Eps.enter_context(nc.sbuf_tensor(f"{name}{l}", shape, dt))
                PRE = ET("PRE", [40, T], F32)
                scanmask = ET("scanmask", [40, T], F32)
                S.op('pool', lambda e: e.memset(scanmask[:], 1.0), writes=['scanmask'])
                smv = scanmask[:].rearrange("p (c t) -> p c t", t=CH)
                S.op('pool', lambda e: e.memset(smv[:, :, 0:1], 0.0), reads=['scanmask'], writes=['scanmask'])
                cl = ET("cl", [40, 8, NCH], F32)
                w0x = ET("w0x", [40, NCH, 16], F32)
                psE = [Eps.enter_context(nc.psum_tensor(f"psE{l}_{i}", [128, 512], F32)) for i in range(3)]
                allG = [('GI', n) for n in range(5)] + [('GF', n) for n in range(5)]
                S.op('act', lambda e: e.activation(out=GF[:], in_=GF[:], func=AF.Ln, bias=1.0), writes=['GF'])
                S.op('dve', lambda e: e.tensor_scalar(out=GF[:], in0=GF[:], scalar1=-1.0, scalar2=None, op0=ALU.mult),
                     reads=['GF'], writes=['GF'])
                if debug and l == 0:
                    S.dma('sp', dbg['gates'][0], GI[:], reads=['GF'])
                    S.dma('sp', dbg['gates'][1], GF[:], reads=['GF'])
                S.op('dve', lambda e: e.tensor_tensor_scan(out=PRE[:], data0=scanmask[:], data1=GF[:], initial=0.0,
                                                           op0=ALU.mult, op1=ALU.add),
                     reads=['GF', 'scanmask'], writes=['PRE'])
                PREv = PRE[:].rearrange("p (c t) -> p c t", t=CH)
                GFv = GF[:].rearrange("p (c t) -> p c t", t=CH)
                GIv = GI[:].rearrange("p (c t) -> p c t", t=CH)
                S.op('dve', lambda e: e.tensor_copy(out=cl[:, 0, :], in_=PREv[:, :, CH - 1]), reads=['PRE'], writes=['cl0'])
                S.op('dve', lambda e: e.tensor_tensor(out=GF[32:40, :], in0=GF[32:40, :], in1=PRE[32:40, :], op=ALU.subtract),
                     reads=['GF', 'PRE'], writes=['GF'])
                S.op('dve', lambda e: e.tensor_tensor(out=GFv[32:40], in0=GFv[32:40], in1=bc(cl[32:40, 0, :], 2, CH), op=ALU.add),
                     reads=['GF', 'cl0'], writes=['GF'])
                S.op('dve', lambda e: e.tensor_copy(out=GF[0:32, :], in_=PRE[0:32, :]), reads=['GF', 'PRE'], writes=['GF'])
                S.op('dve', lambda e: e.tensor_tensor(out=GI[:], in0=GI[:], in1=GF[:], op=ALU.subtract), reads=['GF'], writes=['GI'])
                S.op('dve', lambda e: e.tensor_reduce(out=cl[:, 1, :], in_=GIv, axis=AX.X, op=ALU.max), reads=['GI'], writes=['cl1'])
                S.op('dve', lambda e: e.tensor_tensor(out=cl[:, 2, :], in0=cl[:, 0, :], in1=cl[:, 1, :], op=ALU.add),
                     reads=['cl0', 'cl1'], writes=['cl2'])

                def rev_ap(ap2, lo, n):
                    a = ap2[:, lo:lo + n]
                    return bass.AP(a.tensor, a.offset + (n - 1) * a.ap[1][0], [list(a.ap[0]), [-a.ap[1][0], n]])

                for (so, de) in ((0, 3), (2, 4)):
                    S.op('dve', lambda e: e.tensor_copy(out=cl[0:32, de, :], in_=cl[0:32, so, :]), reads=[f'cl{so}'], writes=[f'cl{de}'])
                    S.op('dve', lambda e: e.tensor_copy(out=cl[32:40, de, 0:NCC], in_=rev_ap(cl[32:40, so, :], 0, NCC)),
                         reads=[f'cl{so}', f'cl{de}'], writes=[f'cl{de}'])
                    S.op('dve', lambda e: e.tensor_copy(out=cl[32:40, de, NCC:NCH], in_=rev_ap(cl[32:40, so, :], NCC, NCH - NCC)),
                         reads=[f'cl{so}', f'cl{de}'], writes=[f'cl{de}'])
                S.op('dve', lambda e: e.tensor_tensor_scan(out=cl[:, 5, :], data0=cl[:, 3, :], data1=cl[:, 4, :], initial=NEG,
                                                           op0=ALU.add, op1=ALU.max),
                     reads=['cl3', 'cl4'], writes=['cl5'])
                S.op('dve', lambda e: e.tensor_tensor(out=cl[:, 6, :], in0=cl[:, 5, :], in1=cl[:, 3, :], op=ALU.subtract),
                     reads=['cl5', 'cl3'], writes=['cl6'])
                S.op('pool', lambda e: e.memset(cl[:, 7, 0:1], NEG), writes=['cl7'])
                S.op('dve', lambda e: e.tensor_copy(out=cl[:, 7, 1:NCH], in_=cl[:, 5, 0:NCH - 1]), reads=['cl5', 'cl7'], writes=['cl7'])
                S.op('dve', lambda e: e.tensor_tensor(out=cl[:, 7, :], in0=cl[:, 7, :], in1=cl[:, 6, :], op=ALU.subtract),
                     reads=['cl7', 'cl6'], writes=['cl7'])
                S.op('act', lambda e: e.activation(out=cl[:, 7, :], in_=cl[:, 7, :], func=AF.Exp), reads=['cl7'], writes=['cl7'])
                S.op('dve', lambda e: e.tensor_copy(out=cl[0:32, 4, :], in_=cl[0:32, 6, :]), reads=['cl6', 'cl4'], writes=['cl4'])
                S.op('dve', lambda e: e.tensor_copy(out=cl[32:40, 4, 0:NCC], in_=rev_ap(cl[32:40, 6, :], 0, NCC)),
                     reads=['cl6', 'cl4'], writes=['cl4'])
                S.op('dve', lambda e: e.tensor_copy(out=cl[32:40, 4, NCC:NCH], in_=rev_ap(cl[32:40, 6, :], NCC, NCH - NCC)),
                     reads=['cl6', 'cl4'], writes=['cl4'])
                S.op('dve', lambda e: e.tensor_tensor(out=GIv, in0=GIv, in1=bc(cl[:, 4, :], 2, CH), op=ALU.subtract),
                     reads=['GI', 'cl4'], writes=['GI'])
                S.op('act', lambda e: e.activation(out=GI[:], in_=GI[:], func=AF.Exp), reads=['GI'], writes=['GI'])
                S.op('dve', lambda e: e.tensor_tensor(out=GFv, in0=GFv, in1=bc(cl[:, 4, :], 2, CH), op=ALU.add),
                     reads=['GF', 'cl4'], writes=['GF'])
                S.op('act', lambda e: e.activation(out=GF[:], in_=GF[:], func=AF.Exp, scale=-1.0), reads=['GF'], writes=['GF'])
                if debug and l == 0:
                    S.dma('sp', dbg['gates'][2], GI[:], reads=['GI'])
                    S.dma('sp', dbg['gates'][3], GF[:], reads=['GF'])
                for q, (srcg, key) in enumerate(((GI, 'GI'), (GF, 'GF'))):
                    for bi, (c0, c1) in enumerate(((0, NCH),)):
                        pb = psE[bi]
                        for c in range(c0, c1):
                            S.op('pe', lambda e: e.matmul(pb[0:CH, (c - c0) * 16:(c - c0 + 1) * 16],
                                                          lhsT=srcg[0:40, c * CH:(c + 1) * CH], rhs=sel16[:, :], start=True, stop=True),
                                 reads=[key], excl=[('psE', bi)])
                        S.op('dve', lambda e: e.tensor_copy(out=etm[q][:, c0:c1, :].rearrange("p c h -> p (c h)"),
                                                            in_=pb[0:CH, 0:(c1 - c0) * 16]),
                             excl=[('psE', bi)], writes=[('etm', q, bi)])
                S.op('dve', lambda e: e.tensor_tensor(out=w0x[:], in0=bc(cl[:, 7, :], 2, 16), in1=bc(sel16[:, :], 1, NCH),
                                                      op=ALU.mult),
                     reads=['cl7'], writes=['w0x'])
                w0xf = w0x[:].rearrange("p c h -> p (c h)")
                w0bf = w0b[:].rearrange("p c h -> p (c h)")
                for i in range(1):
                    S.op('pe', lambda e: e.matmul(psE[i][:, 0:288], lhsT=ones_f[0:40, :], rhs=w0xf[:, i * 288:(i + 1) * 288],
                                                  start=True, stop=True),
                         reads=['w0x'], excl=[('psE', i)])
                    S.op('dve', lambda e: e.tensor_copy(out=w0bf[:, i * 288:(i + 1) * 288], in_=psE[i][:, 0:288]),
                         excl=[('psE', i)], writes=[('w0b', i)])
                if debug and l == 0:
                    for q in range(2):
                        S.dma('sp', dbg['etm'][q], etm[q][:].rearrange("p c h -> p (c h)"), reads=[('etm', q, bi) for bi in range(1)])
                    S.dma('sp', dbg['w0b'], w0bf, reads=[('w0b', i) for i in range(1)])
                S.barrier()
            Gs.close()

            Ss = contextlib.ExitStack()
            with Ss:
                def ST(name, shape, dt):
                    return Ss.enter_context(nc.sbuf_tensor(f"{name}{l}", shape, dt))
                SC = 2
                NSC = NCH // SC
                HA = DH + 1
                qb = [[ST(f"qb{d}_{i}_", [128, H, SC * CH], BF16) for i in range(2)] for d in range(2)]
                kb = [[ST(f"kb{d}_{i}_", [128, H, SC * CH], BF16) for i in range(2)] for d in range(2)]
                ktb = [[ST(f"ktb{d}_{i}_", [CH, SC, WC], BF16) for i in range(2)] for d in range(2)]
                vtb = [[ST(f"vtb{d}_{i}_", [CH, SC, H, HA], BF16) for i in range(2)] for d in range(2)]
                Cst = [ST(f"Cst{d}_", [128, H, HA], F32) for d in range(2)]
                Cb = [ST(f"Cb{d}_", [128, H, HA], BF16) for d in range(2)]
                PT = [[ST(f"PT{d}_{i}_", [CH, H, CH], BF16) for i in range(2)] for d in range(2)]
                kE = [[ST(f"kE{d}_{i}_", [CH, H, DH], BF16) for i in range(2)] for d in range(2)]
                dpos = [ST(f"dpos{d}_", [CH, H], F32) for d in range(2)]
                dden = [ST(f"dden{d}_", [CH, H], F32) for d in range(2)]
                hout = [[ST(f"hout{d}_{i}_", [CH, H, DH], F32) for i in range(2)] for d in range(2)]
                psS = [Ss.enter_context(nc.psum_tensor(f"psS{l}_{hf}", [CH, 4, CH], F32)) for hf in range(2)]
                psN = [Ss.enter_context(nc.psum_tensor(f"psN{l}_{i}", [CH, 3, HA], F32)) for i in range(3)]
                psC = [Ss.enter_context(nc.psum_tensor(f"psC{l}_{i}", [128, 3, HA], F32)) for i in range(3)]
                GH = [(0, 3), (3, 3), (6, 2)]

                for d in range(2):
                    S.op('pool', lambda e: e.memset(Cst[d][:], 0.0), writes=[('Cst', d, h) for h in range(H)])
                    S.op('pool', lambda e: e.memset(Cb[d][:], 0.0), writes=[('Cb', d, 0), ('Cb', d, 1)])
                    for i in range(2):
                        S.op('pool', lambda e: e.memset(vtb[d][i][:, :, :, DH:HA], 1.0), writes=[('vtb1', d, i)])

                def nat_chunk(d, p):
                    if d == 0:
                        return p
                    return NCC - 1 - p if p < NCC else NCH - 1 + NCC - p

                sc_base = {}

                def load_sc(d, sp_):
                    p0 = sp_ * SC
                    cs = sorted(nat_chunk(d, p0 + i) for i in range(SC))
                    c0 = cs[0]
                    assert cs == list(range(c0, c0 + SC))
                    s = sp_ % 2
                    t0 = c0 * CH
                    q_ = 'sp' if d == 0 else 'pool'
                    S.dma(q_, qb[d][s][:], qT_d[:, :, t0:t0 + SC * CH].rearrange("h p t -> p h t"), writes=[('qb', d, s)])
                    S.dma(q_, kb[d][s][:], kT_d[:, :, t0:t0 + SC * CH].rearrange("h p t -> p h t"), writes=[('kb', d, s)])
                    S.dma(q_, ktb[d][s][:], ktm[t0:t0 + SC * CH, :].rearrange("(c s) e -> s c e", s=CH), writes=[('ktb', d, s)])
                    for ci in range(SC):
                        S.dma(q_, vtb[d][s][:, ci, :, 0:DH],
                              vtm[t0 + ci * CH:t0 + (ci + 1) * CH, :].rearrange("s (h e) -> s h e", e=DH),
                              reads=[('vtb1', d, s)], writes=[('vtb', d, s, ci)])
                    sc_base[(d, sp_)] = c0

                def stage1(p, d):
                    sp_ = p // SC
                    s = sp_ % 2
                    par = p % 2
                    c = nat_chunk(d, p)
                    ci = c - sc_base[(d, sp_)]
                    r0 = d * 8
                    tsl = slice(ci * CH, (ci + 1) * CH)
                    Ecol = etm[0][:, c, r0:r0 + 8]
                    S.op('pool', lambda e: e.tensor_tensor(out=kE[d][par][:], in0=ktb[d][s][:, ci, :].rearrange("p (h e) -> p h e", e=DH),
                                                           in1=bc(Ecol, 2, DH), op=ALU.mult),
                         reads=[('ktb', d, s)], writes=[('kE', d, par)])
                    for hf in range(2):
                        for hh in range(4):
                            h = hf * 4 + hh
                            S.op('pe', lambda e: e.matmul(psS[hf][:, hh, :], lhsT=kb[d][s][:, h, tsl], rhs=qb[d][s][:, h, tsl],
                                                          start=True, stop=True),
                                 reads=[('kb', d, s), ('qb', d, s)], excl=[('psS', hf)])
                        for hh in range(4):
                            h = hf * 4 + hh
                            S.op('dve', lambda e: e.scalar_tensor_tensor(out=PT[d][par][:, h, :], in0=psS[hf][:, hh, :],
                                                                         scalar=Ecol[:, h:h + 1], in1=mask2[:, d, :],
                                                                         op0=ALU.mult, op1=ALU.mult),
                                 excl=[('psS', hf)], writes=[('PT', d, par, h)])

                def stage2(p, d):
                    sp_ = p // SC
                    s = sp_ % 2
                    par = p % 2
                    c = nat_chunk(d, p)
                    ci = c - sc_base[(d, sp_)]
                    r0 = d * 8
                    tsl = slice(ci * CH, (ci + 1) * CH)
                    Fcol = etm[1][:, c, r0:r0 + 8]
                    for h in range(H):
                        g, hh = h // 3, h % 3
                        vh = vtb[d][s][:, ci, h, :]
                        S.op('pe', lambda e: e.matmul(psN[g][:, hh, :], lhsT=qb[d][s][:, h, tsl], rhs=Cb[d][:, h, :],
                                                      start=True, stop=False),
                             reads=[('qb', d, s), ('Cb', d, h // 4)], excl=[('psN', g)])
                        S.op('pe', lambda e: e.matmul(psN[g][:, hh, :], lhsT=PT[d][par][:, h, :], rhs=vh, start=False, stop=True),
                             reads=[('PT', d, par, h), ('vtb', d, s, ci), ('vtb1', d, s)], excl=[('psN', g)])
                    last_step = (p + 1 >= NCH)
                    if not last_step:
                        for h in range(H):
                            g, hh = h // 3, h % 3
                            vh = vtb[d][s][:, ci, h, :]
                            S.op('pe', lambda e: e.matmul(psC[g][:, hh, :], lhsT=kE[d][par][:, h, :], rhs=vh, start=True, stop=True),
                                 reads=[('kE', d, par), ('vtb', d, s, ci), ('vtb1', d, s)], excl=[('psC', g)])
                    for g, (h0, nh) in enumerate(GH):
                        S.op('act', lambda e: e.activation(out=dpos[d][:, h0:h0 + nh], in_=psN[g][:, 0:nh, DH], func=AF.Copy),
                             excl=[('psN', g)], writes=[('dpos', d, g)])
                    S.op('dve', lambda e: e.tensor_scalar(out=dden[d][:], in0=dpos[d][:], scalar1=-1.0, scalar2=None, op0=ALU.mult),
                         reads=[('dpos', d, g) for g in range(3)], writes=[('dden', d)])
                    S.op('dve', lambda e: e.tensor_tensor(out=dden[d][:], in0=dden[d][:], in1=dpos[d][:], op=ALU.max),
                         reads=[('dden', d)] + [('dpos', d, g) for g in range(3)], writes=[('dden', d)])
                    S.op('dve', lambda e: e.tensor_tensor(out=dden[d][:], in0=dden[d][:], in1=Fcol, op=ALU.max),
                         reads=[('dden', d)], writes=[('dden', d)])
                    S.op('dve', lambda e: e.reciprocal(out=dden[d][:], in_=dden[d][:]), reads=[('dden', d)], writes=[('dden', d)])
                    for g in range(2):
                        S.op('dve', lambda e: e.tensor_tensor(out=hout[d][par][:, 3 * g:3 * g + 3, :], in0=psN[g][:, 0:3, 0:DH],
                                                              in1=bc(dden[d][:, 3 * g:3 * g + 3], 2, DH), op=ALU.mult),
                             reads=[('dden', d)], excl=[('psN', g)], writes=[('hout', d, par, h) for h in range(3 * g, 3 * g + 3)])
                    for h in range(6, H):
                        g, hh = h // 3, h % 3
                        S.op('act', lambda e: e.activation(out=hout[d][par][:, h, :], in_=psN[g][:, hh, 0:DH], func=AF.Copy,
                                                           scale=dden[d][:, h:h + 1]),
                             reads=[('dden', d)], excl=[('psN', g)], writes=[('hout', d, par, h)])
                    S.dma('sp', hfb[d][c * CH:(c + 1) * CH, :], hout[d][par][:].rearrange("p h e -> p (h e)"),
                          reads=[('hout', d, par, h) for h in range(H)], writes=[('h_d', d, c)])
                    if not last_step:
                        w0c = w0b[:, p, r0:r0 + 8]
                        w0n = w0b[:, p + 1, r0:r0 + 8]
                        for h in range(H):
                            g, hh = h // 3, h % 3
                            S.op('dve', lambda e: e.scalar_tensor_tensor(out=Cst[d][:, h, :], in0=Cst[d][:, h, :], scalar=w0c[:, h:h + 1],
                                                                         in1=psC[g][:, hh, :], op0=ALU.mult, op1=ALU.add),
                                 reads=[('Cst', d, h)], excl=[('psC', g)], writes=[('Cst', d, h)])
                        S.op('dve', lambda e: e.tensor_tensor(out=Cb[d][:, 0:4, :], in0=Cst[d][:, 0:4, :], in1=bc(w0n[:, 0:4], 2, HA), op=ALU.mult),
                             reads=[('Cst', d, h) for h in range(4)], writes=[('Cb', d, 0)])
                        for h in range(4, H):
                            S.op('act', lambda e: e.activation(out=Cb[d][:, h, :], in_=Cst[d][:, h, :], func=AF.Copy, scale=w0n[:, h:h + 1]),
                                 reads=[('Cst', d, h)], writes=[('Cb', d, 1)])

                for d in range(2):
                    load_sc(d, 0)
                for d in range(2):
                    stage1(0, d)
                for p in range(NCH):
                    if p % SC == 0 and p // SC + 1 < NSC:
                        for d in range(2):
                            load_sc(d, p // SC + 1)
                    if p + 1 < NCH:
                        for d in range(2):
                            stage1(p + 1, d)
                    for d in range(2):
                        stage2(p, d)
                S.barrier()
            Xs.close()
            Fs = contextlib.ExitStack()
            with Fs:
                def FT(name, shape, dt):
                    return Fs.enter_context(nc.sbuf_tensor(f"{name}{l}", shape, dt))
                wo = FT("wo", [128, 16, D], BF16)
                gtile = [FT("gtx", [128, D], F32), FT("gtc", [128, D], F32)]
                ghb = FT("ghb", [128, WC], F32)
                NB3 = 3
                hfs = [FT(f"hfs{i}_", [128, H, DH], F32) for i in range(NB3)]
                hbs = [FT(f"hbs{i}_", [128, H, DH], F32) for i in range(NB3)]
                ot = [FT(f"ot{i}_", [128, WC], BF16) for i in range(NB3)]
                zt = [FT(f"zt{i}_", [128, WC], BF16) for i in range(NB3)]
                sgo_ = [FT(f"sgo{i}_", [128, WC], F32) for i in range(2)]
                szm_ = [FT(f"szm{i}_", [128, WC], BF16) for i in range(2)]
                hsq = FT("hsq", [128, H, DH], BF16)
                st8_ = [FT(f"st8{i}_", [128, 7, H], F32) for i in range(2)]
                ym_ = [FT(f"ym{i}_", [128, WC], BF16) for i in range(2)]
                yT = [FT(f"yT{i}_", [128, 16, 128], BF16) for i in range(NB3)]
                xr = [FT(f"xr{i}_", [128, D], F32) for i in range(NB3)]
                xo = [FT(f"xo{i}_", [128, D], F32) for i in range(2)]
                st1 = FT("st1", [128, 8], F32)
                sqj = FT("sqj", [128, 512], BF16)
                pso = Fs.enter_context(nc.psum_tensor(f"pso{l}", [128, D], F32))
                psy = [Fs.enter_context(nc.psum_tensor(f"psy{l}_{i}", [128, 4, 128], BF16)) for i in range(2)]
                for half in range(2):
                    S.dma('pool', wo[:, :, half * 1024:(half + 1) * 1024],
                          w_out[l, :, half * 1024:(half + 1) * 1024].rearrange("(k p) n -> p k n", p=128), writes=[('wo', half)])
                gpb = xr[0]
                S.dma('sp', gpb[:], g_post[l:l + 1, :].to_broadcast([128, D]), writes=[('xr', 0)])
                S.dma('sp', ghb[:], g_head[l:l + 1, :].to_broadcast([128, WC]), writes=['ghb'])
                for jx in range(2):
                    S.dma('sp', gtile[jx][:], adaD[jx:jx + 1, :].to_broadcast([128, D]), writes=[('gt', jx)])
                    S.op('dve', lambda e: e.tensor_tensor(out=gtile[jx][:], in0=gtile[jx][:], in1=gpb[:], op=ALU.mult),
                         reads=[('xr', 0), ('gt', jx)], writes=[('gt', jx)])
                tiles = list(range(NT)) if not last else list(range(2, NT))

                def stageA(it):
                    i = tiles[it]
                    s = it % NB3
                    s2 = it % 2
                    sgo, szm, st8, ym = sgo_[s2], szm_[s2], st8_[s2], ym_[s2]
                    k2 = lambda n: (n, s2)
                    rs_ = slice(i * 128, (i + 1) * 128)
                    S.dma('sp', hfs[s][:].rearrange("p h e -> p (h e)"), hfb[0][rs_, :], writes=[('hfs', s)])
                    S.dma('sp', hbs[s][:].rearrange("p h e -> p (h e)"), hfb[1][rs_, :], writes=[('hbs', s)])
                    S.dma('sp', ot[s][:], otm[rs_, :], writes=[('ot', s)])
                    S.dma('sp', zt[s][:], zmtm[rs_, :], writes=[('zt', s)])
                    S.dma('sp', xr[s][:], src[rs_, :], writes=[('xr', s)])
                    S.dma('sp', yT[s][:, 0:8, :], ycT[:, rs_].rearrange("(j p) t -> p j t", p=128), writes=[('yTc', s)])
                    S.op('act', lambda e: e.activation(out=sgo[:], in_=ot[s][:], func=AF.Sigmoid), reads=[('ot', s)], writes=[k2('sgo')])
                    S.op('act', lambda e: e.activation(out=szm[:], in_=zt[s][:], func=AF.Silu), reads=[('zt', s)], writes=[k2('szm')])
                    S.op('pool', lambda e: e.tensor_tensor(out=hfs[s][:], in0=hfs[s][:], in1=hbs[s][:], op=ALU.add),
                         reads=[('hfs', s), ('hbs', s)], writes=[('hfs', s)])
                    S.op('pool', lambda e: e.tensor_tensor(out=sgo[:], in0=sgo[:], in1=ghb[:], op=ALU.mult),
                         reads=[k2('sgo'), 'ghb'], writes=[k2('sgo')])
                    S.op('pool', lambda e: e.tensor_tensor(out=sgo[:], in0=sgo[:], in1=szm[:], op=ALU.mult),
                         reads=[k2('sgo'), k2('szm')], writes=[k2('sgo')])
                    S.op('dve', lambda e: e.tensor_reduce(out=st8[:, 0, :], in_=hfs[s][:], axis=AX.X, op=ALU.add),
                         reads=[('hfs', s)], writes=[k2('st8_0')])
                    S.op('act', lambda e: e.activation(out=hsq[:], in_=hfs[s][:], func=AF.Square), reads=[('hfs', s)], writes=['hsq'])
                    S.op('dve', lambda e: e.tensor_reduce(out=st8[:, 1, :], in_=hsq[:], axis=AX.X, op=ALU.add),
                         reads=['hsq'], writes=[k2('st8_1')])
                    S.op('dve', lambda e: e.tensor_scalar(out=st8[:, 2, :], in0=st8[:, 0, :], scalar1=1.0 / DH, scalar2=None, op0=ALU.mult),
                         reads=[k2('st8_0')], writes=[k2('st8_2')])
                    S.op('dve', lambda e: e.tensor_tensor(out=st8[:, 3, :], in0=st8[:, 2, :], in1=st8[:, 2, :], op=ALU.mult),
                         reads=[k2('st8_2')], writes=[k2('st8_3')])
                    S.op('dve', lambda e: e.scalar_tensor_tensor(out=st8[:, 4, :], in0=st8[:, 1, :], scalar=1.0 / DH, in1=st8[:, 3, :],
                                                                 op0=ALU.mult, op1=ALU.subtract),
                         reads=[k2('st8_1'), k2('st8_3')], writes=[k2('st8_4')])
                    S.op('dve', lambda e: e.tensor_scalar(out=st8[:, 4, :], in0=st8[:, 4, :], scalar1=EPS, scalar2=None, op0=ALU.add),
                         reads=[k2('st8_4')], writes=[k2('st8_4')])
                    S.op('pool', lambda e: e.tensor_tensor(out=st8[:, 5, :], in0=st8[:, 4, :], in1=neghalf[:, 0:H], op=ALU.pow),
                         reads=[k2('st8_4')], writes=[k2('st8_5')])
                    S.op('dve', lambda e: e.tensor_tensor(out=hfs[s][:], in0=hfs[s][:], in1=bc(st8[:, 2, :], 2, DH), op=ALU.subtract),
                         reads=[('hfs', s), k2('st8_2')], writes=[('hfs', s)])
                    S.op('dve', lambda e: e.tensor_tensor(out=hfs[s][:], in0=hfs[s][:], in1=bc(st8[:, 5, :], 2, DH), op=ALU.mult),
                         reads=[('hfs', s), k2('st8_5')], writes=[('hfs', s)])
                    hflat = hfs[s][:].rearrange("p h e -> p (h e)")
                    S.op('dve', lambda e: e.tensor_tensor(out=ym[:], in0=hflat, in1=sgo[:], op=ALU.mult),
                         reads=[('hfs', s), k2('sgo')], writes=[k2('ym')])
                    for half in range(2):
                        for jj in range(4):
                            j = half * 4 + jj
                            S.op('pe', lambda e: e.transpose(out=psy[half][:, jj, :], in_=ym[:, j * 128:(j + 1) * 128], identity=ident_b[:]),
                                 reads=[k2('ym')], excl=[('psy', half)])
                        if half == 0:
                            S.op('act', lambda e: e.activation(out=yT[s][:, 8:12, :], in_=psy[0][:], func=AF.Copy),
                                 excl=[('psy', 0)], writes=[('yTm', s, 0)])
                        else:
                            S.op('dve', lambda e: e.tensor_copy(out=yT[s][:, 12:16, :], in_=psy[1][:]),
                                 excl=[('psy', 1)], writes=[('yTm', s, 1)])

                def stageB_pe(it):
                    s = it % NB3
                    for nn in range(4):
                        for k in range(16):
                            rk = [('yTc', s)] if k < 8 else [('yTm', s, (k - 8) // 4)]
                            S.op('pe', lambda e: e.matmul(pso[:, nn * 512:(nn + 1) * 512], lhsT=yT[s][:, k, :],
                                                          rhs=wo[:, k, nn * 512:(nn + 1) * 512], start=(k == 0), stop=(k == 15)),
                                 reads=rk + [('wo', nn // 2)], excl=[('pso', nn)])

                def stageB_ep(it):
                    i = tiles[it]
                    s = it % NB3
                    sx = it % 2
                    jx = 1 if i < 2 else 0
                    rs_ = slice(i * 128, (i + 1) * 128)
                    for nn in range(4):
                        cs_ = slice(nn * 512, (nn + 1) * 512)
                        S.op('act', lambda e: e.activation(out=sqj[:], in_=pso[:, cs_], func=AF.Square, accum_out=st1[:, nn:nn + 1]),
                             excl=[('pso', nn)], writes=['sqj', ('st1', nn)])
                        S.op('dve', lambda e: e.tensor_tensor(out=xo[sx][:, cs_], in0=pso[:, cs_], in1=gtile[jx][:, cs_], op=ALU.mult),
                             reads=[('gt', jx)], excl=[('pso', nn)], writes=[('xo', sx, nn)])
                    S.op('dve', lambda e: e.tensor_reduce(out=st1[:, 4:5], in_=st1[:, 0:4], axis=AX.X, op=ALU.add),
                         reads=[('st1', nn) for nn in range(4)], writes=['st1_s'])
                    S.op('dve', lambda e: e.tensor_scalar(out=st1[:, 5:6], in0=st1[:, 4:5], scalar1=1.0 / D, scalar2=EPS,
                                                          op0=ALU.mult, op1=ALU.add),
                         reads=['st1_s'], writes=['st1_1'])
                    S.op('pool', lambda e: e.tensor_tensor(out=st1[:, 6:7], in0=st1[:, 5:6], in1=neghalf[:, 0:1], op=ALU.pow),
                         reads=['st1_1'], writes=['st1_2'])
                    S.op('dve', lambda e: e.scalar_tensor_tensor(out=xo[sx][:], in0=xo[sx][:], scalar=st1[:, 6:7], in1=xr[s][:],
                                                                 op0=ALU.mult, op1=ALU.add),
                         reads=[('xo', sx, nn) for nn in range(4)] + ['st1_2', ('xr', s)], writes=[('xo', sx, nn) for nn in range(4)])
                    S.dma('pool', xout[rs_, :], xo[sx][:], reads=[('xo', sx, nn) for nn in range(4)], writes=[('xout', i)])

                stageA(0)
                if len(tiles) > 1:
                    stageA(1)
                for it in range(len(tiles)):
                    stageB_pe(it)
                    if it + 2 < len(tiles):
                        stageA(it + 2)
                    stageB_ep(it)
                S.barrier()
        S.barrier()
    return nc, S


_CACHE = {}


def kernel(x, c, ctx, c_ctx, w_ada, b_ada, g_pre, g_post, w_in, b_gate, w_dw, b_dw, ln_g, ln_b, w_pw2,
           g_head, w_out):
    if 'nc' not in _CACHE:
        _CACHE['nc'] = build()[0]
    nc = _CACHE['nc']
    f = lambda a: np.ascontiguousarray(np.asarray(a, dtype=np.float32))
    shared = {"w_ada": f(w_ada), "b_ada": f(b_ada), "g_pre": f(g_pre), "g_post": f(g_post), "w_in": f(w_in),
              "b_gate": f(b_gate), "w_dw": f(w_dw), "b_dw": f(b_dw), "ln_g": f(ln_g), "ln_b": f(ln_b),
              "w_pw2": f(w_pw2), "g_head": f(g_head), "w_out": f(w_out)}
    x = f(x); ctx = f(ctx); c = f(c); c_ctx = f(c_ctx)
    in_maps = []
    for core in range(8):
        b = core % 4
        m = dict(shared)
        m["xin"] = np.ascontiguousarray(np.concatenate([ctx[b], x[b]], axis=0))
        m["cc"] = np.ascontiguousarray(np.stack([c[b], c_ctx], axis=0))
        in_maps.append(m)
    res = run_bass_kernel_spmd(nc, in_maps, core_ids=list(range(8)))
    out = np.stack([np.asarray(res.results[b]["xout"])[NCTX:] for b in range(4)], axis=0)
    return out.astype(np.float32)
```

```python
import contextlib
import numpy as np
import concourse.bass as bass
import concourse.mybir as mybir
from concourse.bass_utils import run_bass_kernel_spmd

F32 = mybir.dt.float32
BF16 = mybir.dt.bfloat16
AF = mybir.ActivationFunctionType
ALU = mybir.AluOpType
AX = mybir.AxisListType

D = 2048
NCTX = 256
NLAT = 2048
T = NCTX + NLAT
NT = T // 128
DEPTH = 4
WC = 1024
H = 8
DH = 128
CH = 128
NCC = NCTX // CH
NCH = T // CH
NIN = 8224
OA, OG, OZ, OQ, OK_, OV, OO, OZM, OGT = 0, 1024, 2048, 3072, 4096, 5120, 6144, 7168, 8192
EPS = 1e-6
NEG = -1e30
TG = [(0, 256)] + [(256 + 512 * i, 512) for i in range(4)]


class Sched:
    EPOCH = 30000
    NDMA = 12

    def __init__(self, nc):
        self.nc = nc
        self.engs = {'pe': nc.tensor, 'act': nc.scalar, 'dve': nc.vector,
                     'pool': nc.gpsimd, 'sp': nc.sync}
        self.cnt = {e: 0 for e in self.engs}
        self.sems = {e: [] for e in self.engs}
        self.seen = {e: {} for e in self.engs}
        self.res = {}
        self.dq = {}
        self.nsem = 0
        self.nwait = 0
        self.nins = 0

    def _newsem(self, name):
        self.nsem += 1
        return self.nc.alloc_semaphore(name=name)

    def _esem(self, e, n):
        ep = (n - 1) // self.EPOCH
        while len(self.sems[e]) <= ep:
            self.sems[e].append(self._newsem(f"s_{e}_{len(self.sems[e])}"))
        return self.sems[e][ep], (n - 1) % self.EPOCH + 1

    def _wait(self, e, dep):
        if dep[0] == 'e':
            sem, val = self._esem(dep[1], dep[2])
        else:
            sem, val = dep[1], dep[2]
        k = id(sem)
        if self.seen[e].get(k, 0) >= val:
            return
        self.seen[e][k] = val
        self.engs[e].wait_ge(sem, val)
        self.nwait += 1

    def _deps(self, e, reads, writes, excl):
        deps = []
        for r in reads:
            st = self.res.get(r)
            if st and st['w'] is not None:
                deps.append(('raw', st['w']))
        for w in writes:
            st = self.res.get(w)
            if st:
                if st['w'] is not None:
                    deps.append(('waw', st['w']))
                for d in st['r'].values():
                    deps.append(('war', d))
        for w in excl:
            st = self.res.get(w)
            if st:
                if st['w'] is not None:
                    deps.append(('x', st['w']))
                for d in st['r'].values():
                    deps.append(('x', d))
        for kind, d in deps:
            if d[0] == 'e' and d[1] == e:
                if e == 'pe' or kind in ('war', 'x'):
                    continue
            self._wait(e, d)

    def _commit(self, me, reads, writes, excl, ekey):
        for w in writes:
            self.res[w] = {'w': me, 'r': {}}
        for r in reads:
            st = self.res.setdefault(r, {'w': None, 'r': {}})
            st['r'][ekey] = me
        for w in excl:
            self.res[w] = {'w': me, 'r': {}}

    def op(self, e, fn, reads=(), writes=(), excl=()):
        self._deps(e, reads, writes, excl)
        ins = fn(self.engs[e])
        self.cnt[e] += 1
        n = self.cnt[e]
        sem, val = self._esem(e, n)
        ins.then_inc(sem, 1)
        self.nins += 1
        me = ('e', e, n)
        self._commit(me, reads, writes, excl, e)
        return me

    def dma(self, q, out, in_, reads=(), writes=(), **kw):
        self._deps(q, reads, writes, ())
        st = self.dq.setdefault(q, {'sems': [], 'vals': [], 'i': 0})
        i = st['i'] % self.NDMA
        if len(st['sems']) <= i:
            st['sems'].append(self._newsem(f"d_{q}_{i}"))
            st['vals'].append(0)
        sem = st['sems'][i]
        if st['vals'][i] > 0:
            self._wait(q, ('d', sem, st['vals'][i]))
        st['vals'][i] += 16
        st['i'] += 1
        self.engs[q].dma_start(out=out, in_=in_, **kw).then_inc(sem, 16)
        self.nins += 1
        me = ('d', sem, st['vals'][i])
        self._commit(me, reads, writes, (), ('dma', q, i))
        return me

    def barrier(self):
        for e in self.engs:
            for f in self.engs:
                if f != e and self.cnt[f] > 0:
                    self._wait(e, ('e', f, self.cnt[f]))
            for q, st in self.dq.items():
                for sem, val in zip(st['sems'], st['vals']):
                    if val > 0:
                        self._wait(e, ('d', sem, val))
        self.res.clear()


def bc(ap, axis, n):
    a = ap.unsqueeze(axis)
    shp = list(a.shape)
    shp[axis] = n
    return a.to_broadcast(shp)


def build(nlayers=DEPTH, debug=False):
    nc = bass.Bass("TRN2", target_bir_lowering=False)
    S = Sched(nc)

    def din(name, shape):
        return nc.dram_tensor(name, shape, F32, kind="ExternalInput").ap()

    xin = din("xin", [T, D])
    cc = din("cc", [2, D])
    w_ada = din("w_ada", [DEPTH, D, 3 * D])
    b_ada = din("b_ada", [DEPTH, 3 * D])
    g_pre = din("g_pre", [DEPTH, D])
    g_post = din("g_post", [DEPTH, D])
    w_in = din("w_in", [DEPTH, D, NIN])
    b_gate = din("b_gate", [DEPTH, 32])
    w_dw = din("w_dw", [DEPTH, 31, WC])
    b_dw = din("b_dw", [DEPTH, WC])
    ln_g = din("ln_g", [DEPTH, WC])
    ln_b = din("ln_b", [DEPTH, WC])
    w_pw2 = din("w_pw2", [DEPTH, WC, WC])
    g_head = din("g_head", [DEPTH, WC])
    w_out = din("w_out", [DEPTH, D, D])
    xout = nc.dram_tensor("xout", [T, D], F32, kind="ExternalOutput").ap()

    skind = "ExternalOutput" if debug else "Internal"

    def dscr(name, shape, dt):
        return nc.dram_tensor(name, shape, dt, kind=skind).ap()

    ycT = dscr("ycT", [WC, T], BF16)
    qT_d = dscr("qT_d", [H, DH, T], BF16)
    kT_d = dscr("kT_d", [H, DH, T], BF16)
    ktm = dscr("ktm", [T, WC], BF16)
    vtm = dscr("vtm", [T, WC], BF16)
    otm = dscr("otm", [T, WC], BF16)
    zmtm = dscr("zmtm", [T, WC], BF16)
    hfb = [dscr("hf_d", [T, WC], F32), dscr("hb_d", [T, WC], F32)]
    adaD = dscr("adaD", [2, D], F32)
    dbg = {}
    if debug:
        dbg['hT'] = dscr("dbg_hT", [128, 16, T], BF16)
        dbg['convT'] = dscr("dbg_convT", [128, 8, T], BF16)
        dbg['gates'] = dscr("dbg_gates", [4, 40, T], F32)
        dbg['etm'] = dscr("dbg_etm", [2, CH, NCH * 16], F32)
        dbg['w0b'] = dscr("dbg_w0b", [128, NCH * 16], F32)

    gs = contextlib.ExitStack()
    with gs:
        def GT(name, shape, dt):
            return gs.enter_context(nc.sbuf_tensor(name, shape, dt))

        ident_f = GT("ident_f", [128, 128], F32)
        ident_b = GT("ident_b", [128, 128], BF16)
        ones_b = GT("ones_b", [128, 128], BF16)
        ones_f = GT("ones_f", [128, 128], F32)
        mask2 = GT("mask2", [CH, 2, CH], F32)
        neghalf = GT("neghalf", [128, 512], F32)
        g_preT = GT("g_preT", [128, DEPTH * 16], F32)
        ln_gT = GT("ln_gT", [128, DEPTH * 8], F32)
        ln_bT = GT("ln_bT", [128, DEPTH * 8], F32)
        b_dwT = GT("b_dwT", [128, DEPTH * 8], F32)
        b_adaT = GT("b_adaT", [128, DEPTH * 48], F32)
        w_dwT = GT("w_dwT", [128, DEPTH * 8, 31], F32)
        bgI = GT("bgI", [40, DEPTH], F32)
        bgF = GT("bgF", [40, DEPTH], F32)
        nbgF = GT("nbgF", [40, DEPTH], F32)
        cT = GT("cT", [128, 16, 2], BF16)
        adaT = GT("adaT", [128, 48, 2], F32)
        s1T = GT("s1T", [128, 16, 2], F32)
        shT = GT("shT", [128, 16, 2], F32)
        ones_col = GT("ones_col", [64, 1], BF16)
        sel16 = GT("sel16", [40, 16], F32)

        ss = contextlib.ExitStack()
        with ss:
            stg = [ss.enter_context(nc.sbuf_tensor(f"stg{i}", [128, 128], F32)) for i in range(2)]
            wst = ss.enter_context(nc.sbuf_tensor("wst", [31, DEPTH, WC], F32))
            c32 = ss.enter_context(nc.sbuf_tensor("c32", [32, 128], F32))
            c32s = ss.enter_context(nc.sbuf_tensor("c32s", [32, 128], F32))
            pst = [ss.enter_context(nc.psum_tensor(f"pst{i}", [128, 512], F32)) for i in range(2)]

            S.op('pool', lambda e: e.memset(ident_f[:], 0.0), writes=['ident_f'])
            S.op('pool', lambda e: e.affine_select(out=ident_f[:], in_=ident_f[:], pattern=[[-1, 128]],
                                                   compare_op=ALU.not_equal, fill=1.0, base=0, channel_multiplier=1),
                 reads=['ident_f'], writes=['ident_f'])
            S.op('dve', lambda e: e.tensor_copy(out=ident_b[:], in_=ident_f[:]), reads=['ident_f'], writes=['ident_b'])
            S.op('dve', lambda e: e.tensor_copy(out=sel16[:, 0:8], in_=ident_f[0:40, 0:8]), reads=['ident_f'], writes=['sel16a'])
            S.op('dve', lambda e: e.tensor_copy(out=sel16[:, 8:16], in_=ident_f[0:40, 32:40]), reads=['ident_f'], writes=['sel16b'])
            S.op('pool', lambda e: e.memset(ones_b[:], 1.0), writes=['ones_b'])
            S.op('pool', lambda e: e.memset(ones_f[:], 1.0), writes=['ones_f'])
            S.op('pool', lambda e: e.memset(ones_col[:], 1.0), writes=['ones_col'])
            S.op('pool', lambda e: e.memset(neghalf[:], -0.5), writes=['neghalf'])
            S.op('pool', lambda e: e.memset(mask2[:], 1.0), writes=['mask2'])
            S.op('pool', lambda e: e.affine_select(out=mask2[:, 0, :], in_=mask2[:, 0, :], pattern=[[1, CH]],
                                                   compare_op=ALU.is_ge, fill=0.0, base=0, channel_multiplier=-1),
                 reads=['mask2'], writes=['mask2'])
            S.op('pool', lambda e: e.affine_select(out=mask2[:, 1, :], in_=mask2[:, 1, :], pattern=[[-1, CH]],
                                                   compare_op=ALU.is_ge, fill=0.0, base=0, channel_multiplier=1),
                 reads=['mask2'], writes=['mask2'])
            for t_ in (bgI, bgF):
                S.op('pool', lambda e: e.memset(t_[:], 0.0), writes=[t_.name if hasattr(t_, 'name') else id(t_)])
            S.barrier()
            for (dst, col0, r0) in ((bgI, 0, 0), (bgF, 8, 0), (bgI, 16, 32), (bgF, 24, 32)):
                S.dma('sp', dst[r0:r0 + 8, :], b_gate[:, col0:col0 + 8].rearrange("l h -> h l"),
                      writes=[('bg', col0)], allow_slow_non_contiguous=True)
            S.barrier()
            S.op('dve', lambda e: e.tensor_scalar(out=nbgF[:], in0=bgF[:], scalar1=-1.0, scalar2=None, op0=ALU.mult),
                 writes=['nbgF'])

            tcount = [0]

            def load_T(dst, src_rows, R):
                i = tcount[0] % 2
                tcount[0] += 1
                S.dma('sp', stg[i][0:R, :], src_rows, writes=[('stg', i)])
                S.op('pe', lambda e: e.transpose(out=pst[i][:, 0:R], in_=stg[i][0:R, :], identity=ident_f[0:R, 0:R]),
                     reads=[('stg', i)], excl=[('pst', i)])
                S.op('dve', lambda e: e.tensor_copy(out=dst, in_=pst[i][:, 0:R]), excl=[('pst', i)], writes=[('ld', tcount[0])])

            load_T(g_preT[:, :], g_pre.rearrange("l (k p) -> (l k) p", p=128), 64)
            load_T(ln_gT[:, :], ln_g.rearrange("l (k p) -> (l k) p", p=128), 32)
            load_T(ln_bT[:, :], ln_b.rearrange("l (k p) -> (l k) p", p=128), 32)
            load_T(b_dwT[:, :], b_dw.rearrange("l (k p) -> (l k) p", p=128), 32)
            bav = b_ada.rearrange("l (k p) -> (l k) p", p=128)
            load_T(b_adaT[:, 0:128], bav[0:128, :], 128)
            load_T(b_adaT[:, 128:192], bav[128:192, :], 64)
            S.dma('sp', wst[:], w_dw.rearrange("l k c -> k l c"), writes=['wst'])
            for l in range(DEPTH):
                for j in range(8):
                    i = tcount[0] % 2
                    tcount[0] += 1
                    S.op('pe', lambda e: e.transpose(out=pst[i][:, 0:31], in_=wst[0:31, l, j * 128:(j + 1) * 128],
                                                     identity=ident_f[0:31, 0:31]),
                         reads=['wst'], excl=[('pst', i)])
                    S.op('dve', lambda e: e.tensor_copy(out=w_dwT[:, l * 8 + j, :], in_=pst[i][:, 0:31]),
                         excl=[('pst', i)], writes=[('wdw', l, j)])
            S.dma('sp', c32[:], cc.rearrange("j (k p) -> (j k) p", p=128), writes=['c32'])
            S.op('act', lambda e: e.activation(out=c32s[:], in_=c32[:], func=AF.Silu), reads=['c32'], writes=['c32s'])
            S.op('pe', lambda e: e.transpose(out=pst[0][:, 0:32], in_=c32s[:], identity=ident_f[0:32, 0:32]),
                 reads=['c32s'], excl=[('pst', 0)])
            S.op('dve', lambda e: e.tensor_copy(out=cT[:].rearrange("p k j -> p j k"),
                                                in_=pst[0][:, 0:32].rearrange("p (j k) -> p j k", j=2)),
                 excl=[('pst', 0)], writes=['cT'])
            S.barrier()

        for l in range(nlayers):
            src = xin if l == 0 else xout
            last = (l == DEPTH - 1)
            Xs = contextlib.ExitStack()
            etm = [Xs.enter_context(nc.sbuf_tensor(f"etm{q}_{l}", [CH, NCH, 16], F32)) for q in range(2)]
            w0b = Xs.enter_context(nc.sbuf_tensor(f"w0b{l}", [128, NCH, 16], F32))
            Gs = contextlib.ExitStack()
            GI = Gs.enter_context(nc.sbuf_tensor(f"GI{l}", [40, T], F32))
            GF = Gs.enter_context(nc.sbuf_tensor(f"GF{l}", [40, T], F32))
            Ls = contextlib.ExitStack()
            with Ls:
                hT = Ls.enter_context(nc.sbuf_tensor(f"hT{l}", [128, 16, T], BF16))

                psA = Ls.enter_context(nc.psum_tensor(f"psA{l}", [128, 48, 2], F32))
                wbA = [Ls.enter_context(nc.sbuf_tensor(f"wbA{l}_{i}", [128, 16, 128], BF16)) for i in range(2)]
                actr = [0]

                def ada_chunk(la, blk):
                    s = actr[0] % 2
                    actr[0] += 1
                    S.dma('pool', wbA[s][:], w_ada[la, :, blk * 128:(blk + 1) * 128].rearrange("(k p) n -> p k n", p=128),
                          writes=[('wbA', s)])
                    for k in range(16):
                        S.op('pe', lambda e: e.matmul(psA[:, blk, :], lhsT=wbA[s][:, k, :], rhs=cT[:, k, :],
                                                      start=(k == 0), stop=(k == 15)),
                             reads=[('wbA', s)], excl=['psA'])

                def ada_finish(la):
                    S.op('dve', lambda e: e.tensor_tensor(out=adaT[:], in0=psA[:],
                                                          in1=bc(b_adaT[:, la * 48:(la + 1) * 48], 2, 2), op=ALU.add),
                         excl=['psA'], writes=['adaT'])

                pending = list(range(48)) if l + 1 < nlayers else []

                def ada_bg(n):
                    for _ in range(n):
                        if pending:
                            ada_chunk(l + 1, pending.pop(0))

                As = contextlib.ExitStack()
                with As:
                    if l == 0:
                        for blk in range(48):
                            ada_chunk(0, blk)
                        ada_finish(0)
                    S.op('dve', lambda e: e.scalar_tensor_tensor(out=s1T[:], in0=adaT[:, 16:32, :], scalar=1.0,
                                                                 in1=bc(g_preT[:, l * 16:(l + 1) * 16], 2, 2),
                                                                 op0=ALU.add, op1=ALU.mult),
                         reads=['adaT'], writes=['s1T'])
                    S.op('dve', lambda e: e.tensor_copy(out=shT[:], in_=adaT[:, 0:16, :]), reads=['adaT'], writes=['shT'])
                    for jx in range(2):
                        S.dma('sp', adaD[jx, :].rearrange("(c p) -> p c", p=128), adaT[:, 32:48, jx], reads=['adaT'],
                              writes=[('adaD', jx)], allow_slow_non_contiguous=True)
                    S.barrier()

                Bs = contextlib.ExitStack()
                with Bs:
                    xt = [Bs.enter_context(nc.sbuf_tensor(f"xt{l}_{i}", [128, D], F32)) for i in range(2)]
                    xn = [Bs.enter_context(nc.sbuf_tensor(f"xn{l}_{i}", [128, D], BF16)) for i in range(2)]
                    tmpf = Bs.enter_context(nc.sbuf_tensor(f"tmpf{l}", [128, 8, 128], F32))
                    stt = Bs.enter_context(nc.sbuf_tensor(f"stt{l}", [128, 3 * NT], F32))
                    psT = [Bs.enter_context(nc.psum_tensor(f"psT{l}_{i}", [128, 8, 128], BF16)) for i in range(2)]
                    for i in range(NT):
                        s = i % 2
                        jx = 1 if i < 2 else 0
                        S.dma('sp', xt[s][:], src[i * 128:(i + 1) * 128, :], writes=[('xt', s)])
                        S.op('act', lambda e: e.activation(out=xn[s][:], in_=xt[s][:], func=AF.Square,
                                                           accum_out=stt[:, i:i + 1]),
                             reads=[('xt', s)], writes=[('xn', s), ('ss', i)])
                        S.op('dve', lambda e: e.tensor_scalar(out=stt[:, NT + i:NT + i + 1], in0=stt[:, i:i + 1],
                                                              scalar1=1.0 / D, scalar2=EPS, op0=ALU.mult, op1=ALU.add),
                             reads=[('ss', i)], writes=[('ms', i)])
                        S.op('pool', lambda e: e.tensor_tensor(out=stt[:, 2 * NT + i:2 * NT + i + 1],
                                                               in0=stt[:, NT + i:NT + i + 1], in1=neghalf[:, 0:1], op=ALU.pow),
                             reads=[('ms', i)], writes=[('rs', i)])
                        S.op('act', lambda e: e.activation(out=xn[s][:], in_=xt[s][:], func=AF.Copy,
                                                           scale=stt[:, 2 * NT + i:2 * NT + i + 1]),
                             reads=[('xt', s), ('rs', i)], writes=[('xn', s)])
                        for hh in range(2):
                            for k8 in range(8):
                                k = hh * 8 + k8
                                S.op('pe', lambda e: e.transpose(out=psT[hh][:, k8, :], in_=xn[s][:, k * 128:(k + 1) * 128],
                                                                 identity=ident_b[:]),
                                     reads=[('xn', s)], excl=[('psT', hh)])
                            S.op('dve', lambda e: e.tensor_tensor(out=tmpf[:], in0=psT[hh][:],
                                                                  in1=bc(s1T[:, hh * 8:(hh + 1) * 8, jx], 2, 128), op=ALU.mult),
                                 reads=['s1T'], excl=[('psT', hh)], writes=['tmpf'])
                            S.op('dve', lambda e: e.tensor_tensor(out=hT[:, hh * 8:(hh + 1) * 8, i * 128:(i + 1) * 128],
                                                                  in0=tmpf[:],
                                                                  in1=bc(shT[:, hh * 8:(hh + 1) * 8, jx], 2, 128), op=ALU.add),
                                 reads=['tmpf', 'shT'], writes=[('hT', i)])
                    if debug and l == 0:
                        S.dma('sp', dbg['hT'], hT[:], reads=[('hT', i) for i in range(NT)])
                    S.barrier()
                hT_all = [('hT', i) for i in range(NT)]

                Cs = contextlib.ExitStack()
                with Cs:
                    wb = [Cs.enter_context(nc.sbuf_tensor(f"wbC{l}_{i}", [128, 16, 512], BF16)) for i in range(2)]
                    convT = Cs.enter_context(nc.sbuf_tensor(f"convT{l}", [128, 8, T], BF16))
                    upl = Cs.enter_context(nc.sbuf_tensor(f"upl{l}", [128, 64 * 64], BF16))
                    upc = Cs.enter_context(nc.sbuf_tensor(f"upc{l}", [128, NCTX + 30], BF16))
                    dg = Cs.enter_context(nc.sbuf_tensor(f"dg{l}", [128, 31, 128], BF16))
                    sig = [Cs.enter_context(nc.sbuf_tensor(f"sig{l}_{i}", [128, 512], F32)) for i in range(2)]
                    sq = upl[:, :].rearrange("p (j t) -> p j t", t=512)
                    mean = Cs.enter_context(nc.sbuf_tensor(f"mean{l}", [128, 512], F32))
                    rstd = Cs.enter_context(nc.sbuf_tensor(f"rstd{l}", [128, 512], F32))
                    t1 = sig
                    yco = [Cs.enter_context(nc.sbuf_tensor(f"yco{l}_{i}", [128, 512], BF16)) for i in range(2)]
                    psa = [Cs.enter_context(nc.psum_tensor(f"psa{l}_{i}", [128, 512], F32)) for i in range(2)]
                    psg = [Cs.enter_context(nc.psum_tensor(f"psg{l}_{i}", [128, 512], F32)) for i in range(2)]
                    psc = [Cs.enter_context(nc.psum_tensor(f"psc{l}_{i}", [128, 512], F32)) for i in range(2)]
                    S.op('pool', lambda e: e.memset(upc[:], 0.0), writes=['upc'])
                    uplh = upl[:, 0:32 * 94].rearrange("p (r c) -> p r c", c=94)
                    uplv = upl[:, 0:62 * 64].rearrange("p (r c) -> p r c", c=64)
                    cnt = 0
                    for jp in range(4):
                        s = jp % 2
                        S.dma('pool', wb[s][:, :, 0:256],
                              w_in[l, :, OA + jp * 256:OA + (jp + 1) * 256].rearrange("(k p) n -> p k n", p=128),
                              writes=[('wb', s)])
                        S.dma('pool', wb[s][:, :, 256:512],
                              w_in[l, :, OG + jp * 256:OG + (jp + 1) * 256].rearrange("(k p) n -> p k n", p=128),
                              reads=[('wb', s)], writes=[('wb', s)])
                        for jj in range(2):
                            j = 2 * jp + jj
                            horiz = j < 4
                            if j == 0 or j == 4:
                                S.op('pool', lambda e: e.memset(upl[:], 0.0), writes=['upl'])
                            S.op('dve', lambda e: e.tensor_tensor(out=dg[:], in0=bc(ident_b[:], 1, 31),
                                                                  in1=bc(w_dwT[:, l * 8 + j, :], 2, 128), op=ALU.mult),
                                 writes=['dg'])
                            for n, (t0, tn) in enumerate(TG):
                                b = cnt % 2
                                cnt += 1
                                for k in range(16):
                                    S.op('pe', lambda e: e.matmul(psa[b][:, 0:tn], lhsT=wb[s][:, k, jj * 128:(jj + 1) * 128],
                                                                  rhs=hT[:, k, t0:t0 + tn], start=(k == 0), stop=(k == 15)),
                                         reads=[('wb', s)], excl=[('psa', b)])
                                for k in range(16):
                                    S.op('pe', lambda e: e.matmul(psg[b][:, 0:tn],
                                                                  lhsT=wb[s][:, k, 256 + jj * 128:256 + (jj + 1) * 128],
                                                                  rhs=hT[:, k, t0:t0 + tn], start=(k == 0), stop=(k == 15)),
                                         reads=[('wb', s)], excl=[('psg', b)])
                                S.op('act', lambda e: e.activation(out=sig[b][:, 0:tn], in_=psg[b][:, 0:tn], func=AF.Sigmoid),
                                     excl=[('psg', b)], writes=[('sig', b)])
                                if n == 0:
                                    uo = upc[:, 15:15 + NCTX]
                                    ui = psa[b][:, 0:tn]
                                    si = sig[b][:, 0:tn]
                                    ukey = 'upc'
                                elif horiz:
                                    r0 = 8 * (n - 1)
                                    uo = uplh[:, r0:r0 + 8, 15:79]
                                    ui = psa[b][:, :].rearrange("p (r c) -> p r c", c=64)
                                    si = sig[b][:, :].rearrange("p (r c) -> p r c", c=64)
                                    ukey = 'upl'
                                else:
                                    r0 = 15 + 8 * (n - 1)
                                    uo = uplv[:, r0:r0 + 8, :]
                                    ui = psa[b][:, :].rearrange("p (r c) -> p r c", c=64)
                                    si = sig[b][:, :].rearrange("p (r c) -> p r c", c=64)
                                    ukey = 'upl'
                                S.op('dve', lambda e: e.tensor_tensor(out=uo, in0=ui, in1=si, op=ALU.mult),
                                     reads=[('sig', b), ukey], excl=[('psa', b)], writes=[ukey])
                            for n, (t0, tn) in enumerate(TG):
                                b = cnt % 2
                                cnt += 1
                                for k in range(31):
                                    if n == 0:
                                        win = upc[:, k:k + NCTX]
                                        po = psc[b][:, 0:tn]
                                        ukey = 'upc'
                                    elif horiz:
                                        r0 = 8 * (n - 1)
                                        win = uplh[:, r0:r0 + 8, k:k + 64]
                                        po = psc[b][:, :].rearrange("p (r c) -> p r c", c=64)
                                        ukey = 'upl'
                                    else:
                                        r0 = 8 * (n - 1) + k
                                        win = uplv[:, r0:r0 + 8, :]
                                        po = psc[b][:, :].rearrange("p (r c) -> p r c", c=64)
                                        ukey = 'upl'
                                    S.op('pe', lambda e: e.matmul(po, lhsT=dg[:, k, :], rhs=win, start=(k == 0), stop=(k == 30)),
                                         reads=['dg', ukey], excl=[('psc', b)])
                                S.op('act', lambda e: e.activation(out=convT[:, j, t0:t0 + tn], in_=psc[b][:, 0:tn],
                                                                   func=AF.Identity, bias=b_dwT[:, l * 8 + j:l * 8 + j + 1]),
                                     excl=[('psc', b)], writes=[('cv', j, n)])
                                ada_bg(1)
                    if debug and l == 0:
                        S.dma('sp', dbg['convT'], convT[:], reads=[('cv', j, n) for j in range(8) for n in range(5)])
                    for n, (t0, tn) in enumerate(TG):
                        S.op('act', lambda e: e.activation(out=sq[:, :, 0:tn], in_=convT[:, :, t0:t0 + tn], func=AF.Square),
                             reads=[('cv', j, n) for j in range(8)] + ['upl'], writes=['sq', 'upl'])
                        for j in range(8):
                            S.op('pe', lambda e: e.matmul(psa[0][:, 0:tn], lhsT=ones_b[:], rhs=convT[:, j, t0:t0 + tn],
                                                          start=(j == 0), stop=(j == 7)),
                                 reads=[('cv', j, n)], excl=[('psa', 0)])
                        for j in range(8):
                            S.op('pe', lambda e: e.matmul(psg[0][:, 0:tn], lhsT=ones_b[:], rhs=sq[:, j, 0:tn],
                                                          start=(j == 0), stop=(j == 7)),
                                 reads=['sq'], excl=[('psg', 0)])
                        S.op('act', lambda e: e.activation(out=mean[:, 0:tn], in_=psa[0][:, 0:tn], func=AF.Copy, scale=1.0 / WC),
                             excl=[('psa', 0)], writes=['mean'])
                        S.op('dve', lambda e: e.tensor_tensor(out=t1[0][:, 0:tn], in0=mean[:, 0:tn], in1=mean[:, 0:tn], op=ALU.mult),
                             reads=['mean'], writes=[('sig', 0)])
                        S.op('dve', lambda e: e.scalar_tensor_tensor(out=t1[1][:, 0:tn], in0=psg[0][:, 0:tn], scalar=1.0 / WC,
                                                                     in1=t1[0][:, 0:tn], op0=ALU.mult, op1=ALU.subtract),
                             reads=[('sig', 0)], excl=[('psg', 0)], writes=[('sig', 1)])
                        S.op('dve', lambda e: e.tensor_scalar(out=t1[1][:, 0:tn], in0=t1[1][:, 0:tn], scalar1=EPS, scalar2=None,
                                                              op0=ALU.add),
                             reads=[('sig', 1)], writes=[('sig', 1)])
                        S.op('act', lambda e: e.activation(out=t1[1][:, 0:tn], in_=t1[1][:, 0:tn], func=AF.Sqrt),
                             reads=[('sig', 1)], writes=[('sig', 1)])
                        S.op('dve', lambda e: e.reciprocal(out=rstd[:, 0:tn], in_=t1[1][:, 0:tn]),
                             reads=[('sig', 1)], writes=['rstd'])
                        for j in range(8):
                            b = j % 2
                            S.op('dve', lambda e: e.tensor_tensor(out=t1[b][:, 0:tn], in0=convT[:, j, t0:t0 + tn],
                                                                  in1=mean[:, 0:tn], op=ALU.subtract),
                                 reads=[('cv', j, n), 'mean'], writes=[('sig', b)])
                            S.op('dve', lambda e: e.tensor_tensor(out=t1[b][:, 0:tn], in0=t1[b][:, 0:tn], in1=rstd[:, 0:tn], op=ALU.mult),
                                 reads=[('sig', b), 'rstd'], writes=[('sig', b)])
                            S.op('act', lambda e: e.activation(out=convT[:, j, t0:t0 + tn], in_=t1[b][:, 0:tn], func=AF.Silu,
                                                               scale=ln_gT[:, l * 8 + j:l * 8 + j + 1],
                                                               bias=ln_bT[:, l * 8 + j:l * 8 + j + 1]),
                                 reads=[('sig', b)], writes=[('cv', j, n)])
                    wp = wb[0][:].rearrange("p k n -> p (k n)").rearrange("p (j n) -> p j n", n=WC)
                    S.dma('pool', wp, w_pw2[l].rearrange("(j p) n -> p j n", p=128),
                          reads=[('wb', 0)], writes=[('wb', 0)])
                    for zh in range(2):
                        S.dma('pool', wb[1][:], w_in[l, :, OZ + zh * 512:OZ + (zh + 1) * 512].rearrange("(k p) n -> p k n", p=128),
                              reads=[('wb', 1)], writes=[('wb', 1)])
                        for mm in range(4):
                            m = zh * 4 + mm
                            for n, (t0, tn) in enumerate(TG):
                                b = cnt % 2
                                cnt += 1
                                for j in range(8):
                                    S.op('pe', lambda e: e.matmul(psa[b][:, 0:tn], lhsT=wp[:, j, m * 128:(m + 1) * 128],
                                                                  rhs=convT[:, j, t0:t0 + tn], start=(j == 0), stop=(j == 7)),
                                         reads=[('wb', 0), ('cv', j, n)], excl=[('psa', b)])
                                for k in range(16):
                                    S.op('pe', lambda e: e.matmul(psg[b][:, 0:tn], lhsT=wb[1][:, k, mm * 128:(mm + 1) * 128],
                                                                  rhs=hT[:, k, t0:t0 + tn], start=(k == 0), stop=(k == 15)),
                                         reads=[('wb', 1)], excl=[('psg', b)])
                                S.op('act', lambda e: e.activation(out=sig[b][:, 0:tn], in_=psg[b][:, 0:tn], func=AF.Silu),
                                     excl=[('psg', b)], writes=[('sig', b)])
                                S.op('dve', lambda e: e.tensor_tensor(out=yco[b][:, 0:tn], in0=psa[b][:, 0:tn], in1=sig[b][:, 0:tn],
                                                                      op=ALU.mult),
                                     reads=[('sig', b)], excl=[('psa', b)], writes=[('yco', b)])
                                S.dma('sp', ycT[m * 128:(m + 1) * 128, t0:t0 + tn], yco[b][:, 0:tn], reads=[('yco', b)],
                                      writes=[('ycT', m, n)])
                            ada_bg(1)
                    S.barrier()

                Ds = contextlib.ExitStack()
                with Ds:
                    wb = [Ds.enter_context(nc.sbuf_tensor(f"wbD{l}_{i}", [128, 16, 512], BF16)) for i in range(2)]
                    wg = Ds.enter_context(nc.sbuf_tensor(f"wg{l}", [128, 16, 2, 40], BF16))
                    ev = [Ds.enter_context(nc.sbuf_tensor(f"ev{l}_{i}", [128, 512], BF16)) for i in range(4)]
                    psd = [Ds.enter_context(nc.psum_tensor(f"psd{l}_{i}", [128, 512], F32)) for i in range(4)]
                    cnt = 0
                    gcnt = 0
                    for (off, dst, scl) in ((OQ, qT_d, DH ** -0.5), (OK_, kT_d, 1.0)):
                        for half in range(2):
                            s = gcnt % 2
                            gcnt += 1
                            S.dma('pool', wb[s][:], w_in[l, :, off + half * 512:off + (half + 1) * 512].rearrange("(k p) n -> p k n", p=128),
                                  writes=[('wb', s)])
                            for hb in range(4):
                                hd = half * 4 + hb
                                for n, (t0, tn) in enumerate(TG):
                                    b = cnt % 4
                                    cnt += 1
                                    for k in range(16):
                                        S.op('pe', lambda e: e.matmul(psd[b][:, 0:tn], lhsT=wb[s][:, k, hb * 128:(hb + 1) * 128],
                                                                      rhs=hT[:, k, t0:t0 + tn], start=(k == 0), stop=(k == 15)),
                                             reads=[('wb', s)], excl=[('psd', b)])
                                    if b % 2 == 0:
                                        S.op('act', lambda e: e.activation(out=ev[b][:, 0:tn], in_=psd[b][:, 0:tn], func=AF.Copy, scale=scl),
                                             excl=[('psd', b)], writes=[('ev', b)])
                                    else:
                                        S.op('dve', lambda e: e.tensor_scalar(out=ev[b][:, 0:tn], in0=psd[b][:, 0:tn], scalar1=scl,
                                                                              scalar2=None, op0=ALU.mult),
                                             excl=[('psd', b)], writes=[('ev', b)])
                                    S.dma('sp', dst[hd, :, t0:t0 + tn], ev[b][:, 0:tn], reads=[('ev', b)], writes=[('qk', off, hd, n)])
                    for (off, dst) in ((OK_, ktm), (OV, vtm), (OO, otm), (OZM, zmtm)):
                        for half in range(2):
                            s = gcnt % 2
                            gcnt += 1
                            S.dma('pool', wb[s][:], w_in[l, :, off + half * 512:off + (half + 1) * 512].rearrange("(k p) n -> p k n", p=128),
                                  writes=[('wb', s)])
                            for i in range(NT):
                                b = cnt % 4
                                cnt += 1
                                for k in range(16):
                                    S.op('pe', lambda e: e.matmul(psd[b][:, :], lhsT=hT[:, k, i * 128:(i + 1) * 128],
                                                                  rhs=wb[s][:, k, :], start=(k == 0), stop=(k == 15)),
                                         reads=[('wb', s)], excl=[('psd', b)])
                                if b % 2 == 0:
                                    S.op('act', lambda e: e.activation(out=ev[b][:], in_=psd[b][:], func=AF.Copy),
                                         excl=[('psd', b)], writes=[('ev', b)])
                                else:
                                    S.op('dve', lambda e: e.tensor_copy(out=ev[b][:], in_=psd[b][:]),
                                         excl=[('psd', b)], writes=[('ev', b)])
                                S.dma('sp', dst[i * 128:(i + 1) * 128, half * 512:(half + 1) * 512], ev[b][:], reads=[('ev', b)],
                                      writes=[('tm', off, i, half)])
                    S.op('pool', lambda e: e.memset(wg[:], 0.0), writes=['wg'])
                    for (gi, r0, c0) in ((0, 0, 0), (1, 0, 8), (0, 32, 16), (1, 32, 24)):
                        S.dma('pool', wg[:, :, gi, r0:r0 + 8],
                              w_in[l, :, OGT + c0:OGT + c0 + 8].rearrange("(k p) n -> p k n", p=128),
                              reads=['wg'], writes=['wg'], allow_slow_non_contiguous=True)
                    for n, (t0, tn) in enumerate(TG):
                        for gi in range(2):
                            b = cnt % 4
                            cnt += 1
                            for k in range(16):
                                S.op('pe', lambda e: e.matmul(psd[b][0:40, 0:tn], lhsT=wg[:, k, gi, :], rhs=hT[:, k, t0:t0 + tn],
                                                              start=(k == 0), stop=(k == 15)),
                                     reads=['wg'], excl=[('psd', b)])
                            if gi == 0:
                                S.op('act', lambda e: e.activation(out=GI[:, t0:t0 + tn], in_=psd[b][0:40, 0:tn], func=AF.Identity,
                                                                   bias=bgI[:, l:l + 1]),
                                     excl=[('psd', b)], writes=[('GI', n)])
                            else:
                                S.op('act', lambda e: e.activation(out=GF[:, t0:t0 + tn], in_=psd[b][0:40, 0:tn], func=AF.Exp,
                                                                   scale=-1.0, bias=nbgF[:, l:l + 1]),
                                     excl=[('psd', b)], writes=[('GF', n)])
                    if l + 1 < nlayers:
                        ada_bg(48)
                        ada_finish(l + 1)
                    S.barrier()
            Eps = contextlib.ExitStack()
            with Eps:
                def ET(name, shape, dt):
                    return Eps.enter_context(nc.sbuf_tensor(f"{name}{l}", shape, dt))
                PRE = ET("PRE", [40, T], F32)
                scanmask = ET("scanmask", [40, T], F32)
                S.op('pool', lambda e: e.memset(scanmask[:], 1.0), writes=['scanmask'])
                smv = scanmask[:].rearrange("p (c t) -> p c t", t=CH)
                S.op('pool', lambda e: e.memset(smv[:, :, 0:1], 0.0), reads=['scanmask'], writes=['scanmask'])
                cl = ET("cl", [40, 8, NCH], F32)
                w0x = ET("w0x", [40, NCH, 16], F32)
                psE = [Eps.enter_context(nc.psum_tensor(f"psE{l}_{i}", [128, 512], F32)) for i in range(3)]
                allG = [('GI', n) for n in range(5)] + [('GF', n) for n in range(5)]
                S.op('act', lambda e: e.activation(out=GF[:], in_=GF[:], func=AF.Ln, bias=1.0), writes=['GF'])
                S.op('dve', lambda e: e.tensor_scalar(out=GF[:], in0=GF[:], scalar1=-1.0, scalar2=None, op0=ALU.mult),
                     reads=['GF'], writes=['GF'])
                if debug and l == 0:
                    S.dma('sp', dbg['gates'][0], GI[:], reads=['GF'])
                    S.dma('sp', dbg['gates'][1], GF[:], reads=['GF'])
                S.op('dve', lambda e: e.tensor_tensor_scan(out=PRE[:], data0=scanmask[:], data1=GF[:], initial=0.0,
                                                           op0=ALU.mult, op1=ALU.add),
                     reads=['GF', 'scanmask'], writes=['PRE'])
                PREv = PRE[:].rearrange("p (c t) -> p c t", t=CH)
                GFv = GF[:].rearrange("p (c t) -> p c t", t=CH)
                GIv = GI[:].rearrange("p (c t) -> p c t", t=CH)
                S.op('dve', lambda e: e.tensor_copy(out=cl[:, 0, :], in_=PREv[:, :, CH - 1]), reads=['PRE'], writes=['cl0'])
                S.op('dve', lambda e: e.tensor_tensor(out=GF[32:40, :], in0=GF[32:40, :], in1=PRE[32:40, :], op=ALU.subtract),
                     reads=['GF', 'PRE'], writes=['GF'])
                S.op('dve', lambda e: e.tensor_tensor(out=GFv[32:40], in0=GFv[32:40], in1=bc(cl[32:40, 0, :], 2, CH), op=ALU.add),
                     reads=['GF', 'cl0'], writes=['GF'])
                S.op('dve', lambda e: e.tensor_copy(out=GF[0:32, :], in_=PRE[0:32, :]), reads=['GF', 'PRE'], writes=['GF'])
                S.op('dve', lambda e: e.tensor_tensor(out=GI[:], in0=GI[:], in1=GF[:], op=ALU.subtract), reads=['GF'], writes=['GI'])
                S.op('dve', lambda e: e.tensor_reduce(out=cl[:, 1, :], in_=GIv, axis=AX.X, op=ALU.max), reads=['GI'], writes=['cl1'])
                S.op('dve', lambda e: e.tensor_tensor(out=cl[:, 2, :], in0=cl[:, 0, :], in1=cl[:, 1, :], op=ALU.add),
                     reads=['cl0', 'cl1'], writes=['cl2'])

                def rev_ap(ap2, lo, n):
                    a = ap2[:, lo:lo + n]
                    return bass.AP(a.tensor, a.offset + (n - 1) * a.ap[1][0], [list(a.ap[0]), [-a.ap[1][0], n]])

                for (so, de) in ((0, 3), (2, 4)):
                    S.op('dve', lambda e: e.tensor_copy(out=cl[0:32, de, :], in_=cl[0:32, so, :]), reads=[f'cl{so}'], writes=[f'cl{de}'])
                    S.op('dve', lambda e: e.tensor_copy(out=cl[32:40, de, 0:NCC], in_=rev_ap(cl[32:40, so, :], 0, NCC)),
                         reads=[f'cl{so}', f'cl{de}'], writes=[f'cl{de}'])
                    S.op('dve', lambda e: e.tensor_copy(out=cl[32:40, de, NCC:NCH], in_=rev_ap(cl[32:40, so, :], NCC, NCH - NCC)),
                         reads=[f'cl{so}', f'cl{de}'], writes=[f'cl{de}'])
                S.op('dve', lambda e: e.tensor_tensor_scan(out=cl[:, 5, :], data0=cl[:, 3, :], data1=cl[:, 4, :], initial=NEG,
                                                           op0=ALU.add, op1=ALU.max),
                     reads=['cl3', 'cl4'], writes=['cl5'])
                S.op('dve', lambda e: e.tensor_tensor(out=cl[:, 6, :], in0=cl[:, 5, :], in1=cl[:, 3, :], op=ALU.subtract),
                     reads=['cl5', 'cl3'], writes=['cl6'])
                S.op('pool', lambda e: e.memset(cl[:, 7, 0:1], NEG), writes=['cl7'])
                S.op('dve', lambda e: e.tensor_copy(out=cl[:, 7, 1:NCH], in_=cl[:, 5, 0:NCH - 1]), reads=['cl5', 'cl7'], writes=['cl7'])
                S.op('dve', lambda e: e.tensor_tensor(out=cl[:, 7, :], in0=cl[:, 7, :], in1=cl[:, 6, :], op=ALU.subtract),
                     reads=['cl7', 'cl6'], writes=['cl7'])
                S.op('act', lambda e: e.activation(out=cl[:, 7, :], in_=cl[:, 7, :], func=AF.Exp), reads=['cl7'], writes=['cl7'])
                S.op('dve', lambda e: e.tensor_copy(out=cl[0:32, 4, :], in_=cl[0:32, 6, :]), reads=['cl6', 'cl4'], writes=['cl4'])
                S.op('dve', lambda e: e.tensor_copy(out=cl[32:40, 4, 0:NCC], in_=rev_ap(cl[32:40, 6, :], 0, NCC)),
                     reads=['cl6', 'cl4'], writes=['cl4'])
                S.op('dve', lambda e: e.tensor_copy(out=cl[32:40, 4, NCC:NCH], in_=rev_ap(cl[32:40, 6, :], NCC, NCH - NCC)),
                     reads=['cl6', 'cl4'], writes=['cl4'])
                S.op('dve', lambda e: e.tensor_tensor(out=GIv, in0=GIv, in1=bc(cl[:, 4, :], 2, CH), op=ALU.subtract),
                     reads=['GI', 'cl4'], writes=['GI'])
                S.op('act', lambda e: e.activation(out=GI[:], in_=GI[:], func=AF.Exp), reads=['GI'], writes=['GI'])
                S.op('dve', lambda e: e.tensor_tensor(out=GFv, in0=GFv, in1=bc(cl[:, 4, :], 2, CH), op=ALU.add),
                     reads=['GF', 'cl4'], writes=['GF'])
                S.op('act', lambda e: e.activation(out=GF[:], in_=GF[:], func=AF.Exp, scale=-1.0), reads=['GF'], writes=['GF'])
                if debug and l == 0:
                    S.dma('sp', dbg['gates'][2], GI[:], reads=['GI'])
                    S.dma('sp', dbg['gates'][3], GF[:], reads=['GF'])
                for q, (srcg, key) in enumerate(((GI, 'GI'), (GF, 'GF'))):
                    for bi, (c0, c1) in enumerate(((0, NCH),)):
                        pb = psE[bi]
                        for c in range(c0, c1):
                            S.op('pe', lambda e: e.matmul(pb[0:CH, (c - c0) * 16:(c - c0 + 1) * 16],
                                                          lhsT=srcg[0:40, c * CH:(c + 1) * CH], rhs=sel16[:, :], start=True, stop=True),
                                 reads=[key], excl=[('psE', bi)])
                        S.op('dve', lambda e: e.tensor_copy(out=etm[q][:, c0:c1, :].rearrange("p c h -> p (c h)"),
                                                            in_=pb[0:CH, 0:(c1 - c0) * 16]),
                             excl=[('psE', bi)], writes=[('etm', q, bi)])
                S.op('dve', lambda e: e.tensor_tensor(out=w0x[:], in0=bc(cl[:, 7, :], 2, 16), in1=bc(sel16[:, :], 1, NCH),
                                                      op=ALU.mult),
                     reads=['cl7'], writes=['w0x'])
                w0xf = w0x[:].rearrange("p c h -> p (c h)")
                w0bf = w0b[:].rearrange("p c h -> p (c h)")
                for i in range(1):
                    S.op('pe', lambda e: e.matmul(psE[i][:, 0:288], lhsT=ones_f[0:40, :], rhs=w0xf[:, i * 288:(i + 1) * 288],
                                                  start=True, stop=True),
                         reads=['w0x'], excl=[('psE', i)])
                    S.op('dve', lambda e: e.tensor_copy(out=w0bf[:, i * 288:(i + 1) * 288], in_=psE[i][:, 0:288]),
                         excl=[('psE', i)], writes=[('w0b', i)])
                if debug and l == 0:
                    for q in range(2):
                        S.dma('sp', dbg['etm'][q], etm[q][:].rearrange("p c h -> p (c h)"), reads=[('etm', q, bi) for bi in range(1)])
                    S.dma('sp', dbg['w0b'], w0bf, reads=[('w0b', i) for i in range(1)])
                S.barrier()
            Gs.close()

            Ss = contextlib.ExitStack()
            with Ss:
                def ST(name, shape, dt):
                    return Ss.enter_context(nc.sbuf_tensor(f"{name}{l}", shape, dt))
                SC = 2
                NSC = NCH // SC
                HA = DH + 1
                qb = [[ST(f"qb{d}_{i}_", [128, H, SC * CH], BF16) for i in range(2)] for d in range(2)]
                kb = [[ST(f"kb{d}_{i}_", [128, H, SC * CH], BF16) for i in range(2)] for d in range(2)]
                ktb = [[ST(f"ktb{d}_{i}_", [CH, SC, WC], BF16) for i in range(2)] for d in range(2)]
                vtb = [[ST(f"vtb{d}_{i}_", [CH, SC, H, HA], BF16) for i in range(2)] for d in range(2)]
                Cst = [ST(f"Cst{d}_", [128, H, HA], F32) for d in range(2)]
                Cb = [ST(f"Cb{d}_", [128, H, HA], BF16) for d in range(2)]
                PT = [[ST(f"PT{d}_{i}_", [CH, H, CH], BF16) for i in range(2)] for d in range(2)]
                kE = [[ST(f"kE{d}_{i}_", [CH, H, DH], BF16) for i in range(2)] for d in range(2)]
                dpos = [ST(f"dpos{d}_", [CH, H], F32) for d in range(2)]
                dden = [ST(f"dden{d}_", [CH, H], F32) for d in range(2)]
                hout = [[ST(f"hout{d}_{i}_", [CH, H, DH], F32) for i in range(2)] for d in range(2)]
                psS = [Ss.enter_context(nc.psum_tensor(f"psS{l}_{hf}", [CH, 4, CH], F32)) for hf in range(2)]
                psN = [Ss.enter_context(nc.psum_tensor(f"psN{l}_{i}", [CH, 3, HA], F32)) for i in range(3)]
                psC = [Ss.enter_context(nc.psum_tensor(f"psC{l}_{i}", [128, 3, HA], F32)) for i in range(3)]
                GH = [(0, 3), (3, 3), (6, 2)]

                for d in range(2):
                    S.op('pool', lambda e: e.memset(Cst[d][:], 0.0), writes=[('Cst', d, h) for h in range(H)])
                    S.op('pool', lambda e: e.memset(Cb[d][:], 0.0), writes=[('Cb', d, 0), ('Cb', d, 1)])
                    for i in range(2):
                        S.op('pool', lambda e: e.memset(vtb[d][i][:, :, :, DH:HA], 1.0), writes=[('vtb1', d, i)])

                def nat_chunk(d, p):
                    if d == 0:
                        return p
                    return NCC - 1 - p if p < NCC else NCH - 1 + NCC - p

                sc_base = {}

                def load_sc(d, sp_):
                    p0 = sp_ * SC
                    cs = sorted(nat_chunk(d, p0 + i) for i in range(SC))
                    c0 = cs[0]
                    assert cs == list(range(c0, c0 + SC))
                    s = sp_ % 2
                    t0 = c0 * CH
                    q_ = 'sp' if d == 0 else 'pool'
                    S.dma(q_, qb[d][s][:], qT_d[:, :, t0:t0 + SC * CH].rearrange("h p t -> p h t"), writes=[('qb', d, s)])
                    S.dma(q_, kb[d][s][:], kT_d[:, :, t0:t0 + SC * CH].rearrange("h p t -> p h t"), writes=[('kb', d, s)])
                    S.dma(q_, ktb[d][s][:], ktm[t0:t0 + SC * CH, :].rearrange("(c s) e -> s c e", s=CH), writes=[('ktb', d, s)])
                    for ci in range(SC):
                        S.dma(q_, vtb[d][s][:, ci, :, 0:DH],
                              vtm[t0 + ci * CH:t0 + (ci + 1) * CH, :].rearrange("s (h e) -> s h e", e=DH),
                              reads=[('vtb1', d, s)], writes=[('vtb', d, s, ci)])
                    sc_base[(d, sp_)] = c0

                def stage1(p, d):
                    sp_ = p // SC
                    s = sp_ % 2
                    par = p % 2
                    c = nat_chunk(d, p)
                    ci = c - sc_base[(d, sp_)]
                    r0 = d * 8
                    tsl = slice(ci * CH, (ci + 1) * CH)
                    Ecol = etm[0][:, c, r0:r0 + 8]
                    S.op('pool', lambda e: e.tensor_tensor(out=kE[d][par][:], in0=ktb[d][s][:, ci, :].rearrange("p (h e) -> p h e", e=DH),
                                                           in1=bc(Ecol, 2, DH), op=ALU.mult),
                         reads=[('ktb', d, s)], writes=[('kE', d, par)])
                    for hf in range(2):
                        for hh in range(4):
                            h = hf * 4 + hh
                            S.op('pe', lambda e: e.matmul(psS[hf][:, hh, :], lhsT=kb[d][s][:, h, tsl], rhs=qb[d][s][:, h, tsl],
                                                          start=True, stop=True),
                                 reads=[('kb', d, s), ('qb', d, s)], excl=[('psS', hf)])
                        for hh in range(4):
                            h = hf * 4 + hh
                            S.op('dve', lambda e: e.scalar_tensor_tensor(out=PT[d][par][:, h, :], in0=psS[hf][:, hh, :],
                                                                         scalar=Ecol[:, h:h + 1], in1=mask2[:, d, :],
                                                                         op0=ALU.mult, op1=ALU.mult),
                                 excl=[('psS', hf)], writes=[('PT', d, par, h)])

                def stage2(p, d):
                    sp_ = p // SC
                    s = sp_ % 2
                    par = p % 2
                    c = nat_chunk(d, p)
                    ci = c - sc_base[(d, sp_)]
                    r0 = d * 8
                    tsl = slice(ci * CH, (ci + 1) * CH)
                    Fcol = etm[1][:, c, r0:r0 + 8]
                    for h in range(H):
                        g, hh = h // 3, h % 3
                        vh = vtb[d][s][:, ci, h, :]
                        S.op('pe', lambda e: e.matmul(psN[g][:, hh, :], lhsT=qb[d][s][:, h, tsl], rhs=Cb[d][:, h, :],
                                                      start=True, stop=False),
                             reads=[('qb', d, s), ('Cb', d, h // 4)], excl=[('psN', g)])
                        S.op('pe', lambda e: e.matmul(psN[g][:, hh, :], lhsT=PT[d][par][:, h, :], rhs=vh, start=False, stop=True),
                             reads=[('PT', d, par, h), ('vtb', d, s, ci), ('vtb1', d, s)], excl=[('psN', g)])
                    last_step = (p + 1 >= NCH)
                    if not last_step:
                        for h in range(H):
                            g, hh = h // 3, h % 3
                            vh = vtb[d][s][:, ci, h, :]
                            S.op('pe', lambda e: e.matmul(psC[g][:, hh, :], lhsT=kE[d][par][:, h, :], rhs=vh, start=True, stop=True),
                                 reads=[('kE', d, par), ('vtb', d, s, ci), ('vtb1', d, s)], excl=[('psC', g)])
                    for g, (h0, nh) in enumerate(GH):
                        S.op('act', lambda e: e.activation(out=dpos[d][:, h0:h0 + nh], in_=psN[g][:, 0:nh, DH], func=AF.Copy),
                             excl=[('psN', g)], writes=[('dpos', d, g)])
                    S.op('dve', lambda e: e.tensor_scalar(out=dden[d][:], in0=dpos[d][:], scalar1=-1.0, scalar2=None, op0=ALU.mult),
                         reads=[('dpos', d, g) for g in range(3)], writes=[('dden', d)])
                    S.op('dve', lambda e: e.tensor_tensor(out=dden[d][:], in0=dden[d][:], in1=dpos[d][:], op=ALU.max),
                         reads=[('dden', d)] + [('dpos', d, g) for g in range(3)], writes=[('dden', d)])
                    S.op('dve', lambda e: e.tensor_tensor(out=dden[d][:], in0=dden[d][:], in1=Fcol, op=ALU.max),
                         reads=[('dden', d)], writes=[('dden', d)])
                    S.op('dve', lambda e: e.reciprocal(out=dden[d][:], in_=dden[d][:]), reads=[('dden', d)], writes=[('dden', d)])
                    for g in range(2):
                        S.op('dve', lambda e: e.tensor_tensor(out=hout[d][par][:, 3 * g:3 * g + 3, :], in0=psN[g][:, 0:3, 0:DH],
                                                              in1=bc(dden[d][:, 3 * g:3 * g + 3], 2, DH), op=ALU.mult),
                             reads=[('dden', d)], excl=[('psN', g)], writes=[('hout', d, par, h) for h in range(3 * g, 3 * g + 3)])
                    for h in range(6, H):
                        g, hh = h // 3, h % 3
                        S.op('act', lambda e: e.activation(out=hout[d][par][:, h, :], in_=psN[g][:, hh, 0:DH], func=AF.Copy,
                                                           scale=dden[d][:, h:h + 1]),
                             reads=[('dden', d)], excl=[('psN', g)], writes=[('hout', d, par, h)])
                    S.dma('sp', hfb[d][c * CH:(c + 1) * CH, :], hout[d][par][:].rearrange("p h e -> p (h e)"),
                          reads=[('hout', d, par, h) for h in range(H)], writes=[('h_d', d, c)])
                    if not last_step:
                        w0c = w0b[:, p, r0:r0 + 8]
                        w0n = w0b[:, p + 1, r0:r0 + 8]
                        for h in range(H):
                            g, hh = h // 3, h % 3
                            S.op('dve', lambda e: e.scalar_tensor_tensor(out=Cst[d][:, h, :], in0=Cst[d][:, h, :], scalar=w0c[:, h:h + 1],
                                                                         in1=psC[g][:, hh, :], op0=ALU.mult, op1=ALU.add),
                                 reads=[('Cst', d, h)], excl=[('psC', g)], writes=[('Cst', d, h)])
                        S.op('dve', lambda e: e.tensor_tensor(out=Cb[d][:, 0:4, :], in0=Cst[d][:, 0:4, :], in1=bc(w0n[:, 0:4], 2, HA), op=ALU.mult),
                             reads=[('Cst', d, h) for h in range(4)], writes=[('Cb', d, 0)])
                        for h in range(4, H):
                            S.op('act', lambda e: e.activation(out=Cb[d][:, h, :], in_=Cst[d][:, h, :], func=AF.Copy, scale=w0n[:, h:h + 1]),
                                 reads=[('Cst', d, h)], writes=[('Cb', d, 1)])

                for d in range(2):
                    load_sc(d, 0)
                for d in range(2):
                    stage1(0, d)
                for p in range(NCH):
                    if p % SC == 0 and p // SC + 1 < NSC:
                        for d in range(2):
                            load_sc(d, p // SC + 1)
                    if p + 1 < NCH:
                        for d in range(2):
                            stage1(p + 1, d)
                    for d in range(2):
                        stage2(p, d)
                S.barrier()
            Xs.close()
            Fs = contextlib.ExitStack()
            with Fs:
                def FT(name, shape, dt):
                    return Fs.enter_context(nc.sbuf_tensor(f"{name}{l}", shape, dt))
                wo = FT("wo", [128, 16, D], BF16)
                gtile = [FT("gtx", [128, D], F32), FT("gtc", [128, D], F32)]
                ghb = FT("ghb", [128, WC], F32)
                NB3 = 3
                hfs = [FT(f"hfs{i}_", [128, H, DH], F32) for i in range(NB3)]
                hbs = [FT(f"hbs{i}_", [128, H, DH], F32) for i in range(NB3)]
                ot = [FT(f"ot{i}_", [128, WC], BF16) for i in range(NB3)]
                zt = [FT(f"zt{i}_", [128, WC], BF16) for i in range(NB3)]
                sgo_ = [FT(f"sgo{i}_", [128, WC], F32) for i in range(2)]
                szm_ = [FT(f"szm{i}_", [128, WC], BF16) for i in range(2)]
                hsq = FT("hsq", [128, H, DH], BF16)
                st8_ = [FT(f"st8{i}_", [128, 7, H], F32) for i in range(2)]
                ym_ = [FT(f"ym{i}_", [128, WC], BF16) for i in range(2)]
                yT = [FT(f"yT{i}_", [128, 16, 128], BF16) for i in range(NB3)]
                xr = [FT(f"xr{i}_", [128, D], F32) for i in range(NB3)]
                xo = [FT(f"xo{i}_", [128, D], F32) for i in range(2)]
                st1 = FT("st1", [128, 8], F32)
                sqj = FT("sqj", [128, 512], BF16)
                pso = Fs.enter_context(nc.psum_tensor(f"pso{l}", [128, D], F32))
                psy = [Fs.enter_context(nc.psum_tensor(f"psy{l}_{i}", [128, 4, 128], BF16)) for i in range(2)]
                for half in range(2):
                    S.dma('pool', wo[:, :, half * 1024:(half + 1) * 1024],
                          w_out[l, :, half * 1024:(half + 1) * 1024].rearrange("(k p) n -> p k n", p=128), writes=[('wo', half)])
                gpb = xr[0]
                S.dma('sp', gpb[:], g_post[l:l + 1, :].to_broadcast([128, D]), writes=[('xr', 0)])
                S.dma('sp', ghb[:], g_head[l:l + 1, :].to_broadcast([128, WC]), writes=['ghb'])
                for jx in range(2):
                    S.dma('sp', gtile[jx][:], adaD[jx:jx + 1, :].to_broadcast([128, D]), writes=[('gt', jx)])
                    S.op('dve', lambda e: e.tensor_tensor(out=gtile[jx][:], in0=gtile[jx][:], in1=gpb[:], op=ALU.mult),
                         reads=[('xr', 0), ('gt', jx)], writes=[('gt', jx)])
                tiles = list(range(NT)) if not last else list(range(2, NT))

                def stageA(it):
                    i = tiles[it]
                    s = it % NB3
                    s2 = it % 2
                    sgo, szm, st8, ym = sgo_[s2], szm_[s2], st8_[s2], ym_[s2]
                    k2 = lambda n: (n, s2)
                    rs_ = slice(i * 128, (i + 1) * 128)
                    S.dma('sp', hfs[s][:].rearrange("p h e -> p (h e)"), hfb[0][rs_, :], writes=[('hfs', s)])
                    S.dma('sp', hbs[s][:].rearrange("p h e -> p (h e)"), hfb[1][rs_, :], writes=[('hbs', s)])
                    S.dma('sp', ot[s][:], otm[rs_, :], writes=[('ot', s)])
                    S.dma('sp', zt[s][:], zmtm[rs_, :], writes=[('zt', s)])
                    S.dma('sp', xr[s][:], src[rs_, :], writes=[('xr', s)])
                    S.dma('sp', yT[s][:, 0:8, :], ycT[:, rs_].rearrange("(j p) t -> p j t", p=128), writes=[('yTc', s)])
                    S.op('act', lambda e: e.activation(out=sgo[:], in_=ot[s][:], func=AF.Sigmoid), reads=[('ot', s)], writes=[k2('sgo')])
                    S.op('act', lambda e: e.activation(out=szm[:], in_=zt[s][:], func=AF.Silu), reads=[('zt', s)], writes=[k2('szm')])
                    S.op('pool', lambda e: e.tensor_tensor(out=hfs[s][:], in0=hfs[s][:], in1=hbs[s][:], op=ALU.add),
                         reads=[('hfs', s), ('hbs', s)], writes=[('hfs', s)])
                    S.op('pool', lambda e: e.tensor_tensor(out=sgo[:], in0=sgo[:], in1=ghb[:], op=ALU.mult),
                         reads=[k2('sgo'), 'ghb'], writes=[k2('sgo')])
                    S.op('pool', lambda e: e.tensor_tensor(out=sgo[:], in0=sgo[:], in1=szm[:], op=ALU.mult),
                         reads=[k2('sgo'), k2('szm')], writes=[k2('sgo')])
                    yield
                    S.op('dve', lambda e: e.tensor_reduce(out=st8[:, 0, :], in_=hfs[s][:], axis=AX.X, op=ALU.add),
                         reads=[('hfs', s)], writes=[k2('st8_0')])
                    S.op('act', lambda e: e.activation(out=hsq[:], in_=hfs[s][:], func=AF.Square), reads=[('hfs', s)], writes=['hsq'])
                    S.op('dve', lambda e: e.tensor_reduce(out=st8[:, 1, :], in_=hsq[:], axis=AX.X, op=ALU.add),
                         reads=['hsq'], writes=[k2('st8_1')])
                    S.op('dve', lambda e: e.tensor_scalar(out=st8[:, 2, :], in0=st8[:, 0, :], scalar1=1.0 / DH, scalar2=None, op0=ALU.mult),
                         reads=[k2('st8_0')], writes=[k2('st8_2')])
                    S.op('dve', lambda e: e.tensor_tensor(out=st8[:, 3, :], in0=st8[:, 2, :], in1=st8[:, 2, :], op=ALU.mult),
                         reads=[k2('st8_2')], writes=[k2('st8_3')])
                    S.op('dve', lambda e: e.scalar_tensor_tensor(out=st8[:, 4, :], in0=st8[:, 1, :], scalar=1.0 / DH, in1=st8[:, 3, :],
                                                                 op0=ALU.mult, op1=ALU.subtract),
                         reads=[k2('st8_1'), k2('st8_3')], writes=[k2('st8_4')])
                    S.op('dve', lambda e: e.tensor_scalar(out=st8[:, 4, :], in0=st8[:, 4, :], scalar1=EPS, scalar2=None, op0=ALU.add),
                         reads=[k2('st8_4')], writes=[k2('st8_4')])
                    S.op('pool', lambda e: e.tensor_tensor(out=st8[:, 5, :], in0=st8[:, 4, :], in1=neghalf[:, 0:H], op=ALU.pow),
                         reads=[k2('st8_4')], writes=[k2('st8_5')])
                    yield
                    S.op('dve', lambda e: e.tensor_tensor(out=hfs[s][:], in0=hfs[s][:], in1=bc(st8[:, 2, :], 2, DH), op=ALU.subtract),
                         reads=[('hfs', s), k2('st8_2')], writes=[('hfs', s)])
                    S.op('dve', lambda e: e.tensor_tensor(out=hfs[s][:], in0=hfs[s][:], in1=bc(st8[:, 5, :], 2, DH), op=ALU.mult),
                         reads=[('hfs', s), k2('st8_5')], writes=[('hfs', s)])
                    yield
                    hflat = hfs[s][:].rearrange("p h e -> p (h e)")
                    S.op('dve', lambda e: e.tensor_tensor(out=ym[:], in0=hflat, in1=sgo[:], op=ALU.mult),
                         reads=[('hfs', s), k2('sgo')], writes=[k2('ym')])
                    for half in range(2):
                        for jj in range(4):
                            j = half * 4 + jj
                            S.op('pe', lambda e: e.transpose(out=psy[half][:, jj, :], in_=ym[:, j * 128:(j + 1) * 128], identity=ident_b[:]),
                                 reads=[k2('ym')], excl=[('psy', half)])
                        if half == 0:
                            S.op('act', lambda e: e.activation(out=yT[s][:, 8:12, :], in_=psy[0][:], func=AF.Copy),
                                 excl=[('psy', 0)], writes=[('yTm', s, 0)])
                        else:
                            S.op('dve', lambda e: e.tensor_copy(out=yT[s][:, 12:16, :], in_=psy[1][:]),
                                 excl=[('psy', 1)], writes=[('yTm', s, 1)])

                def stageB_pe(it):
                    s = it % NB3
                    for nn in range(4):
                        for k in range(16):
                            rk = [('yTc', s)] if k < 8 else [('yTm', s, (k - 8) // 4)]
                            S.op('pe', lambda e: e.matmul(pso[:, nn * 512:(nn + 1) * 512], lhsT=yT[s][:, k, :],
                                                          rhs=wo[:, k, nn * 512:(nn + 1) * 512], start=(k == 0), stop=(k == 15)),
                                 reads=rk + [('wo', nn // 2)], excl=[('pso', nn)])

                def stageB_ep(it, agen=None):
                    i = tiles[it]
                    s = it % NB3
                    sx = it % 2
                    jx = 1 if i < 2 else 0
                    rs_ = slice(i * 128, (i + 1) * 128)
                    for nn in range(4):
                        if agen is not None:
                            next(agen, None)
                        cs_ = slice(nn * 512, (nn + 1) * 512)
                        S.op('act', lambda e: e.activation(out=sqj[:], in_=pso[:, cs_], func=AF.Square, accum_out=st1[:, nn:nn + 1]),
                             excl=[('pso', nn)], writes=['sqj', ('st1', nn)])
                        S.op('dve', lambda e: e.tensor_tensor(out=xo[sx][:, cs_], in0=pso[:, cs_], in1=gtile[jx][:, cs_], op=ALU.mult),
                             reads=[('gt', jx)], excl=[('pso', nn)], writes=[('xo', sx, nn)])
                    S.op('dve', lambda e: e.tensor_reduce(out=st1[:, 4:5], in_=st1[:, 0:4], axis=AX.X, op=ALU.add),
                         reads=[('st1', nn) for nn in range(4)], writes=['st1_s'])
                    S.op('dve', lambda e: e.tensor_scalar(out=st1[:, 5:6], in0=st1[:, 4:5], scalar1=1.0 / D, scalar2=EPS,
                                                          op0=ALU.mult, op1=ALU.add),
                         reads=['st1_s'], writes=['st1_1'])
                    S.op('pool', lambda e: e.tensor_tensor(out=st1[:, 6:7], in0=st1[:, 5:6], in1=neghalf[:, 0:1], op=ALU.pow),
                         reads=['st1_1'], writes=['st1_2'])
                    S.op('dve', lambda e: e.scalar_tensor_tensor(out=xo[sx][:], in0=xo[sx][:], scalar=st1[:, 6:7], in1=xr[s][:],
                                                                 op0=ALU.mult, op1=ALU.add),
                         reads=[('xo', sx, nn) for nn in range(4)] + ['st1_2', ('xr', s)], writes=[('xo', sx, nn) for nn in range(4)])
                    S.dma('pool', xout[rs_, :], xo[sx][:], reads=[('xo', sx, nn) for nn in range(4)], writes=[('xout', i)])

                for _ in stageA(0):
                    pass
                if len(tiles) > 1:
                    for _ in stageA(1):
                        pass
                for it in range(len(tiles)):
                    stageB_pe(it)
                    agen = stageA(it + 2) if it + 2 < len(tiles) else None
                    stageB_ep(it, agen)
                    if agen is not None:
                        for _ in agen:
                            pass
                S.barrier()
        S.barrier()
    return nc, S


_CACHE = {}


def kernel(x, c, ctx, c_ctx, w_ada, b_ada, g_pre, g_post, w_in, b_gate, w_dw, b_dw, ln_g, ln_b, w_pw2,
           g_head, w_out):
    if 'nc' not in _CACHE:
        _CACHE['nc'] = build()[0]
    nc = _CACHE['nc']
    f = lambda a: np.ascontiguousarray(np.asarray(a, dtype=np.float32))
    shared = {"w_ada": f(w_ada), "b_ada": f(b_ada), "g_pre": f(g_pre), "g_post": f(g_post), "w_in": f(w_in),
              "b_gate": f(b_gate), "w_dw": f(w_dw), "b_dw": f(b_dw), "ln_g": f(ln_g), "ln_b": f(ln_b),
              "w_pw2": f(w_pw2), "g_head": f(g_head), "w_out": f(w_out)}
    x = f(x); ctx = f(ctx); c = f(c); c_ctx = f(c_ctx)
    in_maps = []
    for core in range(8):
        b = core % 4
        m = dict(shared)
        m["xin"] = np.ascontiguousarray(np.concatenate([ctx[b], x[b]], axis=0))
        m["cc"] = np.ascontiguousarray(np.stack([c[b], c_ctx], axis=0))
        in_maps.append(m)
    res = run_bass_kernel_spmd(nc, in_maps, core_ids=list(range(8)))
    out = np.stack([np.asarray(res.results[b]["xout"])[NCTX:] for b in range(4)], axis=0)
    return out.astype(np.float32)
```

```python
import contextlib
import numpy as np
import concourse.bass as bass
import concourse.mybir as mybir
from concourse.bass_utils import run_bass_kernel_spmd

F32 = mybir.dt.float32
BF16 = mybir.dt.bfloat16
AF = mybir.ActivationFunctionType
ALU = mybir.AluOpType
AX = mybir.AxisListType

D = 2048
NCTX = 256
NLAT = 2048
T = NCTX + NLAT
NT = T // 128
DEPTH = 4
WC = 1024
H = 8
DH = 128
CH = 128
NCC = NCTX // CH
NCH = T // CH
NIN = 8224
OA, OG, OZ, OQ, OK_, OV, OO, OZM, OGT = 0, 1024, 2048, 3072, 4096, 5120, 6144, 7168, 8192
EPS = 1e-6
NEG = -1e30
TG = [(0, 256)] + [(256 + 512 * i, 512) for i in range(4)]


class Sched:
    EPOCH = 30000
    NDMA = 12

    def __init__(self, nc):
        self.nc = nc
        self.engs = {'pe': nc.tensor, 'act': nc.scalar, 'dve': nc.vector,
                     'pool': nc.gpsimd, 'sp': nc.sync}
        self.cnt = {e: 0 for e in self.engs}
        self.sems = {e: [] for e in self.engs}
        self.seen = {e: {} for e in self.engs}
        self.res = {}
        self.dq = {}
        self.nsem = 0
        self.nwait = 0
        self.nins = 0

    def _newsem(self, name):
        self.nsem += 1
        return self.nc.alloc_semaphore(name=name)

    def _esem(self, e, n):
        ep = (n - 1) // self.EPOCH
        while len(self.sems[e]) <= ep:
            self.sems[e].append(self._newsem(f"s_{e}_{len(self.sems[e])}"))
        return self.sems[e][ep], (n - 1) % self.EPOCH + 1

    def _wait(self, e, dep):
        if dep[0] == 'e':
            sem, val = self._esem(dep[1], dep[2])
        else:
            sem, val = dep[1], dep[2]
        k = id(sem)
        if self.seen[e].get(k, 0) >= val:
            return
        self.seen[e][k] = val
        self.engs[e].wait_ge(sem, val)
        self.nwait += 1

    def _deps(self, e, reads, writes, excl):
        deps = []
        for r in reads:
            st = self.res.get(r)
            if st and st['w'] is not None:
                deps.append(('raw', st['w']))
        for w in writes:
            st = self.res.get(w)
            if st:
                if st['w'] is not None:
                    deps.append(('waw', st['w']))
                for d in st['r'].values():
                    deps.append(('war', d))
        for w in excl:
            st = self.res.get(w)
            if st:
                if st['w'] is not None:
                    deps.append(('x', st['w']))
                for d in st['r'].values():
                    deps.append(('x', d))
        for kind, d in deps:
            if d[0] == 'e' and d[1] == e:
                if e == 'pe' or kind == 'x':
                    continue
            self._wait(e, d)

    def _commit(self, me, reads, writes, excl, ekey):
        for w in writes:
            self.res[w] = {'w': me, 'r': {}}
        for r in reads:
            st = self.res.setdefault(r, {'w': None, 'r': {}})
            st['r'][ekey] = me
        for w in excl:
            self.res[w] = {'w': me, 'r': {}}

    def op(self, e, fn, reads=(), writes=(), excl=()):
        self._deps(e, reads, writes, excl)
        ins = fn(self.engs[e])
        self.cnt[e] += 1
        n = self.cnt[e]
        sem, val = self._esem(e, n)
        ins.then_inc(sem, 1)
        self.nins += 1
        me = ('e', e, n)
        self._commit(me, reads, writes, excl, e)
        return me

    def dma(self, q, out, in_, reads=(), writes=(), **kw):
        self._deps(q, reads, writes, ())
        st = self.dq.setdefault(q, {'sems': [], 'vals': [], 'i': 0})
        i = st['i'] % self.NDMA
        if len(st['sems']) <= i:
            st['sems'].append(self._newsem(f"d_{q}_{i}"))
            st['vals'].append(0)
        sem = st['sems'][i]
        if st['vals'][i] > 0:
            self._wait(q, ('d', sem, st['vals'][i]))
        st['vals'][i] += 16
        st['i'] += 1
        self.engs[q].dma_start(out=out, in_=in_, **kw).then_inc(sem, 16)
        self.nins += 1
        me = ('d', sem, st['vals'][i])
        self._commit(me, reads, writes, (), ('dma', q, i))
        return me

    def barrier(self):
        for e in self.engs:
            for f in self.engs:
                if f != e and self.cnt[f] > 0:
                    self._wait(e, ('e', f, self.cnt[f]))
            for q, st in self.dq.items():
                for sem, val in zip(st['sems'], st['vals']):
                    if val > 0:
                        self._wait(e, ('d', sem, val))
        self.res.clear()


def bc(ap, axis, n):
    a = ap.unsqueeze(axis)
    shp = list(a.shape)
    shp[axis] = n
    return a.to_broadcast(shp)


def build(nlayers=DEPTH, debug=False):
    nc = bass.Bass("TRN2", target_bir_lowering=False)
    S = Sched(nc)

    def din(name, shape):
        return nc.dram_tensor(name, shape, F32, kind="ExternalInput").ap()

    xin = din("xin", [T, D])
    cc = din("cc", [2, D])
    w_ada = din("w_ada", [DEPTH, D, 3 * D])
    b_ada = din("b_ada", [DEPTH, 3 * D])
    g_pre = din("g_pre", [DEPTH, D])
    g_post = din("g_post", [DEPTH, D])
    w_in = din("w_in", [DEPTH, D, NIN])
    b_gate = din("b_gate", [DEPTH, 32])
    w_dw = din("w_dw", [DEPTH, 31, WC])
    b_dw = din("b_dw", [DEPTH, WC])
    ln_g = din("ln_g", [DEPTH, WC])
    ln_b = din("ln_b", [DEPTH, WC])
    w_pw2 = din("w_pw2", [DEPTH, WC, WC])
    g_head = din("g_head", [DEPTH, WC])
    w_out = din("w_out", [DEPTH, D, D])
    xout = nc.dram_tensor("xout", [T, D], F32, kind="ExternalOutput").ap()

    skind = "ExternalOutput" if debug else "Internal"

    def dscr(name, shape, dt):
        return nc.dram_tensor(name, shape, dt, kind=skind).ap()

    ycT = dscr("ycT", [WC, T], BF16)
    qT_d = dscr("qT_d", [H, DH, T], BF16)
    kT_d = dscr("kT_d", [H, DH, T], BF16)
    ktm = dscr("ktm", [T, WC], BF16)
    vtm = dscr("vtm", [T, WC], BF16)
    otm = dscr("otm", [T, WC], BF16)
    zmtm = dscr("zmtm", [T, WC], BF16)
    hfb = [dscr("hf_d", [T, WC], F32), dscr("hb_d", [T, WC], F32)]
    adaD = dscr("adaD", [2, D], F32)
    dbg = {}
    if debug:
        dbg['hT'] = dscr("dbg_hT", [128, 16, T], BF16)
        dbg['convT'] = dscr("dbg_convT", [128, 8, T], BF16)
        dbg['gates'] = dscr("dbg_gates", [4, 40, T], F32)
        dbg['etm'] = dscr("dbg_etm", [2, CH, NCH * 16], F32)
        dbg['w0b'] = dscr("dbg_w0b", [128, NCH * 16], F32)

    gs = contextlib.ExitStack()
    with gs:
        def GT(name, shape, dt):
            return gs.enter_context(nc.sbuf_tensor(name, shape, dt))

        ident_f = GT("ident_f", [128, 128], F32)
        ident_b = GT("ident_b", [128, 128], BF16)
        ones_b = GT("ones_b", [128, 128], BF16)
        ones_f = GT("ones_f", [128, 128], F32)
        mask2 = GT("mask2", [CH, 2, CH], F32)
        neghalf = GT("neghalf", [128, 512], F32)
        g_preT = GT("g_preT", [128, DEPTH * 16], F32)
        ln_gT = GT("ln_gT", [128, DEPTH * 8], F32)
        ln_bT = GT("ln_bT", [128, DEPTH * 8], F32)
        b_dwT = GT("b_dwT", [128, DEPTH * 8], F32)
        b_adaT = GT("b_adaT", [128, DEPTH * 48], F32)
        w_dwT = GT("w_dwT", [128, DEPTH * 8, 31], F32)
        bgI = GT("bgI", [40, DEPTH], F32)
        bgF = GT("bgF", [40, DEPTH], F32)
        nbgF = GT("nbgF", [40, DEPTH], F32)
        cT = GT("cT", [128, 16, 2], BF16)
        adaT = GT("adaT", [128, 48, 2], F32)
        s1T = GT("s1T", [128, 16, 2], F32)
        shT = GT("shT", [128, 16, 2], F32)
        ones_col = GT("ones_col", [64, 1], BF16)
        sel16 = GT("sel16", [40, 16], F32)

        ss = contextlib.ExitStack()
        with ss:
            stg = [ss.enter_context(nc.sbuf_tensor(f"stg{i}", [128, 128], F32)) for i in range(2)]
            wst = ss.enter_context(nc.sbuf_tensor("wst", [31, DEPTH, WC], F32))
            c32 = ss.enter_context(nc.sbuf_tensor("c32", [32, 128], F32))
            c32s = ss.enter_context(nc.sbuf_tensor("c32s", [32, 128], F32))
            pst = [ss.enter_context(nc.psum_tensor(f"pst{i}", [128, 512], F32)) for i in range(2)]

            S.op('pool', lambda e: e.memset(ident_f[:], 0.0), writes=['ident_f'])
            S.op('pool', lambda e: e.affine_select(out=ident_f[:], in_=ident_f[:], pattern=[[-1, 128]],
                                                   compare_op=ALU.not_equal, fill=1.0, base=0, channel_multiplier=1),
                 reads=['ident_f'], writes=['ident_f'])
            S.op('dve', lambda e: e.tensor_copy(out=ident_b[:], in_=ident_f[:]), reads=['ident_f'], writes=['ident_b'])
            S.op('dve', lambda e: e.tensor_copy(out=sel16[:, 0:8], in_=ident_f[0:40, 0:8]), reads=['ident_f'], writes=['sel16a'])
            S.op('dve', lambda e: e.tensor_copy(out=sel16[:, 8:16], in_=ident_f[0:40, 32:40]), reads=['ident_f'], writes=['sel16b'])
            S.op('pool', lambda e: e.memset(ones_b[:], 1.0), writes=['ones_b'])
            S.op('pool', lambda e: e.memset(ones_f[:], 1.0), writes=['ones_f'])
            S.op('pool', lambda e: e.memset(ones_col[:], 1.0), writes=['ones_col'])
            S.op('pool', lambda e: e.memset(neghalf[:], -0.5), writes=['neghalf'])
            S.op('pool', lambda e: e.memset(mask2[:], 1.0), writes=['mask2'])
            S.op('pool', lambda e: e.affine_select(out=mask2[:, 0, :], in_=mask2[:, 0, :], pattern=[[1, CH]],
                                                   compare_op=ALU.is_ge, fill=0.0, base=0, channel_multiplier=-1),
                 reads=['mask2'], writes=['mask2'])
            S.op('pool', lambda e: e.affine_select(out=mask2[:, 1, :], in_=mask2[:, 1, :], pattern=[[-1, CH]],
                                                   compare_op=ALU.is_ge, fill=0.0, base=0, channel_multiplier=1),
                 reads=['mask2'], writes=['mask2'])
            for t_ in (bgI, bgF):
                S.op('pool', lambda e: e.memset(t_[:], 0.0), writes=[t_.name if hasattr(t_, 'name') else id(t_)])
            S.barrier()
            for (dst, col0, r0) in ((bgI, 0, 0), (bgF, 8, 0), (bgI, 16, 32), (bgF, 24, 32)):
                S.dma('sp', dst[r0:r0 + 8, :], b_gate[:, col0:col0 + 8].rearrange("l h -> h l"),
                      writes=[('bg', col0)], allow_slow_non_contiguous=True)
            S.barrier()
            S.op('dve', lambda e: e.tensor_scalar(out=nbgF[:], in0=bgF[:], scalar1=-1.0, scalar2=None, op0=ALU.mult),
                 writes=['nbgF'])

            tcount = [0]

            def load_T(dst, src_rows, R):
                i = tcount[0] % 2
                tcount[0] += 1
                S.dma('sp', stg[i][0:R, :], src_rows, writes=[('stg', i)])
                S.op('pe', lambda e: e.transpose(out=pst[i][:, 0:R], in_=stg[i][0:R, :], identity=ident_f[0:R, 0:R]),
                     reads=[('stg', i)], excl=[('pst', i)])
                S.op('dve', lambda e: e.tensor_copy(out=dst, in_=pst[i][:, 0:R]), excl=[('pst', i)], writes=[('ld', tcount[0])])

            load_T(g_preT[:, :], g_pre.rearrange("l (k p) -> (l k) p", p=128), 64)
            load_T(ln_gT[:, :], ln_g.rearrange("l (k p) -> (l k) p", p=128), 32)
            load_T(ln_bT[:, :], ln_b.rearrange("l (k p) -> (l k) p", p=128), 32)
            load_T(b_dwT[:, :], b_dw.rearrange("l (k p) -> (l k) p", p=128), 32)
            bav = b_ada.rearrange("l (k p) -> (l k) p", p=128)
            load_T(b_adaT[:, 0:128], bav[0:128, :], 128)
            load_T(b_adaT[:, 128:192], bav[128:192, :], 64)
            S.dma('sp', wst[:], w_dw.rearrange("l k c -> k l c"), writes=['wst'])
            for l in range(DEPTH):
                for j in range(8):
                    i = tcount[0] % 2
                    tcount[0] += 1
                    S.op('pe', lambda e: e.transpose(out=pst[i][:, 0:31], in_=wst[0:31, l, j * 128:(j + 1) * 128],
                                                     identity=ident_f[0:31, 0:31]),
                         reads=['wst'], excl=[('pst', i)])
                    S.op('dve', lambda e: e.tensor_copy(out=w_dwT[:, l * 8 + j, :], in_=pst[i][:, 0:31]),
                         excl=[('pst', i)], writes=[('wdw', l, j)])
            S.dma('sp', c32[:], cc.rearrange("j (k p) -> (j k) p", p=128), writes=['c32'])
            S.op('act', lambda e: e.activation(out=c32s[:], in_=c32[:], func=AF.Silu), reads=['c32'], writes=['c32s'])
            S.op('pe', lambda e: e.transpose(out=pst[0][:, 0:32], in_=c32s[:], identity=ident_f[0:32, 0:32]),
                 reads=['c32s'], excl=[('pst', 0)])
            S.op('dve', lambda e: e.tensor_copy(out=cT[:].rearrange("p k j -> p j k"),
                                                in_=pst[0][:, 0:32].rearrange("p (j k) -> p j k", j=2)),
                 excl=[('pst', 0)], writes=['cT'])
            S.barrier()

        for l in range(nlayers):
            src = xin if l == 0 else xout
            last = (l == DEPTH - 1)
            Xs = contextlib.ExitStack()
            etm = [Xs.enter_context(nc.sbuf_tensor(f"etm{q}_{l}", [CH, NCH, 16], F32)) for q in range(2)]
            w0b = Xs.enter_context(nc.sbuf_tensor(f"w0b{l}", [128, NCH, 16], F32))
            Gs = contextlib.ExitStack()
            GI = Gs.enter_context(nc.sbuf_tensor(f"GI{l}", [40, T], F32))
            GF = Gs.enter_context(nc.sbuf_tensor(f"GF{l}", [40, T], F32))
            Ls = contextlib.ExitStack()
            with Ls:
                hT = Ls.enter_context(nc.sbuf_tensor(f"hT{l}", [128, 16, T], BF16))

                psA = Ls.enter_context(nc.psum_tensor(f"psA{l}", [128, 48, 2], F32))
                wbA = [Ls.enter_context(nc.sbuf_tensor(f"wbA{l}_{i}", [128, 16, 128], BF16)) for i in range(2)]
                actr = [0]

                def ada_chunk(la, blk):
                    s = actr[0] % 2
                    actr[0] += 1
                    S.dma('pool', wbA[s][:], w_ada[la, :, blk * 128:(blk + 1) * 128].rearrange("(k p) n -> p k n", p=128),
                          writes=[('wbA', s)])
                    for k in range(16):
                        S.op('pe', lambda e: e.matmul(psA[:, blk, :], lhsT=wbA[s][:, k, :], rhs=cT[:, k, :],
                                                      start=(k == 0), stop=(k == 15)),
                             reads=[('wbA', s)], excl=['psA'])

                def ada_finish(la):
                    S.op('dve', lambda e: e.tensor_tensor(out=adaT[:], in0=psA[:],
                                                          in1=bc(b_adaT[:, la * 48:(la + 1) * 48], 2, 2), op=ALU.add),
                         excl=['psA'], writes=['adaT'])

                pending = list(range(48)) if l + 1 < nlayers else []

                def ada_bg(n):
                    for _ in range(n):
                        if pending:
                            ada_chunk(l + 1, pending.pop(0))

                As = contextlib.ExitStack()
                with As:
                    if l == 0:
                        for blk in range(48):
                            ada_chunk(0, blk)
                        ada_finish(0)
                    S.op('dve', lambda e: e.scalar_tensor_tensor(out=s1T[:], in0=adaT[:, 16:32, :], scalar=1.0,
                                                                 in1=bc(g_preT[:, l * 16:(l + 1) * 16], 2, 2),
                                                                 op0=ALU.add, op1=ALU.mult),
                         reads=['adaT'], writes=['s1T'])
                    S.op('dve', lambda e: e.tensor_copy(out=shT[:], in_=adaT[:, 0:16, :]), reads=['adaT'], writes=['shT'])
                    for jx in range(2):
                        S.dma('sp', adaD[jx, :].rearrange("(c p) -> p c", p=128), adaT[:, 32:48, jx], reads=['adaT'],
                              writes=[('adaD', jx)], allow_slow_non_contiguous=True)
                    S.barrier()

                Bs = contextlib.ExitStack()
                with Bs:
                    xt = [Bs.enter_context(nc.sbuf_tensor(f"xt{l}_{i}", [128, D], F32)) for i in range(2)]
                    xn = [Bs.enter_context(nc.sbuf_tensor(f"xn{l}_{i}", [128, D], BF16)) for i in range(2)]
                    tmpf = [Bs.enter_context(nc.sbuf_tensor(f"tmpf{l}_{i}", [128, 8, 128], F32)) for i in range(2)]
                    xjunk = Bs.enter_context(nc.sbuf_tensor(f"xjunk{l}", [128, D], BF16))
                    stt = Bs.enter_context(nc.sbuf_tensor(f"stt{l}", [128, 3 * NT], F32))
                    psT = [Bs.enter_context(nc.psum_tensor(f"psT{l}_{i}", [128, 8, 128], BF16)) for i in range(2)]
                    def b1_x(i):
                        s = i % 2
                        S.dma('sp', xt[s][:], src[i * 128:(i + 1) * 128, :], writes=[('xt', s)])
                        S.op('act', lambda e: e.activation(out=xjunk[:], in_=xt[s][:], func=AF.Square,
                                                           accum_out=stt[:, i:i + 1]),
                             reads=[('xt', s)], writes=['xjunk', ('ss', i)])
                        S.op('dve', lambda e: e.tensor_scalar(out=stt[:, NT + i:NT + i + 1], in0=stt[:, i:i + 1],
                                                              scalar1=1.0 / D, scalar2=EPS, op0=ALU.mult, op1=ALU.add),
                             reads=[('ss', i)], writes=[('ms', i)])
                        S.op('pool', lambda e: e.tensor_tensor(out=stt[:, 2 * NT + i:2 * NT + i + 1],
                                                               in0=stt[:, NT + i:NT + i + 1], in1=neghalf[:, 0:1], op=ALU.pow),
                             reads=[('ms', i)], writes=[('rs', i)])

                    def b1_y(i):
                        s = i % 2
                        jx = 1 if i < 2 else 0
                        S.op('act', lambda e: e.activation(out=xn[s][:], in_=xt[s][:], func=AF.Copy,
                                                           scale=stt[:, 2 * NT + i:2 * NT + i + 1]),
                             reads=[('xt', s), ('rs', i)], writes=[('xn', s)])
                        for hh in range(2):
                            for k8 in range(8):
                                k = hh * 8 + k8
                                S.op('pe', lambda e: e.transpose(out=psT[hh][:, k8, :], in_=xn[s][:, k * 128:(k + 1) * 128],
                                                                 identity=ident_b[:]),
                                     reads=[('xn', s)], excl=[('psT', hh)])
                            S.op('dve', lambda e: e.tensor_tensor(out=tmpf[hh][:], in0=psT[hh][:],
                                                                  in1=bc(s1T[:, hh * 8:(hh + 1) * 8, jx], 2, 128), op=ALU.mult),
                                 reads=['s1T'], excl=[('psT', hh)], writes=[('tmpf', hh)])
                            S.op('dve', lambda e: e.tensor_tensor(out=hT[:, hh * 8:(hh + 1) * 8, i * 128:(i + 1) * 128],
                                                                  in0=tmpf[hh][:],
                                                                  in1=bc(shT[:, hh * 8:(hh + 1) * 8, jx], 2, 128), op=ALU.add),
                                 reads=[('tmpf', hh), 'shT'], writes=[('hT', i)])

                    b1_x(0)
                    for i in range(NT):
                        if i + 1 < NT:
                            b1_x(i + 1)
                        b1_y(i)
                    if debug and l == 0:
                        S.dma('sp', dbg['hT'], hT[:], reads=[('hT', i) for i in range(NT)])
                    S.barrier()
                hT_all = [('hT', i) for i in range(NT)]

                Cs = contextlib.ExitStack()
                with Cs:
                    wb = [Cs.enter_context(nc.sbuf_tensor(f"wbC{l}_{i}", [128, 16, 512], BF16)) for i in range(2)]
                    convT = Cs.enter_context(nc.sbuf_tensor(f"convT{l}", [128, 8, T], BF16))
                    upl = Cs.enter_context(nc.sbuf_tensor(f"upl{l}", [128, 64 * 64], BF16))
                    upc = Cs.enter_context(nc.sbuf_tensor(f"upc{l}", [128, NCTX + 30], BF16))
                    dg = Cs.enter_context(nc.sbuf_tensor(f"dg{l}", [128, 31, 128], BF16))
                    sig = [Cs.enter_context(nc.sbuf_tensor(f"sig{l}_{i}", [128, 512], F32)) for i in range(2)]
                    sq = upl[:, :].rearrange("p (j t) -> p j t", t=512)
                    mean = Cs.enter_context(nc.sbuf_tensor(f"mean{l}", [128, 512], F32))
                    rstd = Cs.enter_context(nc.sbuf_tensor(f"rstd{l}", [128, 512], F32))
                    t1 = sig
                    yco = [Cs.enter_context(nc.sbuf_tensor(f"yco{l}_{i}", [128, 512], BF16)) for i in range(2)]
                    psa = [Cs.enter_context(nc.psum_tensor(f"psa{l}_{i}", [128, 512], F32)) for i in range(2)]
                    psg = [Cs.enter_context(nc.psum_tensor(f"psg{l}_{i}", [128, 512], F32)) for i in range(2)]
                    psc = [Cs.enter_context(nc.psum_tensor(f"psc{l}_{i}", [128, 512], F32)) for i in range(2)]
                    S.op('pool', lambda e: e.memset(upc[:], 0.0), writes=['upc'])
                    uplh = upl[:, 0:32 * 94].rearrange("p (r c) -> p r c", c=94)
                    uplv = upl[:, 0:62 * 64].rearrange("p (r c) -> p r c", c=64)
                    cnt = 0
                    wp = wb[0][:].rearrange("p k n -> p (k n)").rearrange("p (j n) -> p j n", n=WC)

                    def load_ag(jp):
                        s = jp % 2
                        S.dma('pool', wb[s][:, :, 0:256],
                              w_in[l, :, OA + jp * 256:OA + (jp + 1) * 256].rearrange("(k p) n -> p k n", p=128),
                              writes=[('wb', s)])
                        S.dma('pool', wb[s][:, :, 256:512],
                              w_in[l, :, OG + jp * 256:OG + (jp + 1) * 256].rearrange("(k p) n -> p k n", p=128),
                              reads=[('wb', s)], writes=[('wb', s)])

                    load_ag(0)
                    for jp in range(4):
                        s = jp % 2
                        if jp + 1 < 4:
                            load_ag(jp + 1)
                        else:
                            S.dma('pool', wp, w_pw2[l].rearrange("(j p) n -> p j n", p=128),
                                  reads=[('wb', 0)], writes=[('wb', 0)])
                        for jj in range(2):
                            j = 2 * jp + jj
                            horiz = j < 4
                            if j == 0 or j == 4:
                                S.op('pool', lambda e: e.memset(upl[:], 0.0), writes=['upl'])
                            S.op('dve', lambda e: e.tensor_tensor(out=dg[:], in0=bc(ident_b[:], 1, 31),
                                                                  in1=bc(w_dwT[:, l * 8 + j, :], 2, 128), op=ALU.mult),
                                 writes=['dg'])
                            for n, (t0, tn) in enumerate(TG):
                                if last and n == 0:
                                    continue
                                b = cnt % 2
                                cnt += 1
                                for k in range(16):
                                    S.op('pe', lambda e: e.matmul(psa[b][:, 0:tn], lhsT=wb[s][:, k, jj * 128:(jj + 1) * 128],
                                                                  rhs=hT[:, k, t0:t0 + tn], start=(k == 0), stop=(k == 15)),
                                         reads=[('wb', s)], excl=[('psa', b)])
                                for k in range(16):
                                    S.op('pe', lambda e: e.matmul(psg[b][:, 0:tn],
                                                                  lhsT=wb[s][:, k, 256 + jj * 128:256 + (jj + 1) * 128],
                                                                  rhs=hT[:, k, t0:t0 + tn], start=(k == 0), stop=(k == 15)),
                                         reads=[('wb', s)], excl=[('psg', b)])
                                S.op('act', lambda e: e.activation(out=sig[b][:, 0:tn], in_=psg[b][:, 0:tn], func=AF.Sigmoid),
                                     excl=[('psg', b)], writes=[('sig', b)])
                                if n == 0:
                                    uo = upc[:, 15:15 + NCTX]
                                    ui = psa[b][:, 0:tn]
                                    si = sig[b][:, 0:tn]
                                    ukey = 'upc'
                                elif horiz:
                                    r0 = 8 * (n - 1)
                                    uo = uplh[:, r0:r0 + 8, 15:79]
                                    ui = psa[b][:, :].rearrange("p (r c) -> p r c", c=64)
                                    si = sig[b][:, :].rearrange("p (r c) -> p r c", c=64)
                                    ukey = 'upl'
                                else:
                                    r0 = 15 + 8 * (n - 1)
                                    uo = uplv[:, r0:r0 + 8, :]
                                    ui = psa[b][:, :].rearrange("p (r c) -> p r c", c=64)
                                    si = sig[b][:, :].rearrange("p (r c) -> p r c", c=64)
                                    ukey = 'upl'
                                S.op('dve', lambda e: e.tensor_tensor(out=uo, in0=ui, in1=si, op=ALU.mult),
                                     reads=[('sig', b), ukey], excl=[('psa', b)], writes=[ukey])
                            for n, (t0, tn) in enumerate(TG):
                                if last and n == 0:
                                    continue
                                b = cnt % 2
                                cnt += 1
                                for k in range(31):
                                    if n == 0:
                                        win = upc[:, k:k + NCTX]
                                        po = psc[b][:, 0:tn]
                                        ukey = 'upc'
                                    elif horiz:
                                        r0 = 8 * (n - 1)
                                        win = uplh[:, r0:r0 + 8, k:k + 64]
                                        po = psc[b][:, :].rearrange("p (r c) -> p r c", c=64)
                                        ukey = 'upl'
                                    else:
                                        r0 = 8 * (n - 1) + k
                                        win = uplv[:, r0:r0 + 8, :]
                                        po = psc[b][:, :].rearrange("p (r c) -> p r c", c=64)
                                        ukey = 'upl'
                                    S.op('pe', lambda e: e.matmul(po, lhsT=dg[:, k, :], rhs=win, start=(k == 0), stop=(k == 30)),
                                         reads=['dg', ukey], excl=[('psc', b)])
                                S.op('act', lambda e: e.activation(out=convT[:, j, t0:t0 + tn], in_=psc[b][:, 0:tn],
                                                                   func=AF.Identity, bias=b_dwT[:, l * 8 + j:l * 8 + j + 1]),
                                     excl=[('psc', b)], writes=[('cv', j, n)])
                                ada_bg(1)
                    if debug and l == 0:
                        S.dma('sp', dbg['convT'], convT[:], reads=[('cv', j, n) for j in range(8) for n in range(5)])
                    S.dma('pool', wb[1][:], w_in[l, :, OZ:OZ + 512].rearrange("(k p) n -> p k n", p=128),
                          reads=[('wb', 1)], writes=[('wb', 1)])
                    for n, (t0, tn) in enumerate(TG):
                        if last and n == 0:
                            continue
                        S.op('act', lambda e: e.activation(out=sq[:, :, 0:tn], in_=convT[:, :, t0:t0 + tn], func=AF.Square),
                             reads=[('cv', j, n) for j in range(8)] + ['upl'], writes=['sq', 'upl'])
                        for j in range(8):
                            S.op('pe', lambda e: e.matmul(psa[0][:, 0:tn], lhsT=ones_b[:], rhs=convT[:, j, t0:t0 + tn],
                                                          start=(j == 0), stop=(j == 7)),
                                 reads=[('cv', j, n)], excl=[('psa', 0)])
                        for j in range(8):
                            S.op('pe', lambda e: e.matmul(psg[0][:, 0:tn], lhsT=ones_b[:], rhs=sq[:, j, 0:tn],
                                                          start=(j == 0), stop=(j == 7)),
                                 reads=['sq'], excl=[('psg', 0)])
                        S.op('act', lambda e: e.activation(out=mean[:, 0:tn], in_=psa[0][:, 0:tn], func=AF.Copy, scale=1.0 / WC),
                             excl=[('psa', 0)], writes=['mean'])
                        S.op('dve', lambda e: e.tensor_tensor(out=t1[0][:, 0:tn], in0=mean[:, 0:tn], in1=mean[:, 0:tn], op=ALU.mult),
                             reads=['mean'], writes=[('sig', 0)])
                        S.op('dve', lambda e: e.scalar_tensor_tensor(out=t1[1][:, 0:tn], in0=psg[0][:, 0:tn], scalar=1.0 / WC,
                                                                     in1=t1[0][:, 0:tn], op0=ALU.mult, op1=ALU.subtract),
                             reads=[('sig', 0)], excl=[('psg', 0)], writes=[('sig', 1)])
                        S.op('dve', lambda e: e.tensor_scalar(out=t1[1][:, 0:tn], in0=t1[1][:, 0:tn], scalar1=EPS, scalar2=None,
                                                              op0=ALU.add),
                             reads=[('sig', 1)], writes=[('sig', 1)])
                        S.op('act', lambda e: e.activation(out=t1[1][:, 0:tn], in_=t1[1][:, 0:tn], func=AF.Sqrt),
                             reads=[('sig', 1)], writes=[('sig', 1)])
                        S.op('dve', lambda e: e.reciprocal(out=rstd[:, 0:tn], in_=t1[1][:, 0:tn]),
                             reads=[('sig', 1)], writes=['rstd'])
                        for j in range(8):
                            b = j % 2
                            S.op('dve', lambda e: e.tensor_tensor(out=t1[b][:, 0:tn], in0=convT[:, j, t0:t0 + tn],
                                                                  in1=mean[:, 0:tn], op=ALU.subtract),
                                 reads=[('cv', j, n), 'mean'], writes=[('sig', b)])
                            S.op('dve', lambda e: e.tensor_tensor(out=t1[b][:, 0:tn], in0=t1[b][:, 0:tn], in1=rstd[:, 0:tn], op=ALU.mult),
                                 reads=[('sig', b), 'rstd'], writes=[('sig', b)])
                            S.op('act', lambda e: e.activation(out=convT[:, j, t0:t0 + tn], in_=t1[b][:, 0:tn], func=AF.Silu,
                                                               scale=ln_gT[:, l * 8 + j:l * 8 + j + 1],
                                                               bias=ln_bT[:, l * 8 + j:l * 8 + j + 1]),
                                 reads=[('sig', b)], writes=[('cv', j, n)])
                    for zh in range(2):
                        if zh == 1:
                            S.dma('pool', wb[1][:], w_in[l, :, OZ + 512:OZ + 1024].rearrange("(k p) n -> p k n", p=128),
                                  reads=[('wb', 1)], writes=[('wb', 1)])
                        for mm in range(4):
                            m = zh * 4 + mm
                            for n, (t0, tn) in enumerate(TG):
                                if last and n == 0:
                                    continue
                                b = cnt % 2
                                cnt += 1
                                for j in range(8):
                                    S.op('pe', lambda e: e.matmul(psa[b][:, 0:tn], lhsT=wp[:, j, m * 128:(m + 1) * 128],
                                                                  rhs=convT[:, j, t0:t0 + tn], start=(j == 0), stop=(j == 7)),
                                         reads=[('wb', 0), ('cv', j, n)], excl=[('psa', b)])
                                for k in range(16):
                                    S.op('pe', lambda e: e.matmul(psg[b][:, 0:tn], lhsT=wb[1][:, k, mm * 128:(mm + 1) * 128],
                                                                  rhs=hT[:, k, t0:t0 + tn], start=(k == 0), stop=(k == 15)),
                                         reads=[('wb', 1)], excl=[('psg', b)])
                                S.op('act', lambda e: e.activation(out=sig[b][:, 0:tn], in_=psg[b][:, 0:tn], func=AF.Silu),
                                     excl=[('psg', b)], writes=[('sig', b)])
                                S.op('dve', lambda e: e.tensor_tensor(out=yco[b][:, 0:tn], in0=psa[b][:, 0:tn], in1=sig[b][:, 0:tn],
                                                                      op=ALU.mult),
                                     reads=[('sig', b)], excl=[('psa', b)], writes=[('yco', b)])
                                S.dma('sp', ycT[m * 128:(m + 1) * 128, t0:t0 + tn], yco[b][:, 0:tn], reads=[('yco', b)],
                                      writes=[('ycT', m, n)])
                            ada_bg(1)
                    S.barrier()

                Ds = contextlib.ExitStack()
                with Ds:
                    wb = [Ds.enter_context(nc.sbuf_tensor(f"wbD{l}_{i}", [128, 16, 512], BF16)) for i in range(2)]
                    wg = Ds.enter_context(nc.sbuf_tensor(f"wg{l}", [128, 16, 2, 40], BF16))
                    ev = [Ds.enter_context(nc.sbuf_tensor(f"ev{l}_{i}", [128, 512], BF16)) for i in range(4)]
                    psd = [Ds.enter_context(nc.psum_tensor(f"psd{l}_{i}", [128, 512], F32)) for i in range(4)]
                    psk = Ds.enter_context(nc.psum_tensor(f"psk{l}", [128, 4, 128], BF16))
                    ev2 = [Ds.enter_context(nc.sbuf_tensor(f"ev2{l}_{i}", [128, 4, 128], BF16)) for i in range(2)]
                    kcnt = [0]
                    cnt = 0
                    gcnt = 0
                    dgroups = [(off, half) for off in (OQ, OK_) for half in range(2)] + \
                              [(off, half) for off in (OV, OO, OZM) for half in range(2)]

                    def load_dg(gi):
                        off, half = dgroups[gi]
                        S.dma('pool', wb[gi % 2][:], w_in[l, :, off + half * 512:off + (half + 1) * 512].rearrange("(k p) n -> p k n", p=128),
                              writes=[('wb', gi % 2)])

                    load_dg(0)
                    for (off, dst, scl) in ((OQ, qT_d, DH ** -0.5), (OK_, kT_d, 1.0)):
                        for half in range(2):
                            s = gcnt % 2
                            gcnt += 1
                            if gcnt < len(dgroups):
                                load_dg(gcnt)
                            for hb in range(4):
                                hd = half * 4 + hb
                                for n, (t0, tn) in enumerate(TG):
                                    b = cnt % 4
                                    cnt += 1
                                    for k in range(16):
                                        S.op('pe', lambda e: e.matmul(psd[b][:, 0:tn], lhsT=wb[s][:, k, hb * 128:(hb + 1) * 128],
                                                                      rhs=hT[:, k, t0:t0 + tn], start=(k == 0), stop=(k == 15)),
                                             reads=[('wb', s)], excl=[('psd', b)])
                                    if b % 2 == 0:
                                        S.op('act', lambda e: e.activation(out=ev[b][:, 0:tn], in_=psd[b][:, 0:tn], func=AF.Copy, scale=scl),
                                             excl=[('psd', b)], writes=[('ev', b)])
                                    else:
                                        S.op('dve', lambda e: e.tensor_scalar(out=ev[b][:, 0:tn], in0=psd[b][:, 0:tn], scalar1=scl,
                                                                              scalar2=None, op0=ALU.mult),
                                             excl=[('psd', b)], writes=[('ev', b)])
                                    S.dma('sp', dst[hd, :, t0:t0 + tn], ev[b][:, 0:tn], reads=[('ev', b)], writes=[('qk', off, hd, n)])
                                    if off == OK_:
                                        nsub = tn // 128
                                        for sb_ in range(nsub):
                                            S.op('pe', lambda e: e.transpose(out=psk[:, sb_, :], in_=ev[b][:, sb_ * 128:(sb_ + 1) * 128],
                                                                             identity=ident_b[:]),
                                                 reads=[('ev', b)], excl=['psk'])
                                        kb2 = kcnt[0] % 2
                                        kcnt[0] += 1
                                        S.op('act', lambda e: e.activation(out=ev2[kb2][:, 0:nsub, :], in_=psk[:, 0:nsub, :], func=AF.Copy),
                                             excl=['psk'], writes=[('ev2', kb2)])
                                        S.dma('sp', ktm[t0:t0 + tn, hd * 128:(hd + 1) * 128].rearrange("(s p) d -> p s d", p=128),
                                              ev2[kb2][:, 0:nsub, :], reads=[('ev2', kb2)], writes=[('ktm', hd, n)])
                    for (off, dst) in ((OV, vtm), (OO, otm), (OZM, zmtm)):
                        for half in range(2):
                            s = gcnt % 2
                            gcnt += 1
                            if gcnt < len(dgroups):
                                load_dg(gcnt)
                            for i in range(NT):
                                b = cnt % 4
                                cnt += 1
                                for k in range(16):
                                    S.op('pe', lambda e: e.matmul(psd[b][:, :], lhsT=hT[:, k, i * 128:(i + 1) * 128],
                                                                  rhs=wb[s][:, k, :], start=(k == 0), stop=(k == 15)),
                                         reads=[('wb', s)], excl=[('psd', b)])
                                if b % 2 == 0:
                                    S.op('act', lambda e: e.activation(out=ev[b][:], in_=psd[b][:], func=AF.Copy),
                                         excl=[('psd', b)], writes=[('ev', b)])
                                else:
                                    S.op('dve', lambda e: e.tensor_copy(out=ev[b][:], in_=psd[b][:]),
                                         excl=[('psd', b)], writes=[('ev', b)])
                                S.dma('sp', dst[i * 128:(i + 1) * 128, half * 512:(half + 1) * 512], ev[b][:], reads=[('ev', b)],
                                      writes=[('tm', off, i, half)])
                    S.op('pool', lambda e: e.memset(wg[:], 0.0), writes=['wg'])
                    for (gi, r0, c0) in ((0, 0, 0), (1, 0, 8), (0, 32, 16), (1, 32, 24)):
                        S.dma('pool', wg[:, :, gi, r0:r0 + 8],
                              w_in[l, :, OGT + c0:OGT + c0 + 8].rearrange("(k p) n -> p k n", p=128),
                              reads=['wg'], writes=['wg'], allow_slow_non_contiguous=True)
                    for n, (t0, tn) in enumerate(TG):
                        for gi in range(2):
                            b = cnt % 4
                            cnt += 1
                            for k in range(16):
                                S.op('pe', lambda e: e.matmul(psd[b][0:40, 0:tn], lhsT=wg[:, k, gi, :], rhs=hT[:, k, t0:t0 + tn],
                                                              start=(k == 0), stop=(k == 15)),
                                     reads=['wg'], excl=[('psd', b)])
                            if gi == 0:
                                S.op('act', lambda e: e.activation(out=GI[:, t0:t0 + tn], in_=psd[b][0:40, 0:tn], func=AF.Identity,
                                                                   bias=bgI[:, l:l + 1]),
                                     excl=[('psd', b)], writes=[('GI', n)])
                            else:
                                S.op('act', lambda e: e.activation(out=GF[:, t0:t0 + tn], in_=psd[b][0:40, 0:tn], func=AF.Exp,
                                                                   scale=-1.0, bias=nbgF[:, l:l + 1]),
                                     excl=[('psd', b)], writes=[('GF', n)])
                    if l + 1 < nlayers:
                        ada_bg(48)
                        ada_finish(l + 1)
                    S.barrier()
            Eps = contextlib.ExitStack()
            with Eps:
                def ET(name, shape, dt):
                    return Eps.enter_context(nc.sbuf_tensor(f"{name}{l}", shape, dt))
                PRE = ET("PRE", [40, T], F32)
                scanmask = ET("scanmask", [40, T], F32)
                S.op('pool', lambda e: e.memset(scanmask[:], 1.0), writes=['scanmask'])
                smv = scanmask[:].rearrange("p (c t) -> p c t", t=CH)
                S.op('pool', lambda e: e.memset(smv[:, :, 0:1], 0.0), reads=['scanmask'], writes=['scanmask'])
                cl = ET("cl", [40, 8, NCH], F32)
                w0x = ET("w0x", [40, NCH, 16], F32)
                psE = [Eps.enter_context(nc.psum_tensor(f"psE{l}_{i}", [128, 512], F32)) for i in range(3)]
                allG = [('GI', n) for n in range(5)] + [('GF', n) for n in range(5)]
                S.op('act', lambda e: e.activation(out=GF[:], in_=GF[:], func=AF.Ln, bias=1.0), writes=['GF'])
                S.op('dve', lambda e: e.tensor_scalar(out=GF[:], in0=GF[:], scalar1=-1.0, scalar2=None, op0=ALU.mult),
                     reads=['GF'], writes=['GF'])
                if debug and l == 0:
                    S.dma('sp', dbg['gates'][0], GI[:], reads=['GF'])
                    S.dma('sp', dbg['gates'][1], GF[:], reads=['GF'])
                S.op('dve', lambda e: e.tensor_tensor_scan(out=PRE[:], data0=scanmask[:], data1=GF[:], initial=0.0,
                                                           op0=ALU.mult, op1=ALU.add),
                     reads=['GF', 'scanmask'], writes=['PRE'])
                PREv = PRE[:].rearrange("p (c t) -> p c t", t=CH)
                GFv = GF[:].rearrange("p (c t) -> p c t", t=CH)
                GIv = GI[:].rearrange("p (c t) -> p c t", t=CH)
                S.op('dve', lambda e: e.tensor_copy(out=cl[:, 0, :], in_=PREv[:, :, CH - 1]), reads=['PRE'], writes=['cl0'])
                S.op('dve', lambda e: e.tensor_tensor(out=GF[32:40, :], in0=GF[32:40, :], in1=PRE[32:40, :], op=ALU.subtract),
                     reads=['GF', 'PRE'], writes=['GF'])
                S.op('dve', lambda e: e.tensor_tensor(out=GFv[32:40], in0=GFv[32:40], in1=bc(cl[32:40, 0, :], 2, CH), op=ALU.add),
                     reads=['GF', 'cl0'], writes=['GF'])
                S.op('dve', lambda e: e.tensor_copy(out=GF[0:32, :], in_=PRE[0:32, :]), reads=['GF', 'PRE'], writes=['GF'])
                S.op('dve', lambda e: e.tensor_tensor(out=GI[:], in0=GI[:], in1=GF[:], op=ALU.subtract), reads=['GF'], writes=['GI'])
                S.op('dve', lambda e: e.tensor_reduce(out=cl[:, 1, :], in_=GIv, axis=AX.X, op=ALU.max), reads=['GI'], writes=['cl1'])
                S.op('dve', lambda e: e.tensor_tensor(out=cl[:, 2, :], in0=cl[:, 0, :], in1=cl[:, 1, :], op=ALU.add),
                     reads=['cl0', 'cl1'], writes=['cl2'])

                def rev_ap(ap2, lo, n):
                    a = ap2[:, lo:lo + n]
                    return bass.AP(a.tensor, a.offset + (n - 1) * a.ap[1][0], [list(a.ap[0]), [-a.ap[1][0], n]])

                for (so, de) in ((0, 3), (2, 4)):
                    S.op('dve', lambda e: e.tensor_copy(out=cl[0:32, de, :], in_=cl[0:32, so, :]), reads=[f'cl{so}'], writes=[f'cl{de}'])
                    S.op('dve', lambda e: e.tensor_copy(out=cl[32:40, de, 0:NCC], in_=rev_ap(cl[32:40, so, :], 0, NCC)),
                         reads=[f'cl{so}', f'cl{de}'], writes=[f'cl{de}'])
                    S.op('dve', lambda e: e.tensor_copy(out=cl[32:40, de, NCC:NCH], in_=rev_ap(cl[32:40, so, :], NCC, NCH - NCC)),
                         reads=[f'cl{so}', f'cl{de}'], writes=[f'cl{de}'])
                S.op('dve', lambda e: e.tensor_tensor_scan(out=cl[:, 5, :], data0=cl[:, 3, :], data1=cl[:, 4, :], initial=NEG,
                                                           op0=ALU.add, op1=ALU.max),
                     reads=['cl3', 'cl4'], writes=['cl5'])
                S.op('dve', lambda e: e.tensor_tensor(out=cl[:, 6, :], in0=cl[:, 5, :], in1=cl[:, 3, :], op=ALU.subtract),
                     reads=['cl5', 'cl3'], writes=['cl6'])
                S.op('pool', lambda e: e.memset(cl[:, 7, 0:1], NEG), writes=['cl7'])
                S.op('dve', lambda e: e.tensor_copy(out=cl[:, 7, 1:NCH], in_=cl[:, 5, 0:NCH - 1]), reads=['cl5', 'cl7'], writes=['cl7'])
                S.op('dve', lambda e: e.tensor_tensor(out=cl[:, 7, :], in0=cl[:, 7, :], in1=cl[:, 6, :], op=ALU.subtract),
                     reads=['cl7', 'cl6'], writes=['cl7'])
                S.op('act', lambda e: e.activation(out=cl[:, 7, :], in_=cl[:, 7, :], func=AF.Exp), reads=['cl7'], writes=['cl7'])
                S.op('dve', lambda e: e.tensor_copy(out=cl[0:32, 4, :], in_=cl[0:32, 6, :]), reads=['cl6', 'cl4'], writes=['cl4'])
                S.op('dve', lambda e: e.tensor_copy(out=cl[32:40, 4, 0:NCC], in_=rev_ap(cl[32:40, 6, :], 0, NCC)),
                     reads=['cl6', 'cl4'], writes=['cl4'])
                S.op('dve', lambda e: e.tensor_copy(out=cl[32:40, 4, NCC:NCH], in_=rev_ap(cl[32:40, 6, :], NCC, NCH - NCC)),
                     reads=['cl6', 'cl4'], writes=['cl4'])
                S.op('dve', lambda e: e.tensor_tensor(out=GIv, in0=GIv, in1=bc(cl[:, 4, :], 2, CH), op=ALU.subtract),
                     reads=['GI', 'cl4'], writes=['GI'])
                S.op('act', lambda e: e.activation(out=GI[:], in_=GI[:], func=AF.Exp), reads=['GI'], writes=['GI'])
                S.op('dve', lambda e: e.tensor_tensor(out=GFv, in0=GFv, in1=bc(cl[:, 4, :], 2, CH), op=ALU.add),
                     reads=['GF', 'cl4'], writes=['GF'])
                S.op('act', lambda e: e.activation(out=GF[:], in_=GF[:], func=AF.Exp, scale=-1.0), reads=['GF'], writes=['GF'])
                if debug and l == 0:
                    S.dma('sp', dbg['gates'][2], GI[:], reads=['GI'])
                    S.dma('sp', dbg['gates'][3], GF[:], reads=['GF'])
                for q, (srcg, key) in enumerate(((GI, 'GI'), (GF, 'GF'))):
                    for bi, (c0, c1) in enumerate(((0, NCH),)):
                        pb = psE[bi]
                        for c in range(c0, c1):
                            S.op('pe', lambda e: e.matmul(pb[0:CH, (c - c0) * 16:(c - c0 + 1) * 16],
                                                          lhsT=srcg[0:40, c * CH:(c + 1) * CH], rhs=sel16[:, :], start=True, stop=True),
                                 reads=[key], excl=[('psE', bi)])
                        S.op('dve', lambda e: e.tensor_copy(out=etm[q][:, c0:c1, :].rearrange("p c h -> p (c h)"),
                                                            in_=pb[0:CH, 0:(c1 - c0) * 16]),
                             excl=[('psE', bi)], writes=[('etm', q, bi)])
                S.op('dve', lambda e: e.tensor_tensor(out=w0x[:], in0=bc(cl[:, 7, :], 2, 16), in1=bc(sel16[:, :], 1, NCH),
                                                      op=ALU.mult),
                     reads=['cl7'], writes=['w0x'])
                w0xf = w0x[:].rearrange("p c h -> p (c h)")
                w0bf = w0b[:].rearrange("p c h -> p (c h)")
                for i in range(1):
                    S.op('pe', lambda e: e.matmul(psE[i][:, 0:288], lhsT=ones_f[0:40, :], rhs=w0xf[:, i * 288:(i + 1) * 288],
                                                  start=True, stop=True),
                         reads=['w0x'], excl=[('psE', i)])
                    S.op('dve', lambda e: e.tensor_copy(out=w0bf[:, i * 288:(i + 1) * 288], in_=psE[i][:, 0:288]),
                         excl=[('psE', i)], writes=[('w0b', i)])
                if debug and l == 0:
                    for q in range(2):
                        S.dma('sp', dbg['etm'][q], etm[q][:].rearrange("p c h -> p (c h)"), reads=[('etm', q, bi) for bi in range(1)])
                    S.dma('sp', dbg['w0b'], w0bf, reads=[('w0b', i) for i in range(1)])
                S.barrier()
            Gs.close()
            Ws = contextlib.ExitStack()
            wo = Ws.enter_context(nc.sbuf_tensor(f"wo{l}", [128, 16, D], BF16))
            wo_pending = list(range(16))

            def wo_bg(n):
                for _ in range(n):
                    if wo_pending:
                        k = wo_pending.pop(0)
                        S.dma('pool', wo[:, k, :], w_out[l, k * 128:(k + 1) * 128, :], writes=[('wo', k)])

            Ss = contextlib.ExitStack()
            with Ss:
                def ST(name, shape, dt):
                    return Ss.enter_context(nc.sbuf_tensor(f"{name}{l}", shape, dt))
                SC = 2
                NSC = NCH // SC
                HA = DH + 1
                qb = [[ST(f"qb{d}_{i}_", [128, H, SC * CH], BF16) for i in range(2)] for d in range(2)]
                kb = [[ST(f"kb{d}_{i}_", [128, H, SC * CH], BF16) for i in range(2)] for d in range(2)]
                ktb = [[ST(f"ktb{d}_{i}_", [CH, SC, WC], BF16) for i in range(2)] for d in range(2)]
                vtb = [[ST(f"vtb{d}_{i}_", [CH, SC, H, HA], BF16) for i in range(2)] for d in range(2)]
                Cst = [ST(f"Cst{d}_", [128, H, HA], F32) for d in range(2)]
                Cb = [ST(f"Cb{d}_", [128, H, HA], BF16) for d in range(2)]
                PT = [[ST(f"PT{d}_{i}_", [CH, H, CH], BF16) for i in range(2)] for d in range(2)]
                kE = [[ST(f"kE{d}_{i}_", [CH, H, DH], BF16) for i in range(2)] for d in range(2)]
                dpos = [ST(f"dpos{d}_", [CH, H], F32) for d in range(2)]
                dden = [ST(f"dden{d}_", [CH, H], F32) for d in range(2)]
                hout = [[ST(f"hout{d}_{i}_", [CH, H, DH], F32) for i in range(2)] for d in range(2)]
                psS = [Ss.enter_context(nc.psum_tensor(f"psS{l}_{hf}", [CH, 4, CH], F32)) for hf in range(2)]
                psN = [Ss.enter_context(nc.psum_tensor(f"psN{l}_{i}", [CH, 3, HA], F32)) for i in range(3)]
                psC = [Ss.enter_context(nc.psum_tensor(f"psC{l}_{i}", [128, 3, HA], F32)) for i in range(3)]
                GH = [(0, 3), (3, 3), (6, 2)]

                for d in range(2):
                    S.op('pool', lambda e: e.memset(Cst[d][:], 0.0), writes=[('Cst', d, h) for h in range(H)])
                    S.op('pool', lambda e: e.memset(Cb[d][:], 0.0), writes=[('Cb', d, 0), ('Cb', d, 1)])
                    for i in range(2):
                        S.op('pool', lambda e: e.memset(vtb[d][i][:, :, :, DH:HA], 1.0), writes=[('vtb1', d, i)])

                def nat_chunk(d, p):
                    if d == 0:
                        return p
                    return NCC - 1 - p if p < NCC else NCH - 1 + NCC - p

                sc_base = {}

                def load_sc(d, sp_):
                    p0 = sp_ * SC
                    cs = sorted(nat_chunk(d, p0 + i) for i in range(SC))
                    c0 = cs[0]
                    assert cs == list(range(c0, c0 + SC))
                    s = sp_ % 2
                    t0 = c0 * CH
                    q_ = 'sp' if d == 0 else 'pool'
                    S.dma(q_, qb[d][s][:], qT_d[:, :, t0:t0 + SC * CH].rearrange("h p t -> p h t"), writes=[('qb', d, s)])
                    S.dma(q_, kb[d][s][:], kT_d[:, :, t0:t0 + SC * CH].rearrange("h p t -> p h t"), writes=[('kb', d, s)])
                    S.dma(q_, ktb[d][s][:], ktm[t0:t0 + SC * CH, :].rearrange("(c s) e -> s c e", s=CH), writes=[('ktb', d, s)])
                    for ci in range(SC):
                        S.dma(q_, vtb[d][s][:, ci, :, 0:DH],
                              vtm[t0 + ci * CH:t0 + (ci + 1) * CH, :].rearrange("s (h e) -> s h e", e=DH),
                              reads=[('vtb1', d, s)], writes=[('vtb', d, s, ci)])
                    sc_base[(d, sp_)] = c0

                def stage1(p, d):
                    sp_ = p // SC
                    s = sp_ % 2
                    par = p % 2
                    c = nat_chunk(d, p)
                    ci = c - sc_base[(d, sp_)]
                    r0 = d * 8
                    tsl = slice(ci * CH, (ci + 1) * CH)
                    Ecol = etm[0][:, c, r0:r0 + 8]
                    S.op('pool', lambda e: e.tensor_tensor(out=kE[d][par][:], in0=ktb[d][s][:, ci, :].rearrange("p (h e) -> p h e", e=DH),
                                                           in1=bc(Ecol, 2, DH), op=ALU.mult),
                         reads=[('ktb', d, s)], writes=[('kE', d, par)])
                    for hf in range(2):
                        for hh in range(4):
                            h = hf * 4 + hh
                            S.op('pe', lambda e: e.matmul(psS[hf][:, hh, :], lhsT=kb[d][s][:, h, tsl], rhs=qb[d][s][:, h, tsl],
                                                          start=True, stop=True),
                                 reads=[('kb', d, s), ('qb', d, s)], excl=[('psS', hf)])
                        for hh in range(4):
                            h = hf * 4 + hh
                            S.op('dve', lambda e: e.scalar_tensor_tensor(out=PT[d][par][:, h, :], in0=psS[hf][:, hh, :],
                                                                         scalar=Ecol[:, h:h + 1], in1=mask2[:, d, :],
                                                                         op0=ALU.mult, op1=ALU.mult),
                                 excl=[('psS', hf)], writes=[('PT', d, par, h)])

                def stage2(p, d):
                    sp_ = p // SC
                    s = sp_ % 2
                    par = p % 2
                    c = nat_chunk(d, p)
                    ci = c - sc_base[(d, sp_)]
                    r0 = d * 8
                    tsl = slice(ci * CH, (ci + 1) * CH)
                    Fcol = etm[1][:, c, r0:r0 + 8]
                    for h in range(H):
                        g, hh = h // 3, h % 3
                        vh = vtb[d][s][:, ci, h, :]
                        S.op('pe', lambda e: e.matmul(psN[g][:, hh, :], lhsT=qb[d][s][:, h, tsl], rhs=Cb[d][:, h, :],
                                                      start=True, stop=False),
                             reads=[('qb', d, s), ('Cb', d, h // 4)], excl=[('psN', g)])
                        S.op('pe', lambda e: e.matmul(psN[g][:, hh, :], lhsT=PT[d][par][:, h, :], rhs=vh, start=False, stop=True),
                             reads=[('PT', d, par, h), ('vtb', d, s, ci), ('vtb1', d, s)], excl=[('psN', g)])
                    last_step = (p + 1 >= NCH)
                    if not last_step:
                        for h in range(H):
                            g, hh = h // 3, h % 3
                            vh = vtb[d][s][:, ci, h, :]
                            S.op('pe', lambda e: e.matmul(psC[g][:, hh, :], lhsT=kE[d][par][:, h, :], rhs=vh, start=True, stop=True),
                                 reads=[('kE', d, par), ('vtb', d, s, ci), ('vtb1', d, s)], excl=[('psC', g)])
                    for g, (h0, nh) in enumerate(GH):
                        S.op('act', lambda e: e.activation(out=dpos[d][:, h0:h0 + nh], in_=psN[g][:, 0:nh, DH], func=AF.Copy),
                             excl=[('psN', g)], writes=[('dpos', d, g)])
                    S.op('dve', lambda e: e.tensor_scalar(out=dden[d][:], in0=dpos[d][:], scalar1=-1.0, scalar2=None, op0=ALU.mult),
                         reads=[('dpos', d, g) for g in range(3)], writes=[('dden', d)])
                    S.op('dve', lambda e: e.tensor_tensor(out=dden[d][:], in0=dden[d][:], in1=dpos[d][:], op=ALU.max),
                         reads=[('dden', d)] + [('dpos', d, g) for g in range(3)], writes=[('dden', d)])
                    S.op('dve', lambda e: e.tensor_tensor(out=dden[d][:], in0=dden[d][:], in1=Fcol, op=ALU.max),
                         reads=[('dden', d)], writes=[('dden', d)])
                    S.op('dve', lambda e: e.reciprocal(out=dden[d][:], in_=dden[d][:]), reads=[('dden', d)], writes=[('dden', d)])
                    for g in range(2):
                        S.op('dve', lambda e: e.tensor_tensor(out=hout[d][par][:, 3 * g:3 * g + 3, :], in0=psN[g][:, 0:3, 0:DH],
                                                              in1=bc(dden[d][:, 3 * g:3 * g + 3], 2, DH), op=ALU.mult),
                             reads=[('dden', d)], excl=[('psN', g)], writes=[('hout', d, par, h) for h in range(3 * g, 3 * g + 3)])
                    for h in range(6, H):
                        g, hh = h // 3, h % 3
                        S.op('act', lambda e: e.activation(out=hout[d][par][:, h, :], in_=psN[g][:, hh, 0:DH], func=AF.Copy,
                                                           scale=dden[d][:, h:h + 1]),
                             reads=[('dden', d)], excl=[('psN', g)], writes=[('hout', d, par, h)])
                    S.dma('sp', hfb[d][c * CH:(c + 1) * CH, :], hout[d][par][:].rearrange("p h e -> p (h e)"),
                          reads=[('hout', d, par, h) for h in range(H)], writes=[('h_d', d, c)])
                    if not last_step:
                        w0c = w0b[:, p, r0:r0 + 8]
                        w0n = w0b[:, p + 1, r0:r0 + 8]
                        for h in range(H):
                            g, hh = h // 3, h % 3
                            S.op('dve', lambda e: e.scalar_tensor_tensor(out=Cst[d][:, h, :], in0=Cst[d][:, h, :], scalar=w0c[:, h:h + 1],
                                                                         in1=psC[g][:, hh, :], op0=ALU.mult, op1=ALU.add),
                                 reads=[('Cst', d, h)], excl=[('psC', g)], writes=[('Cst', d, h)])
                        S.op('dve', lambda e: e.tensor_tensor(out=Cb[d][:, 0:4, :], in0=Cst[d][:, 0:4, :], in1=bc(w0n[:, 0:4], 2, HA), op=ALU.mult),
                             reads=[('Cst', d, h) for h in range(4)], writes=[('Cb', d, 0)])
                        for h in range(4, H):
                            S.op('act', lambda e: e.activation(out=Cb[d][:, h, :], in_=Cst[d][:, h, :], func=AF.Copy, scale=w0n[:, h:h + 1]),
                                 reads=[('Cst', d, h)], writes=[('Cb', d, 1)])

                for d in range(2):
                    load_sc(d, 0)
                for d in range(2):
                    stage1(0, d)
                for p in range(NCH):
                    if p % SC == 0 and p // SC + 1 < NSC:
                        for d in range(2):
                            load_sc(d, p // SC + 1)
                    wo_bg(1)
                    if p + 1 < NCH:
                        for d in range(2):
                            stage1(p + 1, d)
                    for d in range(2):
                        stage2(p, d)
                S.barrier()
            Fs = contextlib.ExitStack()
            with Fs:
                def FT(name, shape, dt):
                    return Fs.enter_context(nc.sbuf_tensor(f"{name}{l}", shape, dt))
                gtile = [FT("gtx", [128, D], F32), FT("gtc", [128, D], F32)]
                ghb = FT("ghb", [128, WC], F32)
                NB3 = 3
                hfs = [FT(f"hfs{i}_", [128, H, DH], F32) for i in range(NB3)]
                hbs = [FT(f"hbs{i}_", [128, H, DH], F32) for i in range(NB3)]
                ot = [FT(f"ot{i}_", [128, WC], BF16) for i in range(NB3)]
                zt = [FT(f"zt{i}_", [128, WC], BF16) for i in range(NB3)]
                sgo_ = [FT(f"sgo{i}_", [128, WC], F32) for i in range(2)]
                szm_ = [FT(f"szm{i}_", [128, WC], BF16) for i in range(2)]
                hsq = FT("hsq", [128, H, DH], BF16)
                st8_ = [FT(f"st8{i}_", [128, 7, H], F32) for i in range(2)]
                ym_ = [FT(f"ym{i}_", [128, WC], BF16) for i in range(2)]
                yT = [FT(f"yT{i}_", [128, 16, 128], BF16) for i in range(NB3)]
                xr = [FT(f"xr{i}_", [128, D], F32) for i in range(NB3)]
                xo = [FT(f"xo{i}_", [128, D], F32) for i in range(2)]
                st1 = FT("st1", [128, 8], F32)
                sqj = FT("sqj", [128, 512], BF16)
                pso = Fs.enter_context(nc.psum_tensor(f"pso{l}", [128, D], F32))
                psy = [Fs.enter_context(nc.psum_tensor(f"psy{l}_{i}", [128, 4, 128], BF16)) for i in range(2)]
                wo_bg(16)
                gpb = xr[0]
                S.dma('sp', gpb[:], g_post[l:l + 1, :].to_broadcast([128, D]), writes=[('xr', 0)])
                S.dma('sp', ghb[:], g_head[l:l + 1, :].to_broadcast([128, WC]), writes=['ghb'])
                for jx in range(2):
                    S.dma('sp', gtile[jx][:], adaD[jx:jx + 1, :].to_broadcast([128, D]), writes=[('gt', jx)])
                    S.op('dve', lambda e: e.tensor_tensor(out=gtile[jx][:], in0=gtile[jx][:], in1=gpb[:], op=ALU.mult),
                         reads=[('xr', 0), ('gt', jx)], writes=[('gt', jx)])
                tiles = list(range(NT)) if not last else list(range(2, NT))

                def stageA(it):
                    i = tiles[it]
                    s = it % NB3
                    s2 = it % 2
                    sgo, szm, st8, ym = sgo_[s2], szm_[s2], st8_[s2], ym_[s2]
                    k2 = lambda n: (n, s2)
                    rs_ = slice(i * 128, (i + 1) * 128)
                    S.dma('sp', hfs[s][:].rearrange("p h e -> p (h e)"), hfb[0][rs_, :], writes=[('hfs', s)])
                    S.dma('sp', hbs[s][:].rearrange("p h e -> p (h e)"), hfb[1][rs_, :], writes=[('hbs', s)])
                    S.dma('sp', ot[s][:], otm[rs_, :], writes=[('ot', s)])
                    S.dma('sp', zt[s][:], zmtm[rs_, :], writes=[('zt', s)])
                    S.dma('sp', xr[s][:], src[rs_, :], writes=[('xr', s)])
                    S.dma('sp', yT[s][:, 0:8, :], ycT[:, rs_].rearrange("(j p) t -> p j t", p=128), writes=[('yTc', s)])
                    S.op('act', lambda e: e.activation(out=sgo[:], in_=ot[s][:], func=AF.Sigmoid), reads=[('ot', s)], writes=[k2('sgo')])
                    S.op('act', lambda e: e.activation(out=szm[:], in_=zt[s][:], func=AF.Silu), reads=[('zt', s)], writes=[k2('szm')])
                    S.op('pool', lambda e: e.tensor_tensor(out=hfs[s][:], in0=hfs[s][:], in1=hbs[s][:], op=ALU.add),
                         reads=[('hfs', s), ('hbs', s)], writes=[('hfs', s)])
                    S.op('pool', lambda e: e.tensor_tensor(out=sgo[:], in0=sgo[:], in1=ghb[:], op=ALU.mult),
                         reads=[k2('sgo'), 'ghb'], writes=[k2('sgo')])
                    S.op('pool', lambda e: e.tensor_tensor(out=sgo[:], in0=sgo[:], in1=szm[:], op=ALU.mult),
                         reads=[k2('sgo'), k2('szm')], writes=[k2('sgo')])
                    yield
                    S.op('dve', lambda e: e.tensor_reduce(out=st8[:, 0, :], in_=hfs[s][:], axis=AX.X, op=ALU.add),
                         reads=[('hfs', s)], writes=[k2('st8_0')])
                    S.op('act', lambda e: e.activation(out=hsq[:], in_=hfs[s][:], func=AF.Square), reads=[('hfs', s)], writes=['hsq'])
                    S.op('dve', lambda e: e.tensor_reduce(out=st8[:, 1, :], in_=hsq[:], axis=AX.X, op=ALU.add),
                         reads=['hsq'], writes=[k2('st8_1')])
                    S.op('dve', lambda e: e.tensor_scalar(out=st8[:, 2, :], in0=st8[:, 0, :], scalar1=1.0 / DH, scalar2=None, op0=ALU.mult),
                         reads=[k2('st8_0')], writes=[k2('st8_2')])
                    S.op('dve', lambda e: e.tensor_tensor(out=st8[:, 3, :], in0=st8[:, 2, :], in1=st8[:, 2, :], op=ALU.mult),
                         reads=[k2('st8_2')], writes=[k2('st8_3')])
                    S.op('dve', lambda e: e.scalar_tensor_tensor(out=st8[:, 4, :], in0=st8[:, 1, :], scalar=1.0 / DH, in1=st8[:, 3, :],
                                                                 op0=ALU.mult, op1=ALU.subtract),
                         reads=[k2('st8_1'), k2('st8_3')], writes=[k2('st8_4')])
                    S.op('dve', lambda e: e.tensor_scalar(out=st8[:, 4, :], in0=st8[:, 4, :], scalar1=EPS, scalar2=None, op0=ALU.add),
                         reads=[k2('st8_4')], writes=[k2('st8_4')])
                    S.op('pool', lambda e: e.tensor_tensor(out=st8[:, 5, :], in0=st8[:, 4, :], in1=neghalf[:, 0:H], op=ALU.pow),
                         reads=[k2('st8_4')], writes=[k2('st8_5')])
                    yield
                    S.op('dve', lambda e: e.tensor_tensor(out=hfs[s][:], in0=hfs[s][:], in1=bc(st8[:, 2, :], 2, DH), op=ALU.subtract),
                         reads=[('hfs', s), k2('st8_2')], writes=[('hfs', s)])
                    S.op('dve', lambda e: e.tensor_tensor(out=hfs[s][:], in0=hfs[s][:], in1=bc(st8[:, 5, :], 2, DH), op=ALU.mult),
                         reads=[('hfs', s), k2('st8_5')], writes=[('hfs', s)])
                    yield
                    hflat = hfs[s][:].rearrange("p h e -> p (h e)")
                    S.op('dve', lambda e: e.tensor_tensor(out=ym[:], in0=hflat, in1=sgo[:], op=ALU.mult),
                         reads=[('hfs', s), k2('sgo')], writes=[k2('ym')])
                    for half in range(2):
                        for jj in range(4):
                            j = half * 4 + jj
                            S.op('pe', lambda e: e.transpose(out=psy[half][:, jj, :], in_=ym[:, j * 128:(j + 1) * 128], identity=ident_b[:]),
                                 reads=[k2('ym')], excl=[('psy', half)])
                        if half == 0:
                            S.op('act', lambda e: e.activation(out=yT[s][:, 8:12, :], in_=psy[0][:], func=AF.Copy),
                                 excl=[('psy', 0)], writes=[('yTm', s, 0)])
                        else:
                            S.op('dve', lambda e: e.tensor_copy(out=yT[s][:, 12:16, :], in_=psy[1][:]),
                                 excl=[('psy', 1)], writes=[('yTm', s, 1)])

                def stageB_pe(it):
                    s = it % NB3
                    for nn in range(4):
                        for k in range(16):
                            rk = [('yTc', s)] if k < 8 else [('yTm', s, (k - 8) // 4)]
                            S.op('pe', lambda e: e.matmul(pso[:, nn * 512:(nn + 1) * 512], lhsT=yT[s][:, k, :],
                                                          rhs=wo[:, k, nn * 512:(nn + 1) * 512], start=(k == 0), stop=(k == 15)),
                                 reads=rk + [('wo', k)], excl=[('pso', nn)])

                def stageB_ep(it, agen=None):
                    i = tiles[it]
                    s = it % NB3
                    sx = it % 2
                    jx = 1 if i < 2 else 0
                    rs_ = slice(i * 128, (i + 1) * 128)
                    for nn in range(4):
                        if agen is not None:
                            next(agen, None)
                        cs_ = slice(nn * 512, (nn + 1) * 512)
                        S.op('act', lambda e: e.activation(out=sqj[:], in_=pso[:, cs_], func=AF.Square, accum_out=st1[:, nn:nn + 1]),
                             excl=[('pso', nn)], writes=['sqj', ('st1', nn)])
                        S.op('dve', lambda e: e.tensor_tensor(out=xo[sx][:, cs_], in0=pso[:, cs_], in1=gtile[jx][:, cs_], op=ALU.mult),
                             reads=[('gt', jx)], excl=[('pso', nn)], writes=[('xo', sx, nn)])
                    S.op('dve', lambda e: e.tensor_reduce(out=st1[:, 4:5], in_=st1[:, 0:4], axis=AX.X, op=ALU.add),
                         reads=[('st1', nn) for nn in range(4)], writes=['st1_s'])
                    S.op('dve', lambda e: e.tensor_scalar(out=st1[:, 5:6], in0=st1[:, 4:5], scalar1=1.0 / D, scalar2=EPS,
                                                          op0=ALU.mult, op1=ALU.add),
                         reads=['st1_s'], writes=['st1_1'])
                    S.op('pool', lambda e: e.tensor_tensor(out=st1[:, 6:7], in0=st1[:, 5:6], in1=neghalf[:, 0:1], op=ALU.pow),
                         reads=['st1_1'], writes=['st1_2'])
                    S.op('dve', lambda e: e.scalar_tensor_tensor(out=xo[sx][:], in0=xo[sx][:], scalar=st1[:, 6:7], in1=xr[s][:],
                                                                 op0=ALU.mult, op1=ALU.add),
                         reads=[('xo', sx, nn) for nn in range(4)] + ['st1_2', ('xr', s)], writes=[('xo', sx, nn) for nn in range(4)])
                    S.dma('pool', xout[rs_, :], xo[sx][:], reads=[('xo', sx, nn) for nn in range(4)], writes=[('xout', i)])

                for _ in stageA(0):
                    pass
                if len(tiles) > 1:
                    for _ in stageA(1):
                        pass
                for it in range(len(tiles)):
                    stageB_pe(it)
                    agen = stageA(it + 2) if it + 2 < len(tiles) else None
                    stageB_ep(it, agen)
                    if agen is not None:
                        for _ in agen:
                            pass
                S.barrier()
            Ws.close()
            Xs.close()
        S.barrier()
    return nc, S


_CACHE = {}


def kernel(x, c, ctx, c_ctx, w_ada, b_ada, g_pre, g_post, w_in, b_gate, w_dw, b_dw, ln_g, ln_b, w_pw2,
           g_head, w_out):
    if 'nc' not in _CACHE:
        _CACHE['nc'] = build()[0]
    nc = _CACHE['nc']
    f = lambda a: np.ascontiguousarray(np.asarray(a, dtype=np.float32))
    shared = {"w_ada": f(w_ada), "b_ada": f(b_ada), "g_pre": f(g_pre), "g_post": f(g_post), "w_in": f(w_in),
              "b_gate": f(b_gate), "w_dw": f(w_dw), "b_dw": f(b_dw), "ln_g": f(ln_g), "ln_b": f(ln_b),
              "w_pw2": f(w_pw2), "g_head": f(g_head), "w_out": f(w_out)}
    x = f(x); ctx = f(ctx); c = f(c); c_ctx = f(c_ctx)
    in_maps = []
    for core in range(8):
        b = core % 4
        m = dict(shared)
        m["xin"] = np.ascontiguousarray(np.concatenate([ctx[b], x[b]], axis=0))
        m["cc"] = np.ascontiguousarray(np.stack([c[b], c_ctx], axis=0))
        in_maps.append(m)
    res = run_bass_kernel_spmd(nc, in_maps, core_ids=list(range(8)))
    out = np.stack([np.asarray(res.results[b]["xout"])[NCTX:] for b in range(4)], axis=0)
    return out.astype(np.float32)
```
